# Optimizing a Trainium2 kernel written in Bass

```python
import math, functools
import jax, jax.numpy as jnp
from jax import lax
import numpy as np

D_MODEL = 2048
BATCH = 8
SEQ = 2048
DEPTH = 1
DEC_BATCH = 128
DEC_SEQ = 8
PAST_LEN = 16384
PAGE_SIZE = 128

HG_HEADS = 8
HG_DK = 128
HG_DV = 128
HG_WIDTH = HG_HEADS * HG_DV
HG_CHUNK = 64
MLA_HEADS = 8
MLA_Q_LORA = 512
MLA_KV_LORA = 256
MLA_NOPE = 128
MLA_ROPE = 64
MLA_V = 128
MLA_WIDTH = MLA_HEADS * MLA_V
MLA_SCALE = (MLA_NOPE + MLA_ROPE) ** -0.5
ROPE_THETA = 10000.0
MIX_WIDTH = HG_WIDTH + MLA_WIDTH
IN_WIDTH = 2 * HG_HEADS * HG_DK + 2 * HG_WIDTH + MLA_Q_LORA + MLA_KV_LORA + MLA_ROPE
D_FF = 5632
CONV_W = 3
PLE_DIM = 256
ATTN_BLOCK = 128
EPS = 1e-6
NEG_INF = -1e30

kernel_name = 'hybrid_hgrn2_mla_convffn_step'


def rmsnorm(x, g):
    xf = x.astype(jnp.float32)
    xf = xf * lax.rsqrt(jnp.mean(xf * xf, axis=-1, keepdims=True) + EPS)
    return xf.astype(x.dtype) * g.astype(x.dtype)


def rope_tables(pos):
    half = MLA_ROPE // 2
    inv = 1.0 / (ROPE_THETA ** (jnp.arange(half, dtype=jnp.float32) / half))
    ang = pos.astype(jnp.float32)[:, None] * inv[None, :]
    return jnp.cos(ang), jnp.sin(ang)


def apply_rope(x, cos, sin):
    x1, x2 = jnp.split(x.astype(jnp.float32), 2, axis=-1)
    return jnp.concatenate([x1 * cos - x2 * sin, x2 * cos + x1 * sin], axis=-1).astype(x.dtype)


def hgrn2_recurrence(q, k, v, logf, s0, chunk):
    b, t, h, dk = q.shape
    dv = v.shape[-1]
    n = t // chunk

    def to_chunks(a):
        return jnp.moveaxis(a.astype(jnp.float32).reshape(b, n, chunk, h, a.shape[-1]), 1, 0)

    causal = jnp.tril(jnp.ones((chunk, chunk), dtype=bool))

    def step(s, blk):
        qc, kc, vc, lf = blk
        cum = jnp.cumsum(lf, axis=1)
        q_dec = qc * jnp.exp(cum)
        k_inv = kc * jnp.exp(-cum)
        a = jnp.einsum('bthk,bshk->bhts', q_dec, k_inv)
        a = jnp.where(causal, a, 0.0)
        o = jnp.einsum('bhts,bshv->bthv', a, vc) + jnp.einsum('bthk,bhkv->bthv', q_dec, s)
        total = cum[:, -1]
        k_end = kc * jnp.exp(total[:, None] - cum)
        s_new = jnp.exp(total)[..., None] * s + jnp.einsum('bshk,bshv->bhkv', k_end, vc)
        return s_new, o

    s_final, o = lax.scan(step, s0.astype(jnp.float32),
                          (to_chunks(q), to_chunks(k), to_chunks(v), to_chunks(logf)))
    o = jnp.moveaxis(o, 0, 1).reshape(b, t, h, dv)
    return o, s_final


def hgrn2_mixer(q_raw, f_raw, i_raw, g_raw, lb, onorm, s0):
    b, t, _ = q_raw.shape
    q = jax.nn.silu(q_raw.astype(jnp.float32)).reshape(b, t, HG_HEADS, HG_DK)
    fx = f_raw.astype(jnp.float32)
    f = lb + (1.0 - lb) * jax.nn.sigmoid(fx)
    k = ((1.0 - lb) * jax.nn.sigmoid(-fx)).reshape(b, t, HG_HEADS, HG_DK)
    logf = jnp.log(f).reshape(b, t, HG_HEADS, HG_DK)
    v = i_raw.reshape(b, t, HG_HEADS, HG_DV)
    o, s = hgrn2_recurrence(q, k, v, logf, s0, math.gcd(t, HG_CHUNK))
    o = rmsnorm(o, onorm.reshape(HG_HEADS, HG_DV))
    o = o * jax.nn.sigmoid(g_raw.astype(jnp.float32)).reshape(b, t, HG_HEADS, HG_DV)
    return o.reshape(b, t, HG_WIDTH).astype(q_raw.dtype), s.astype(q_raw.dtype)


def mla_project(cq_raw, ckv_raw, kr_raw, q_norm, kv_norm, w_q_b, w_kv_b, pos):
    b, t, _ = cq_raw.shape
    cq = rmsnorm(cq_raw, q_norm)
    q = (cq @ w_q_b).reshape(b, t, MLA_HEADS, MLA_NOPE + MLA_ROPE)
    q_nope, q_rope = q[..., :MLA_NOPE], q[..., MLA_NOPE:]
    cos, sin = rope_tables(pos)
    q_rope = apply_rope(q_rope, cos[:, None, :], sin[:, None, :]) * MLA_SCALE
    k_rope = apply_rope(kr_raw, cos, sin)
    ckv = rmsnorm(ckv_raw, kv_norm)
    wkv = w_kv_b.reshape(MLA_KV_LORA, MLA_HEADS, MLA_NOPE + MLA_V)
    w_uk, w_uv = wkv[..., :MLA_NOPE], wkv[..., MLA_NOPE:]
    q_lat = jnp.einsum('bthn,chn->bthc', q_nope, w_uk) * MLA_SCALE
    return q_lat, q_rope, ckv, k_rope, w_uv


def mla_prompt_attention(q_lat, q_rope, ckv, k_rope):
    b, t, h, c = q_lat.shape
    nb = t // ATTN_BLOCK
    key_pos = jnp.arange(t)
    ckv32 = ckv.astype(jnp.float32)
    kr32 = k_rope.astype(jnp.float32)

    def block(args):
        ql, qr, i = args
        s = (jnp.einsum('bqhc,bsc->bhqs', ql.astype(jnp.float32), ckv32)
             + jnp.einsum('bqhr,bsr->bhqs', qr.astype(jnp.float32), kr32))
        qpos = i * ATTN_BLOCK + jnp.arange(ATTN_BLOCK)
        s = jnp.where(key_pos[None, :] <= qpos[:, None], s, NEG_INF)
        p = jax.nn.softmax(s, axis=-1)
        return jnp.einsum('bhqs,bsc->bqhc', p, ckv32)

    qb = jnp.moveaxis(q_lat.reshape(b, nb, ATTN_BLOCK, h, c), 1, 0)
    rb = jnp.moveaxis(q_rope.reshape(b, nb, ATTN_BLOCK, h, MLA_ROPE), 1, 0)
    out = lax.map(block, (qb, rb, jnp.arange(nb)))
    return jnp.moveaxis(out, 0, 1).reshape(b, t, h, c).astype(q_lat.dtype)


def mla_sample_attention(q_lat, q_rope, ckv, k_rope, cache_ckv, cache_krope, page_table, layer):
    b, t, h, c = q_lat.shape
    ql = q_lat.astype(jnp.float32)
    qr = q_rope.astype(jnp.float32)

    def page_step(carry, pids):
        m, l, acc = carry
        kc = cache_ckv[layer, pids].astype(jnp.float32)
        kr = cache_krope[layer, pids].astype(jnp.float32)
        s = jnp.einsum('bthc,bpc->bhtp', ql, kc) + jnp.einsum('bthr,bpr->bhtp', qr, kr)
        m_new = jnp.maximum(m, s.max(axis=-1))
        corr = jnp.exp(m - m_new)
        p = jnp.exp(s - m_new[..., None])
        l = l * corr + p.sum(axis=-1)
        acc = acc * corr[..., None] + jnp.einsum('bhtp,bpc->bhtc', p, kc)
        return (m_new, l, acc), None

    init = (jnp.full((b, h, t), NEG_INF, jnp.float32), jnp.zeros((b, h, t), jnp.float32),
            jnp.zeros((b, h, t, c), jnp.float32))
    (m, l, acc), _ = lax.scan(page_step, init, page_table.T)
    kc = ckv.astype(jnp.float32)
    kr = k_rope.astype(jnp.float32)
    s = jnp.einsum('bthc,bsc->bhts', ql, kc) + jnp.einsum('bthr,bsr->bhts', qr, kr)
    s = jnp.where(jnp.tril(jnp.ones((t, t), dtype=bool)), s, NEG_INF)
    m_new = jnp.maximum(m, s.max(axis=-1))
    corr = jnp.exp(m - m_new)
    p = jnp.exp(s - m_new[..., None])
    l = l * corr + p.sum(axis=-1)
    acc = acc * corr[..., None] + jnp.einsum('bhts,bsc->bhtc', p, kc)
    out = acc / l[..., None]
    return jnp.transpose(out, (0, 2, 1, 3)).astype(q_lat.dtype)


def conv_ffn(h, conv_s0, w_up, conv_w, conv_b, w_down):
    t = h.shape[1]
    u = h @ w_up
    a, gate_b = u[..., :D_FF], u[..., D_FF:]
    a_ext = jnp.concatenate([conv_s0.astype(a.dtype), a], axis=1)
    conv = conv_b
    for j in range(CONV_W):
        conv = conv + conv_w[j] * a_ext[:, j:j + t]
    y = (jax.nn.silu(conv) * gate_b) @ w_down
    return y, a_ext[:, -(CONV_W - 1):]


def trunk_layer(x, p_l, pos, hg_s0, conv_s0, attend, norm_mix, w_in, lb, hg_onorm,
                mla_q_norm, w_q_b, mla_kv_norm, w_kv_b, w_out, norm_ffn, w_up, conv_w,
                conv_b, w_down, norm_ple, w_ple_gate, w_ple_proj):
    b, t, _ = x.shape
    proj = rmsnorm(x, norm_mix) @ w_in
    o1 = HG_HEADS * HG_DK
    o2 = 2 * o1
    o3 = o2 + HG_WIDTH
    o4 = o3 + HG_WIDTH
    o5 = o4 + MLA_Q_LORA
    o6 = o5 + MLA_KV_LORA
    o_hg, hg_s = hgrn2_mixer(proj[..., :o1], proj[..., o1:o2], proj[..., o2:o3],
                             proj[..., o3:o4], lb, hg_onorm, hg_s0)
    q_lat, q_rope, ckv, k_rope, w_uv = mla_project(proj[..., o4:o5], proj[..., o5:o6],
                                                   proj[..., o6:], mla_q_norm, mla_kv_norm,
                                                   w_q_b, w_kv_b, pos)
    lat = attend(q_lat, q_rope, ckv, k_rope)
    o_mla = jnp.einsum('bthc,chv->bthv', lat, w_uv).reshape(b, t, MLA_WIDTH)
    x = x + jnp.concatenate([o_hg, o_mla.astype(x.dtype)], axis=-1) @ w_out
    y, conv_s = conv_ffn(rmsnorm(x, norm_ffn), conv_s0, w_up, conv_w, conv_b, w_down)
    x = x + y
    x = x + jax.nn.sigmoid(rmsnorm(x, norm_ple) @ w_ple_gate) * (p_l @ w_ple_proj)
    return x, ckv, k_rope, hg_s, conv_s


def setup_inputs(seed: int = 0) -> dict:
    key = jax.random.key(seed)
    ks = jax.random.split(key, 28)
    f32 = jnp.float32
    n_pages = PAST_LEN // PAGE_SIZE
    n_used = DEC_BATCH * n_pages
    n_pool = n_used + n_used // 4

    def nrm(k, shape, scale):
        return scale * jax.random.normal(k, shape, f32)

    def gain(k, shape):
        return 1.0 + 0.05 * jax.random.normal(k, shape, f32)

    page_table = jax.random.permutation(ks[0], n_pool)[:n_used].reshape(DEC_BATCH, n_pages).astype(jnp.int32)
    return {
        'x_prompt': nrm(ks[1], (BATCH, SEQ, D_MODEL), 1.0),
        'x_sample': nrm(ks[2], (DEC_BATCH, DEC_SEQ, D_MODEL), 1.0),
        'cache_ckv': nrm(ks[3], (DEPTH, n_pool, PAGE_SIZE, MLA_KV_LORA), 1.0),
        'cache_krope': nrm(ks[4], (DEPTH, n_pool, PAGE_SIZE, MLA_ROPE), 1.0),
        'state_hgrn': nrm(ks[5], (DEPTH, DEC_BATCH, HG_HEADS, HG_DK, HG_DV), 0.5),
        'state_conv': nrm(ks[6], (DEPTH, DEC_BATCH, CONV_W - 1, D_FF), 1.0),
        'page_table': page_table,
        'p_prompt': nrm(ks[7], (DEPTH, BATCH, SEQ, PLE_DIM), 1.0),
        'p_sample': nrm(ks[8], (DEPTH, DEC_BATCH, DEC_SEQ, PLE_DIM), 1.0),
        'norm_mix': gain(ks[9], (DEPTH, D_MODEL)),
        'w_in': nrm(ks[10], (DEPTH, D_MODEL, IN_WIDTH), D_MODEL ** -0.5),
        'hg_lower': nrm(ks[11], (DEPTH + 1, HG_HEADS * HG_DK), 0.5),
        'hg_onorm': gain(ks[12], (DEPTH, HG_WIDTH)),
        'mla_q_norm': gain(ks[13], (DEPTH, MLA_Q_LORA)),
        'w_q_b': nrm(ks[14], (DEPTH, MLA_Q_LORA, MLA_HEADS * (MLA_NOPE + MLA_ROPE)), MLA_Q_LORA ** -0.5),
        'mla_kv_norm': gain(ks[15], (DEPTH, MLA_KV_LORA)),
        'w_kv_b': nrm(ks[16], (DEPTH, MLA_KV_LORA, MLA_HEADS * (MLA_NOPE + MLA_V)), MLA_KV_LORA ** -0.5),
        'w_out': nrm(ks[17], (DEPTH, MIX_WIDTH, D_MODEL), MIX_WIDTH ** -0.5),
        'norm_ffn': gain(ks[18], (DEPTH, D_MODEL)),
        'w_up': nrm(ks[19], (DEPTH, D_MODEL, 2 * D_FF), D_MODEL ** -0.5),
        'conv_w': nrm(ks[20], (DEPTH, CONV_W, D_FF), CONV_W ** -0.5),
        'conv_b': nrm(ks[21], (DEPTH, D_FF), 0.02),
        'w_down': nrm(ks[22], (DEPTH, D_FF, D_MODEL), D_FF ** -0.5),
        'norm_ple': gain(ks[23], (DEPTH, D_MODEL)),
        'w_ple_gate': nrm(ks[24], (DEPTH, D_MODEL, D_MODEL), D_MODEL ** -0.5),
        'w_ple_proj': nrm(ks[25], (DEPTH, PLE_DIM, D_MODEL), PLE_DIM ** -0.5),
        'norm_final': gain(ks[26], (D_MODEL,)),
    }


def reference(x_prompt, x_sample, cache_ckv, cache_krope, state_hgrn, state_conv, page_table,
              p_prompt, p_sample, norm_mix, w_in, hg_lower, hg_onorm, mla_q_norm, w_q_b,
              mla_kv_norm, w_kv_b, w_out, norm_ffn, w_up, conv_w, conv_b, w_down, norm_ple,
              w_ple_gate, w_ple_proj, norm_final):
    lb_all = jnp.cumsum(jax.nn.softmax(hg_lower.astype(jnp.float32), axis=0), axis=0)
    bp, tp, _ = x_prompt.shape
    pos_prompt = jnp.arange(tp, dtype=jnp.int32)
    pos_sample = PAST_LEN + jnp.arange(x_sample.shape[1], dtype=jnp.int32)
    hg0_prompt = jnp.zeros((bp, HG_HEADS, HG_DK, HG_DV), jnp.float32)
    conv0_prompt = jnp.zeros((bp, CONV_W - 1, D_FF), x_prompt.dtype)
    xp, xs = x_prompt, x_sample
    new_p, new_s = [], []
    for l in range(DEPTH):
        w = (norm_mix[l], w_in[l], lb_all[l], hg_onorm[l], mla_q_norm[l], w_q_b[l],
             mla_kv_norm[l], w_kv_b[l], w_out[l], norm_ffn[l], w_up[l], conv_w[l],
             conv_b[l], w_down[l], norm_ple[l], w_ple_gate[l], w_ple_proj[l])
        xp, ckv_p, kr_p, hg_p, cv_p = trunk_layer(xp, p_prompt[l], pos_prompt, hg0_prompt,
                                                  conv0_prompt, mla_prompt_attention, *w)
        attend_s = functools.partial(mla_sample_attention, cache_ckv=cache_ckv,
                                     cache_krope=cache_krope, page_table=page_table, layer=l)
        xs, ckv_s, kr_s, hg_s, cv_s = trunk_layer(xs, p_sample[l], pos_sample, state_hgrn[l],
                                                  state_conv[l], attend_s, *w)
        new_p.append((ckv_p, kr_p, hg_p, cv_p))
        new_s.append((ckv_s, kr_s, hg_s, cv_s))
    y_prompt = rmsnorm(xp, norm_final)
    y_sample = rmsnorm(xs, norm_final)
    ckv_prompt = jnp.stack([e[0] for e in new_p])
    krope_prompt = jnp.stack([e[1] for e in new_p])
    hgrn_prompt = jnp.stack([e[2] for e in new_p])
    conv_prompt = jnp.stack([e[3] for e in new_p])
    ckv_sample = jnp.stack([e[0] for e in new_s])
    krope_sample = jnp.stack([e[1] for e in new_s])
    hgrn_sample = jnp.stack([e[2] for e in new_s])
    conv_sample = jnp.stack([e[3] for e in new_s])
    return (y_prompt, y_sample, ckv_prompt, krope_prompt, hgrn_prompt, conv_prompt,
            ckv_sample, krope_sample, hgrn_sample, conv_sample)
```

```python
import contextlib
import os
import numpy as np
import ml_dtypes
import concourse.bass as bass
import concourse.mybir as mybir
from concourse.bass_utils import run_bass_kernel_spmd

F32 = mybir.dt.float32
BF16 = mybir.dt.bfloat16
I32 = mybir.dt.int32
AF = mybir.ActivationFunctionType
ALU = mybir.AluOpType
AX = mybir.AxisListType

D = 2048
NTOK_P = 2048
NTOK = 2176
NBLK = 17
H = 8
DFF = 5632
NFC = 44
INW = 4928
EPS = 1e-6
SCALE = 192.0 ** -0.5
NEG = -1e30
NPAGES = 128
NB = 16
NGRP = 32


class Res:
    __slots__ = ("name", "w", "rs", "excl")

    ALL = []

    def __init__(self, name, excl=False):
        self.name = name
        self.w = None
        self.rs = []
        self.excl = excl
        Res.ALL.append(self)


class Op:
    __slots__ = ("eng", "fn", "deps", "is_dma", "sem", "semval", "signal", "count", "waits", "pos", "phase")

    def __init__(self, eng, fn, deps, is_dma=False):
        self.eng = eng
        self.fn = fn
        self.deps = deps
        self.is_dma = is_dma
        self.sem = None
        self.semval = 0
        self.signal = False
        self.count = 0
        self.waits = []
        self.pos = 0


LAST_PROG = None
ENGS = ("pe", "act", "dve", "pool", "sp")


class Prog:
    def __init__(self, nc, es):
        self.nc = nc
        self.es = es
        self.streams = {e: [] for e in ENGS}
        self.ops = []
        self.dkeys = {}
        self.esem = {}
        self.phase = "pro"

    def _deps(self, r, w):
        deps = []
        for x in r:
            if x.w is not None:
                deps.append(x.w)
            if x.excl:
                deps.extend(x.rs)
        for x in w:
            if x.w is not None:
                deps.append(x.w)
            deps.extend(x.rs)
        return deps

    def _commit(self, op, r, w):
        for x in r:
            x.rs.append(op)
        for x in w:
            x.w = op
            x.rs = []
        op.pos = len(self.streams[op.eng])
        op.phase = self.phase
        self.streams[op.eng].append(op)
        self.ops.append(op)

    def op(self, eng, fn, r=(), w=()):
        op = Op(eng, fn, self._deps(r, w))
        self._commit(op, r, w)
        return op

    def dma(self, eng, fn, r=(), w=(), key=None):
        op = Op(eng, fn, self._deps(r, w), is_dma=True)
        ent = self.dkeys.get(key)
        if ent is None:
            sem = self.es.enter_context(self.nc.semaphore("d_" + key))
            ent = [sem, 0, None]
            self.dkeys[key] = ent
        if ent[2] is not None:
            op.deps.append(ent[2])
        ent[1] += 16
        ent[2] = op
        op.sem = ent[0]
        op.semval = ent[1]
        self._commit(op, r, w)
        return op

    def finalize(self):
        for e in ENGS:
            self.esem[e] = self.es.enter_context(self.nc.semaphore("e_" + e))
        for op in self.ops:
            best = {}
            for d in op.deps:
                if d.is_dma:
                    continue
                if d.eng == op.eng and op.eng == "pe" and not op.is_dma:
                    continue
                if d.eng not in best or best[d.eng].pos < d.pos:
                    best[d.eng] = d
            op.deps = [d for d in op.deps if d.is_dma] + list(best.values())
            for d in best.values():
                d.signal = True
        for e in ENGS:
            c = 0
            for op in self.streams[e]:
                if op.signal:
                    c += 1
                    op.count = c
        for op in self.ops:
            ws = {}
            for d in op.deps:
                if d.is_dma:
                    k = id(d.sem)
                    if k not in ws or ws[k][1] < d.semval:
                        ws[k] = (d.sem, d.semval)
                else:
                    if d.eng == op.eng and op.eng == "pe" and not op.is_dma:
                        continue
                    sem = self.esem[d.eng]
                    k = id(sem)
                    if k not in ws or ws[k][1] < d.count:
                        ws[k] = (sem, d.count)
            op.waits = list(ws.values())

    def emit(self, engname, eng, final_waits=False):
        seen = {}
        for op in self.streams[engname]:
            for sem, val in op.waits:
                k = id(sem)
                if seen.get(k, 0) >= val:
                    continue
                eng.wait_ge(sem, val)
                seen[k] = val
            inst = op.fn(eng)
            if op.is_dma:
                inst.then_inc(op.sem, 16)
            elif op.signal:
                inst.then_inc(self.esem[engname], 1)
        if final_waits:
            for key, ent in self.dkeys.items():
                if ent[1] > 0 and seen.get(id(ent[0]), 0) < ent[1]:
                    eng.wait_ge(ent[0], ent[1])
            for e in ENGS:
                if e == engname:
                    continue
                last = 0
                for op in self.streams[e]:
                    if op.signal:
                        last = op.count
                if last > 0 and seen.get(id(self.esem[e]), 0) < last:
                    eng.wait_ge(self.esem[e], last)


def build_program(npool, nblk_run=NBLK, do_sample=True):
    nc = bass.Bass("TRN2", target_bir_lowering=False)
    es = contextlib.ExitStack()
    P = Prog(nc, es)
    Res.ALL = []

    def din(name, shape, dt=F32):
        return nc.dram_tensor(name, list(shape), dt, kind="ExternalInput").ap()

    def dout(name, shape, dt=F32):
        return nc.dram_tensor(name, list(shape), dt, kind="ExternalOutput").ap()

    def dscr(name, shape, dt=BF16):
        return nc.dram_tensor(name, list(shape), dt).ap()

    def sb(name, shape, dt):
        return es.enter_context(nc.sbuf_tensor(name, list(shape), dt))

    ARENA_BYTES = 58 * 1024
    arena = sb("arena", [128, ARENA_BYTES // 2], BF16)

    class Carver:
        def __init__(self):
            self.off = 0

        def __call__(self, name, shape, dt):
            esz = 2 if dt == BF16 else 4
            n = 1
            for d_ in shape[1:]:
                n *= d_
            nb = n * esz
            a = self.off
            self.off += (nb + 31) // 32 * 32
            assert self.off <= ARENA_BYTES, (name, self.off)
            ap = arena[0:shape[0], a // 2:(a + nb) // 2]
            if dt != BF16:
                ap = ap.bitcast(dt)
            if len(shape) == 3:
                ap = ap.rearrange("p (a b) -> p a b", a=shape[1])
            elif len(shape) == 4:
                ap = ap.rearrange("p (a b c) -> p a b c", a=shape[1], b=shape[2])
            return ap

    cvP = Carver()
    cvS = Carver()

    x_d = din("x", [NTOK, D])
    p_d = din("pl", [NTOK, 256])
    ck_d = din("cache_ckv", [npool * 16, 2048])
    kr_d = din("cache_krope", [npool * 16, 512])
    ptq_d = din("ptq", [8, NB * 16], I32)
    sth_d = din("state_hgrn", [NB, H, 128, 128])
    stc_d = din("state_conv", [NB * 2, DFF])
    w_in_d = din("w_in", [D, INW])
    w_qb_d = din("w_q_b", [512, 1536])
    w_kvb_d = din("w_kv_b", [256, 2048])
    w_out_d = din("w_out", [D, D])
    w_up_d = din("w_up", [D, 2 * DFF])
    w_down_d = din("w_down", [DFF, D])
    w_pg_d = din("w_ple_gate", [D, D])
    w_pp_d = din("w_ple_proj", [256, D])
    nmix_d = din("norm_mix", [D])
    nffn_d = din("norm_ffn", [D])
    nple_d = din("norm_ple", [D])
    nfin_d = din("norm_final", [D])
    hgl_d = din("hg_lower", [2, 1024])
    onorm_d = din("hg_onorm", [1024])
    qnorm_d = din("mla_q_norm", [512])
    kvnorm_d = din("mla_kv_norm", [256])
    convw_d = din("conv_w", [3, DFF])
    convb_d = din("conv_b", [DFF])
    ident_d = din("c_ident", [128, 128], BF16)
    identf_d = din("c_identf", [128, 128], F32)
    ones_d = din("c_ones", [128, 128], BF16)
    cmask_d = din("c_cmask", [128, 128], BF16)
    bd64_d = din("c_bd64", [128, 128], BF16)
    bd8_d = din("c_bd8", [128, 128], BF16)
    m64_d = din("c_m64", [128, 512], F32)
    m8_d = din("c_m8", [128, 512], F32)
    bmask_d = din("c_bmask", [128, NB], F32)
    smask_d = din("c_smask", [64, 248], BF16)
    rcol_d = din("c_rcol", [128, 1], F32)
    cosT_d = din("c_cosT", [64, NTOK])
    sinT_d = din("c_sinT", [64, NTOK])
    cosm_d = din("c_cosm", [NTOK, 32])
    sinm_d = din("c_sinm", [NTOK, 32])

    y_o = dout("y", [NTOK, D])
    ckv_o = dout("ckv_o", [NTOK, 256])
    kro_o = dout("kr_o", [NTOK, 64])
    hgp_o = dout("hg_p", [H, 128, 128])
    cvp_o = dout("conv_p", [2, DFF])
    hgs_o = dout("hg_s", [NB, H, 128, 128])
    cvs_o = dout("conv_s", [NB, 2, DFF])
    DBG = os.environ.get("KDBG", "0") == "1"
    if DBG:
        dbg1 = dout("dbg1", [128, 2048], BF16)
        dbg2 = dout("dbg2", [64, 264], F32)
        dbg3 = dout("dbg3", [128, NB * 16], I32)
        dbg4 = dout("dbg4", [128, 4 * 256], BF16)

    wb_in = dscr("wb_in", [10, 128, 16, 512])
    wb_up = dscr("wb_up", [22, 128, 16, 512])
    wb_dn = dscr("wb_dn", [16, 128, 11, 512])
    wb_out = dscr("wb_out", [4, 128, 16, 512])
    wb_pg = dscr("wb_pg", [4, 128, 16, 512])
    wb_pp = dscr("wb_pp", [4, 128, 2, 512])
    wb_q = dscr("wb_q", [4, 128, 4, 512])
    R_wb = {n: Res(n) for n in ("in", "up", "dn", "out", "pg", "pp", "q")}

    ps = es.enter_context(nc.psum_tensor("ps", [128, 4096], F32))
    R_bank = [Res("bank%d" % i, excl=True) for i in range(8)]

    def bank(i, n=1):
        return ps[:, i * 512:(i + n) * 512]

    def bankbf(i, n=1):
        return ps[:, i * 512:(i + n) * 512].bitcast(BF16)

    ident = sb("ident", [128, 128], BF16)
    identf = sb("identf", [128, 128], F32)
    ones = sb("ones", [128, 128], BF16)
    cmask = sb("cmask", [128, 128], BF16)
    bd64 = sb("bd64", [128, 128], BF16)
    bd8 = sb("bd8", [128, 128], BF16)
    bmask = sb("bmask", [128, NB], F32)
    smask = sb("smask", [64, 248], BF16)
    rcol = sb("rcol", [128, 1], F32)
    gT = sb("gT", [128, 3, 16], F32)
    hgl = sb("hgl", [128, 2, H], F32)
    lbT = sb("lbT", [128, H], F32)
    omlT = sb("omlT", [128, H], F32)
    nomlT = sb("nomlT", [128, H], F32)
    onormT = sb("onormT", [128, H], F32)
    qnorm_bc = sb("qnorm_bc", [128, 512], F32)
    kvnorm_bc = sb("kvnorm_bc", [128, 256], F32)
    cwT = sb("cwT", [128, 3, NFC], F32)
    cbT = sb("cbT", [128, NFC], F32)
    cvW = Carver()
    cvW.off = 52 * 1024
    wuk = cvW("wuk", [128, 2, H, 128], BF16)
    wukT = sb("wukT", [128, H, 256], BF16)
    wuv = sb("wuv", [128, 2, H, 128], BF16)
    R_const = Res("const")
    R_wq = Res("wq")
    R_wukT = Res("wukT")
    R_lb = Res("lb")

    def cload(dst, src, key="const", eng="sp", w=None):
        P.dma(eng, lambda e, dst=dst, src=src: e.dma_start(out=dst, in_=src, allow_slow_non_contiguous=True), w=[w or R_const], key=key)

    with nc.allow_non_contiguous_dma(reason="small constant layouts"):
        cload(ident[:], ident_d)
        cload(identf[:], identf_d)
        cload(ones[:], ones_d)
        cload(cmask[:], cmask_d)
        cload(bd64[:], bd64_d)
        cload(bd8[:], bd8_d)
        cload(bmask[:], bmask_d)
        cload(smask[:], smask_d)
        cload(rcol[:], rcol_d)
        cload(gT[:, 0, :], nmix_d.rearrange("(k p) -> p k", p=128))
        cload(gT[:, 1, :], nffn_d.rearrange("(k p) -> p k", p=128))
        cload(gT[:, 2, :], nple_d.rearrange("(k p) -> p k", p=128))
        cload(hgl[:, 0, :], hgl_d[0].rearrange("(h p) -> p h", p=128))
        cload(hgl[:, 1, :], hgl_d[1].rearrange("(h p) -> p h", p=128))
        cload(onormT[:], onorm_d.rearrange("(h p) -> p h", p=128))
        cload(qnorm_bc[:], qnorm_d.partition_broadcast(128))
        cload(kvnorm_bc[:], kvnorm_d.partition_broadcast(128))
        cload(cwT[:], convw_d.rearrange("j (c p) -> p j c", p=128))
        cload(cbT[:], convb_d.rearrange("(c p) -> p c", p=128))
        for kc in range(4):
            wqv = w_qb_d[kc * 128:(kc + 1) * 128, :].rearrange("p (h c) -> p h c", c=192)
            for pn in range(4):
                dstv = wb_q[pn, :, kc, :].rearrange("p (hh c) -> p hh c", c=256)
                srcv = wqv[:, 2 * pn:2 * pn + 2, :]
                for (d0, d1, s0, s1) in ((0, 192, 0, 192), (192, 224, 160, 192), (224, 256, 128, 160)):
                    P.dma("pool", lambda e, d=dstv[:, :, d0:d1], s_=srcv[:, :, s0:s1]: e.dma_start(out=d, in_=s_, allow_slow_non_contiguous=True),
                          w=[R_wb["q"]], key="cast_q")
        for kc in range(2):
            wkv = w_kvb_d[kc * 128:(kc + 1) * 128, :].rearrange("p (h c) -> p h c", c=256)
            cload(wuk[:, kc, :, :], wkv[:, :, 0:128], key="wq", eng="pool", w=R_wq)
            cload(wuv[:, kc, :, :], wkv[:, :, 128:256], key="wq", eng="pool", w=R_wq)

    dlb = sb("dlb", [128, H], F32)
    P.op("dve", lambda e: e.tensor_tensor(out=dlb[:], in0=hgl[:, 0, :], in1=hgl[:, 1, :], op=ALU.subtract),
         r=[R_const], w=[R_lb])
    P.op("act", lambda e: e.activation(out=lbT[:], in_=dlb[:], func=AF.Sigmoid), r=[R_lb], w=[R_lb])
    P.op("dve", lambda e: e.tensor_scalar(out=omlT[:], in0=lbT[:], scalar1=-1.0, scalar2=1.0,
                                          op0=ALU.mult, op1=ALU.add), r=[R_lb], w=[R_lb])
    P.op("dve", lambda e: e.tensor_scalar(out=nomlT[:], in0=lbT[:], scalar1=1.0, scalar2=-1.0,
                                          op0=ALU.mult, op1=ALU.add), r=[R_lb], w=[R_lb])

    T2 = bankbf(0, 2)
    R_T = Res("Tbanks", excl=True)
    for h in range(H):
        for kc in range(2):
            P.op("pe", lambda e, h=h, kc=kc: e.transpose(out=T2[:, (h * 2 + kc) * 128:(h * 2 + kc + 1) * 128],
                                                         in_=wuk[:, kc, h, :], identity=ident[:]),
                 r=[R_wq, R_const], w=[R_T])
    P.op("dve", lambda e: e.tensor_copy(out=wukT[:].rearrange("p h c -> p (h c)"), in_=T2[:, 0:2048]),
         r=[R_T], w=[R_wukT])

    def cast_panels(dst, src, ncols_total, kc_n, name, col0=0, kbase=0, pidx0=0, npan=None):
        npan = npan if npan is not None else (ncols_total + 511) // 512
        for j in range(npan):
            c0 = col0 + j * 512
            cw = min(512, col0 + ncols_total - c0)
            srcv = src[kbase:kbase + kc_n * 128, c0:c0 + cw].rearrange("(k p) c -> p k c", p=128)
            P.dma("pool", lambda e, d=dst[pidx0 + j, :, :, 0:cw], s=srcv: e.dma_start(out=d, in_=s),
                  w=[R_wb[name]], key="cast_" + name)

    cast_panels(wb_in, w_in_d, INW, 16, "in")
    cast_panels(wb_out, w_out_d, D, 16, "out")
    cast_panels(wb_up, w_up_d, 2 * DFF, 16, "up")
    for cb in range(4):
        for kg in range(4):
            srcv = w_down_d[kg * 1408:(kg + 1) * 1408, cb * 512:(cb + 1) * 512].rearrange("(k p) c -> p k c", p=128)
            P.dma("pool", lambda e, d=wb_dn[cb * 4 + kg], s=srcv: e.dma_start(out=d, in_=s),
                  w=[R_wb["dn"]], key="cast_dn")
    cast_panels(wb_pg, w_pg_d, D, 16, "pg")
    cast_panels(wb_pp, w_pp_d, D, 2, "pp")

    class Rot:
        def __init__(self, name, shape, dt, n, alloc=None):
            alloc = alloc or sb
            self.t = [alloc("%s%d" % (name, i), shape, dt) for i in range(n)]
            self.r = [Res("%s%d" % (name, i)) for i in range(n)]
            self.i = 0
            self.n = n
            self.name = name

        def next(self):
            k = self.i % self.n
            self.i += 1
            return self.t[k], self.r[k], "%s%d" % (self.name, k)

    panels = Rot("pan", [128, 16, 512], BF16, 3)

    panq = [0, False]

    def load_panel(src_panel, kc_n, rname, cw=512):
        t, r, key = panels.next()
        panq[0] += 1
        qeng = "pool" if (panq[0] % 2 == 1 and not panq[1]) else "sp"
        P.dma(qeng, lambda e, t=t, s=src_panel, kc_n=kc_n, cw=cw: e.dma_start(out=t[:, 0:kc_n, 0:cw], in_=s[:, 0:kc_n, 0:cw]),
              r=[R_wb[rname]], w=[r], key=key)
        return t, r

    xt = sb("xt", [128, D], F32)
    R_x = Res("x")
    xsq = sb("xsq", [128, D], BF16)
    R_xsq = Res("xsq")
    st1 = sb("st1", [128, 8], F32)
    R_st = Res("st1")
    xnT = sb("xnT", [128, 16, 128], BF16)
    R_xnT = Res("xnT")
    mixT = sb("mixT", [128, 16, 128], BF16)
    R_mix = Res("mixT")
    hT = sb("hT", [128, NFC, 128], BF16)
    R_hT = Res("hT")

    m64 = cvP("m64", [128, 512], F32)
    m8 = cvS("m8", [128, 512], F32)
    R_m = Res("scanmask")
    P.dma("sp", lambda e: e.dma_start(out=m64[:], in_=m64_d), w=[R_m], key="m64")
    KaugT = cvP("KaugT", [128, 3, NTOK_P], BF16)
    ckv_tm = cvP("ckv_tm", [128, 16, 256], BF16)
    R_K = Res("Kcache")
    Sst = cvP("Sst", [128, H, 128], F32)
    Sbf = cvP("Sbf", [128, H, 128], BF16)
    R_S = [Res("S%d" % h) for h in range(H)]
    convc = cvP("convc", [128, NFC, 2], F32)
    R_convc = Res("convc")
    P.op("dve", lambda e: e.memset(Sst[:], 0.0), w=R_S)
    P.op("dve", lambda e: e.memset(Sbf[:], 0.0), w=R_S)
    P.op("dve", lambda e: e.memset(convc[:], 0.0), w=[R_convc])
    P.op("dve", lambda e: e.memset(KaugT[:], 0.0), w=[R_K])

    def rstd_from_ss(ss_ap, out_ap, n, rs, ws):
        P.op("act", lambda e: e.activation(out=out_ap, in_=ss_ap, func=AF.Ln, scale=1.0 / n, bias=epsb[:, 0:1]),
             r=rs, w=ws)
        P.op("act", lambda e: e.activation(out=out_ap, in_=out_ap, func=AF.Exp, scale=-0.5), r=ws, w=ws)

    epsb = sb("epsb", [128, 1], F32)
    P.op("dve", lambda e: e.memset(epsb[:], EPS), w=[R_const])

    def norm_to_xnT(gi):
        P.op("act", lambda e: e.activation(out=xsq[:], in_=xt[:], func=AF.Square, accum_out=st1[:, 0:1]),
             r=[R_x], w=[R_xsq, R_st])
        rstd_from_ss(st1[:, 0:1], st1[:, 1:2], D, [R_st, R_const], [R_st])
        P.op("act", lambda e: e.activation(out=xsq[:], in_=xt[:], func=AF.Copy, scale=st1[:, 1:2]),
             r=[R_x, R_st], w=[R_xsq])
        for kc in range(16):
            P.op("pe", lambda e, kc=kc: e.transpose(out=T2[:, kc * 128:(kc + 1) * 128], in_=xsq[:, kc * 128:(kc + 1) * 128],
                                                    identity=ident[:]), r=[R_xsq, R_const], w=[R_T])
        for kc in range(16):
            eng = "dve" if kc % 2 == 0 else "act"
            if eng == "dve":
                P.op("dve", lambda e, kc=kc: e.tensor_scalar(out=xnT[:, kc, :], in0=T2[:, kc * 128:(kc + 1) * 128],
                                                             scalar1=gT[:, gi, kc:kc + 1], scalar2=None, op0=ALU.mult),
                     r=[R_T, R_const], w=[R_xnT])
            else:
                P.op("act", lambda e, kc=kc: e.activation(out=xnT[:, kc, :], in_=T2[:, kc * 128:(kc + 1) * 128],
                                                          func=AF.Copy, scale=gT[:, gi, kc:kc + 1]),
                     r=[R_T, R_const], w=[R_xnT])

    def mm_fm(out_ap, pan, c0, ncol, act, kcn, rs, ws, rows=128):
        for kc in range(kcn):
            P.op("pe", lambda e, kc=kc: e.matmul(out_ap, lhsT=pan[:, kc, c0:c0 + ncol], rhs=act[:, kc, :],
                                                 start=(kc == 0), stop=(kc == kcn - 1)), r=rs, w=ws)

    def mm_tm(out_ap, act, pan, c0, ncol, kcn, rs, ws, kc0=0, first=True, last=True, pk0=0):
        for kc in range(kcn):
            P.op("pe", lambda e, kc=kc: e.matmul(out_ap, lhsT=act[:, kc0 + kc, :], rhs=pan[:, pk0 + kc, c0:c0 + ncol],
                                                 start=(first and kc == 0), stop=(last and kc == kcn - 1)), r=rs, w=ws)

    vtm = sb("vtm", [128, 1024], BF16)
    R_v = Res("vtm")
    qdT = sb("qdT", [128, H, 128], BF16)
    keT = sb("keT", [128, H, 128], BF16)
    gateT = sb("gateT", [128, H, 128], BF16)
    Edec = sb("Edec", [128, H, 128], F32)
    R_hg = [Res("hg%d" % i) for i in range(2)]
    tA = [sb("tA%d" % i, [128, 512], F32) for i in range(5)]
    R_tA = [Res("tA%d" % i) for i in range(5)]
    tA.append(tA[1])
    R_tA.append(R_tA[1])
    ketm = sb("ketm", [128, H, 128], BF16)
    R_ketm = Res("ketm")
    ATm = sb("ATm", [128, 128], BF16)
    R_ATm = Res("ATm")
    osq = sb("osq", [128, 128], BF16)
    R_osq = Res("osq")
    orst = sb("orst", [128, 128], F32)
    R_orst = Res("orst")
    otmp = sb("otmp", [128, 128], F32)
    R_otmp = Res("otmp")
    cqn = sb("cqn", [128, 512], BF16)
    R_cqn = Res("cqn")
    cqnT = sb("cqnT", [128, 4, 128], BF16)
    R_cqnT = Res("cqnT")
    ckvf = Rot("ckvf", [128, 256], F32, 2)
    krf = Rot("krf", [128, 64], F32, 2)
    krtmp = sb("krtmp", [128, 6, 32], F32)
    R_krtmp = Res("krtmp")
    krbf = sb("krbf", [128, 64], BF16)
    ckvbf_s = cvS("ckvbf_s", [128, 256], BF16)
    KaugT_s = cvS("KaugT_s", [128, 3, 128], BF16)
    R_Ks = Res("Ks")
    cosm = Rot("cosm", [128, 32], F32, 2)
    sinm = Rot("sinm", [128, 32], F32, 2)
    cosT = Rot("cosT", [64, 128], F32, 2)
    sinT = Rot("sinT", [64, 128], F32, 2)
    qnT = sb("qnT", [128, 128], BF16)
    R_qnT = Res("qnT")
    QaugT = cvP("QaugT", [128, 3, 128], BF16)
    R_Q = Res("QaugT")
    QaugT_b = cvP("QaugT_b", [128, 3, 128], BF16)
    R_Qb = Res("QaugT_b")
    st2 = sb("st2", [128, 2], F32)
    R_st2 = Res("st2")
    QaugT_all = cvS("QaugT_all", [128, 3, NB, 64], BF16)
    QaugT_s = cvS("QaugT_s", [128, 3, 128], BF16)
    R_Qs = Res("QaugT_s")
    R_Qall = Res("Qall")
    qtmp = sb("qtmp", [64, 2, 128], F32)
    R_qtmp = Res("qtmp")
    Pf = cvP("Pf", [128, 2048], F32)
    R_Pf = Res("Pf")
    Pn = cvP("Pn", [128, 2048], BF16)
    R_Pn = Res("Pn")
    PT = cvP("PT", [128, 16, 128], BF16)
    R_PT = Res("PT")
    latT = cvP("latT", [128, 2, 128], BF16)
    R_latT = Res("latT")
    ptile = sb("ptile", [128, 256], F32)
    R_ptile = Res("ptile")
    pbf = sb("pbf", [128, 256], BF16)
    R_pbf = Res("pbf")
    pT = sb("pT", [128, 2, 128], BF16)
    R_pT = Res("pT")
    aext = Rot("aext", [128, 160], F32, 1)
    cva = Rot("cva", [128, 128], F32, 1)
    cvb = Rot("cvb", [128, 128], F32, 1)
    atmr = Rot("atmr", [128, 512], F32, 1)
    sconvT = cvS("sconvT", [128, NFC, 32], F32)
    R_sconv = Res("sconv")
    R_sctm = Res("sctm")
    sgt = Rot("sgt", [128, 512], F32, 1)

    ptb = cvS("ptb", [128, NB * 16], I32)
    idx = cvS("idx", [128, NB * 16], I32)
    R_idx = Res("idx")
    accs = cvS("accs", [64, 256], F32)
    fst = cvS("fst", [64, 8], F32)
    R_acc = Res("acc")
    R_fst = Res("fst")
    lat_s = cvS("lat_s", [64, 256], BF16)
    R_lats = Res("lat_s")
    latT_s = cvS("latT_s", [128, 2, H, 128], BF16)
    R_latTs = Res("latT_s")
    cvS2 = Carver()
    cvS2.off = cvS.off
    cvS3 = Carver()
    cvS3.off = cvS.off
    Sload = Rot("Sload", [128, H, 128], F32, 2, cvS)
    Sloadbf = Rot("Sloadbf", [128, H, 128], BF16, 2, cvS)
    Gc = Rot("Gc", [128, 8, 256], BF16, 3, cvS2)
    Gr = Rot("Gr", [128, 8, 64], BF16, 3, cvS2)
    KT = Rot("KT", [128, 3, 512], BF16, 3, cvS2)
    Pbs = Rot("Pbs", [64, 512], BF16, 2, cvS2)
    PTs = Rot("PTs", [128, 4, 64], BF16, 2, cvS2)
    sctm = cvS3("sctm", [32, 2048], F32)

    def block(T):
        sample = (T == 16)
        tok0 = T * 128
        P.dma("sp", lambda e: e.dma_start(out=xt[:], in_=x_d[tok0:tok0 + 128, :]), w=[R_x], key="x")
        cm, r_cm, k_cm = cosm.next()
        sm, r_sm, k_sm = sinm.next()
        cT, r_cT, k_cT = cosT.next()
        sT, r_sT, k_sT = sinT.next()
        P.dma("sp", lambda e: e.dma_start(out=cm[:], in_=cosm_d[tok0:tok0 + 128, :]), w=[r_cm], key=k_cm)
        P.dma("sp", lambda e: e.dma_start(out=sm[:], in_=sinm_d[tok0:tok0 + 128, :]), w=[r_sm], key=k_sm)
        P.dma("sp", lambda e: e.dma_start(out=cT[:], in_=cosT_d[:, tok0:tok0 + 128]), w=[r_cT], key=k_cT)
        P.dma("sp", lambda e: e.dma_start(out=sT[:], in_=sinT_d[:, tok0:tok0 + 128]), w=[r_sT], key=k_sT)

        STG = int(os.environ.get("KSTAGE", "99"))
        P.phase = "%d:norm1" % T
        norm_to_xnT(0)
        if STG <= 1:
            return
        P.phase = "%d:w_in" % T
        for half in range(2):
            for j, (pidx, bk) in enumerate(((half, 2), (2 + half, 3), (6 + half, 4))):
                pan, rp = load_panel(wb_in[pidx], 16, "in")
                for hh in range(4):
                    mm_fm(bank(bk)[:, hh * 128:(hh + 1) * 128], pan, hh * 128, 128, xnT, 16,
                          [rp, R_xnT], [R_bank[bk]])
            hg_elementwise(half, sample)
        if STG <= 2:
            return
        for j in range(2):
            pan, rp = load_panel(wb_in[4 + j], 16, "in")
            mm_tm(bank(5), xnT, pan, 0, 512, 16, [rp, R_xnT], [R_bank[5]])
            P.op("act", lambda e, j=j: e.activation(out=vtm[:, j * 512:(j + 1) * 512], in_=bank(5), func=AF.Copy),
                 r=[R_bank[5]], w=[R_v])
        pan, rp = load_panel(wb_in[8], 16, "in")
        mm_tm(bank(6), xnT, pan, 0, 512, 16, [rp, R_xnT], [R_bank[6]])
        pan, rp = load_panel(wb_in[9], 16, "in", cw=320)
        mm_tm(bank(7)[:, 0:320], xnT, pan, 0, 320, 16, [rp, R_xnT], [R_bank[7]])
        P.phase = "%d:latents" % T
        mla_latents(T, sample, cm, r_cm, sm, r_sm)
        if STG <= 3:
            return
        P.phase = "%d:hgrec" % T
        hg_recurrence(T, sample)
        P.phase = "%d:attn" % T
        if STG <= 4:
            return
        if sample:
            barrier()
            for h in range(H):
                if h % 2 == 0:
                    wqp, r_wqp = load_panel(wb_q[h // 2], 4, "q")
                mla_q(h, cT, r_cT, sT, r_sT, QaugT_s[:], R_Qs, wqp, r_wqp)
                for c in range(3):
                    rows = 128 if c < 2 else 64
                    P.op("dve" if c != 1 else "act",
                         (lambda e, c=c, rows=rows, h=h: e.tensor_copy(
                             out=QaugT_all[0:rows, c, :, h * 8:(h + 1) * 8],
                             in_=QaugT_s[0:rows, c, :].rearrange("p (b t) -> p b t", t=8))) if c != 1 else
                         (lambda e, c=c, rows=rows, h=h: e.activation(
                             out=QaugT_all[0:rows, c, :, h * 8:(h + 1) * 8],
                             in_=QaugT_s[0:rows, c, :].rearrange("p (b t) -> p b t", t=8), func=AF.Copy)),
                         r=[R_Qs], w=[R_Qall])
            sample_attention()
        else:
            Qs2 = [(QaugT, R_Q), (QaugT_b, R_Qb)]
            wqp, r_wqp = load_panel(wb_q[0], 4, "q")
            mla_q(0, cT, r_cT, sT, r_sT, QaugT[:], R_Q, wqp, r_wqp)
            pa_S(T, 0, Qs2[0])
            for h in range(H):
                pa_softmax(T, h)
                if h + 1 < H:
                    if (h + 1) % 2 == 0:
                        wqp, r_wqp = load_panel(wb_q[(h + 1) // 2], 4, "q")
                    Qn, r_Qn = Qs2[(h + 1) % 2]
                    mla_q1(h + 1, cT, r_cT, sT, r_sT, Qn, r_Qn, wqp, r_wqp)
                pa_PT(T, h)
                if h + 1 < H:
                    mla_q2(h + 1, Qn, r_Qn)
                pa_rest(T, h)
                if h + 1 < H:
                    pa_S(T, h + 1, Qs2[(h + 1) % 2])
        if DBG and sample:
            P.dma("sp", lambda e: e.dma_start(out=dbg1, in_=mixT[:].rearrange("p k t -> p (k t)")), r=[R_mix], key="dbg1")
            P.dma("sp", lambda e: e.dma_start(out=dbg2[:, 0:256], in_=accs[:]), r=[R_acc], key="dbg2")
            P.dma("sp", lambda e: e.dma_start(out=dbg2[:, 256:264], in_=fst[:]), r=[R_fst], key="dbg2")
            P.dma("sp", lambda e: e.dma_start(out=dbg3, in_=idx[:]), r=[R_idx], key="dbg3")
            pass
        if STG <= 5:
            return
        P.phase = "%d:w_out" % T
        for cb in range(4):
            pan, rp = load_panel(wb_out[cb], 16, "out")
            bk = 5 + (cb % 2)
            mm_tm(bank(bk), mixT, pan, 0, 512, 16, [rp, R_mix], [R_bank[bk]])
            P.op("dve", lambda e, cb=cb, bk=bk: e.tensor_tensor(out=xt[:, cb * 512:(cb + 1) * 512],
                                                                in0=xt[:, cb * 512:(cb + 1) * 512], in1=bank(bk), op=ALU.add),
                 r=[R_bank[bk], R_x], w=[R_x])

        if STG <= 6:
            return
        P.phase = "%d:ffn_up" % T
        norm_to_xnT(1)
        need_atm = sample or T == 15
        if sample:
            barrier()
            for g3 in range(3):
                n = 16 if g3 < 2 else 12
                P.dma("sp", lambda e, g3=g3, n=n: e.dma_start(out=sctm[:, 0:n * 128], in_=stc_d[:, g3 * 2048:g3 * 2048 + n * 128]),
                      w=[R_sctm], key="sctm")
                for i in range(n):
                    P.op("pe", lambda e, i=i, g3=g3: e.transpose(out=bank(2 + g3)[:, i * 32:(i + 1) * 32],
                                                                 in_=sctm[:, i * 128:(i + 1) * 128], identity=identf[0:32, 0:32]),
                         r=[R_sctm, R_const], w=[R_bank[2 + g3]])
                P.op("dve", lambda e, g3=g3, n=n: e.tensor_copy(out=sconvT[:, g3 * 16:g3 * 16 + n, :].rearrange("p c j -> p (c j)"),
                                                                in_=bank(2 + g3)[:, 0:n * 32]),
                     r=[R_bank[2 + g3]], w=[R_sconv])
        for pa in range(11):
            pana, rpa = load_panel(wb_up[pa], 16, "up")
            pang, rpg = load_panel(wb_up[11 + pa], 16, "up")
            if need_atm:
                mm_tm(bank(7), xnT, pana, 0, 512, 16, [rpa, R_xnT], [R_bank[7]])
                at_, r_at, k_at = atmr.next()
                P.op("act", lambda e, at_=at_: e.activation(out=at_[:], in_=bank(7), func=AF.Copy),
                     r=[R_bank[7]], w=[r_at])
                if sample:
                    for t in range(2):
                        P.dma("sp", lambda e, t=t, at_=at_, pa=pa: e.dma_start(out=cvs_o[:, t, pa * 512:(pa + 1) * 512],
                                                                               in_=at_[6 + t:128:8, :]), r=[r_at], key=k_at)
                else:
                    P.dma("sp", lambda e, at_=at_, pa=pa: e.dma_start(out=cvp_o[:, pa * 512:(pa + 1) * 512], in_=at_[126:128, :]),
                          r=[r_at], key=k_at)
            bA = 2 + (pa % 2)
            bG = 4 + (pa % 2)
            for j in range(4):
                mm_fm(bank(bA)[:, j * 128:(j + 1) * 128], pana, j * 128, 128, xnT, 16, [rpa, R_xnT], [R_bank[bA]])
            for j in range(4):
                mm_fm(bank(bG)[:, j * 128:(j + 1) * 128], pang, j * 128, 128, xnT, 16, [rpg, R_xnT], [R_bank[bG]])
            if sample:
                for j in range(4):
                    ffn_chunk(pa * 4 + j, bank(bA)[:, j * 128:(j + 1) * 128], R_bank[bA],
                              bank(bG)[:, j * 128:(j + 1) * 128], R_bank[bG], True)
            else:
                ffn_group(pa, bA, bG)
        P.phase = "%d:ffn_dn" % T
        for cb in range(4):
            bk = 5 + (cb % 2)
            for kg in range(4):
                pan, rp = load_panel(wb_dn[cb * 4 + kg], 11, "dn")
                mm_tm(bank(bk), hT, pan, 0, 512, 11, [rp, R_hT], [R_bank[bk]], kc0=kg * 11,
                      first=(kg == 0), last=(kg == 3))
            P.op("dve", lambda e, cb=cb, bk=bk: e.tensor_tensor(out=xt[:, cb * 512:(cb + 1) * 512],
                                                                in0=xt[:, cb * 512:(cb + 1) * 512], in1=bank(bk), op=ALU.add),
                 r=[R_bank[bk], R_x], w=[R_x])

        if STG <= 7:
            return
        P.phase = "%d:ple" % T
        norm_to_xnT(2)
        P.dma("sp", lambda e: e.dma_start(out=ptile[:], in_=p_d[tok0:tok0 + 128, :]), w=[R_ptile], key="ptile")
        P.op("dve", lambda e: e.tensor_copy(out=pbf[:], in_=ptile[:]), r=[R_ptile], w=[R_pbf])
        for kc in range(2):
            P.op("pe", lambda e, kc=kc: e.transpose(out=T2[:, kc * 128:(kc + 1) * 128], in_=pbf[:, kc * 128:(kc + 1) * 128],
                                                    identity=ident[:]), r=[R_pbf, R_const], w=[R_T])
        P.op("dve", lambda e: e.tensor_copy(out=pT[:].rearrange("p k t -> p (k t)"), in_=T2[:, 0:256]), r=[R_T], w=[R_pT])
        for cb in range(4):
            pan, rp = load_panel(wb_pg[cb], 16, "pg")
            pan2, rp2 = load_panel(wb_pp[cb], 2, "pp")
            bk = 5 + (cb % 2)
            mm_tm(bank(bk), xnT, pan, 0, 512, 16, [rp, R_xnT], [R_bank[bk]])
            mm_tm(bank(7), pT, pan2, 0, 512, 2, [rp2, R_pT], [R_bank[7]])
            sg, r_sg, _ = sgt.next()
            P.op("act", lambda e, sg=sg, bk=bk: e.activation(out=sg[:], in_=bank(bk), func=AF.Sigmoid),
                 r=[R_bank[bk]], w=[r_sg])
            P.op("dve", lambda e, sg=sg: e.tensor_tensor(out=sg[:], in0=sg[:], in1=bank(7), op=ALU.mult),
                 r=[R_bank[7], r_sg], w=[r_sg])
            P.op("dve", lambda e, sg=sg, cb=cb: e.tensor_tensor(out=xt[:, cb * 512:(cb + 1) * 512],
                                                                in0=xt[:, cb * 512:(cb + 1) * 512], in1=sg[:], op=ALU.add),
                 r=[r_sg, R_x], w=[R_x])

        if STG <= 8:
            return
        P.op("act", lambda e: e.activation(out=xsq[:], in_=xt[:], func=AF.Square, accum_out=st1[:, 0:1]),
             r=[R_x], w=[R_xsq, R_st])
        rstd_from_ss(st1[:, 0:1], st1[:, 1:2], D, [R_st, R_const], [R_st])
        gt_, r_gt, k_gt = panels.next()
        gfin = gt_[:, 0:8, :].rearrange("p k c -> p (k c)").bitcast(F32)
        P.dma("sp", lambda e, gfin=gfin: e.dma_start(out=gfin, in_=nfin_d.partition_broadcast(128)), w=[r_gt], key=k_gt)
        P.op("dve", lambda e, gfin=gfin: e.scalar_tensor_tensor(out=xt[:], in0=xt[:], scalar=st1[:, 1:2], in1=gfin,
                                                                op0=ALU.mult, op1=ALU.mult),
             r=[R_x, R_st, r_gt], w=[R_x])
        P.dma("sp", lambda e: e.dma_start(out=y_o[tok0:tok0 + 128, :], in_=xt[:]), r=[R_x], key="yout")

    def hg_elementwise(half, sample):
        h0 = half * 4
        R = R_hg[half]
        qs, sg, ff, kk, cum, E = tA
        rq, rsg, rff, rkk, rcum, rE = R_tA
        mask = m8 if sample else m64
        hsl = slice(h0, h0 + 4)

        def v3(t):
            return t[:].rearrange("p (h t) -> p h t", h=4)

        P.op("act", lambda e: e.activation(out=qs[:], in_=bank(2), func=AF.Silu), r=[R_bank[2]], w=[rq])
        P.op("act", lambda e: e.activation(out=sg[:], in_=bank(3), func=AF.Sigmoid), r=[R_bank[3]], w=[rsg])
        P.op("act", lambda e: e.activation(out=gateT[:, hsl, :].rearrange("p h t -> p (h t)"), in_=bank(4), func=AF.Sigmoid),
             r=[R_bank[4]], w=[R])
        for hh in range(4):
            h = h0 + hh
            P.op("dve", lambda e, hh=hh, h=h: e.tensor_scalar(out=ff[:, hh * 128:(hh + 1) * 128], in0=sg[:, hh * 128:(hh + 1) * 128],
                                                              scalar1=omlT[:, h:h + 1], scalar2=lbT[:, h:h + 1],
                                                              op0=ALU.mult, op1=ALU.add), r=[rsg, R_lb], w=[rff])
            P.op("dve", lambda e, hh=hh, h=h: e.tensor_scalar(out=kk[:, hh * 128:(hh + 1) * 128], in0=sg[:, hh * 128:(hh + 1) * 128],
                                                              scalar1=nomlT[:, h:h + 1], scalar2=omlT[:, h:h + 1],
                                                              op0=ALU.mult, op1=ALU.add), r=[rsg, R_lb], w=[rkk])
        P.op("act", lambda e: e.activation(out=ff[:], in_=ff[:], func=AF.Ln), r=[rff], w=[rff])
        P.op("dve", lambda e: e.tensor_tensor_scan(out=cum[:], data0=mask[:], data1=ff[:], initial=0.0,
                                                   op0=ALU.mult, op1=ALU.add), r=[rff, R_m], w=[rcum])
        P.op("act", lambda e: e.activation(out=Edec[:, hsl, :].rearrange("p h t -> p (h t)"), in_=cum[:], func=AF.Exp),
             r=[rcum], w=[R])
        P.op("act", lambda e: e.activation(out=E[:], in_=cum[:], func=AF.Exp, scale=-1.0), r=[rcum], w=[rE])
        P.op("dve", lambda e: e.tensor_tensor(out=qdT[:, hsl, :].rearrange("p h t -> p (h t)"), in0=qs[:],
                                              in1=Edec[:, hsl, :].rearrange("p h t -> p (h t)"), op=ALU.mult),
             r=[rq, R], w=[R])
        P.op("dve", lambda e: e.tensor_tensor(out=kk[:], in0=kk[:], in1=E[:], op=ALU.mult), r=[rkk, rE], w=[rkk])
        P.op("act", lambda e: e.activation(out=kiT[:, hsl, :].rearrange("p h t -> p (h t)"), in_=kk[:], func=AF.Copy),
             r=[rkk], w=[R])
        seg = 8 if sample else 64
        ns = 512 // seg
        kk3 = kk[:].rearrange("p (s t) -> p s t", t=seg)
        ed3 = Edec[:, hsl, :].rearrange("p h (s t) -> p (h s) t", t=seg)
        P.op("dve", lambda e: e.tensor_tensor(out=keT[:, hsl, :].rearrange("p h (s t) -> p (h s) t", t=seg), in0=kk3,
                                              in1=ed3[:, :, seg - 1:seg].to_broadcast([128, ns, seg]), op=ALU.mult),
             r=[rkk, R], w=[R])

    def mla_latents(T, sample, cm, r_cm, sm, r_sm):
        tok0 = T * 128
        P.op("act", lambda e: e.activation(out=xsq[:, 0:512], in_=bank(6), func=AF.Square, accum_out=st1[:, 2:3]),
             r=[R_bank[6]], w=[R_xsq, R_st])
        rstd_from_ss(st1[:, 2:3], st1[:, 3:4], 512, [R_st, R_const], [R_st])
        P.op("dve", lambda e: e.scalar_tensor_tensor(out=cqn[:], in0=bank(6), scalar=st1[:, 3:4], in1=qnorm_bc[:],
                                                     op0=ALU.mult, op1=ALU.mult), r=[R_bank[6], R_st, R_const], w=[R_cqn])
        for kc in range(4):
            P.op("pe", lambda e, kc=kc: e.transpose(out=T2[:, kc * 128:(kc + 1) * 128], in_=cqn[:, kc * 128:(kc + 1) * 128],
                                                    identity=ident[:]), r=[R_cqn, R_const], w=[R_T])
        P.op("dve", lambda e: e.tensor_copy(out=cqnT[:].rearrange("p k t -> p (k t)"), in_=T2[:, 0:512]),
             r=[R_T], w=[R_cqnT])
        cf, r_cf, k_cf = ckvf.next()
        P.op("act", lambda e: e.activation(out=xsq[:, 512:768], in_=bank(7)[:, 0:256], func=AF.Square, accum_out=st1[:, 4:5]),
             r=[R_bank[7]], w=[R_xsq, R_st])
        rstd_from_ss(st1[:, 4:5], st1[:, 5:6], 256, [R_st, R_const], [R_st])
        P.op("dve", lambda e: e.scalar_tensor_tensor(out=cf[:], in0=bank(7)[:, 0:256], scalar=st1[:, 5:6], in1=kvnorm_bc[:],
                                                     op0=ALU.mult, op1=ALU.mult), r=[R_bank[7], R_st, R_const], w=[r_cf])
        P.dma("sp", lambda e: e.dma_start(out=ckv_o[tok0:tok0 + 128, :], in_=cf[:]), r=[r_cf], key=k_cf)
        cbf = ckvbf_s[:] if sample else ckv_tm[:, T, :]
        RK = R_Ks if sample else R_K
        P.op("act", lambda e: e.activation(out=cbf, in_=cf[:], func=AF.Copy), r=[r_cf], w=[RK])
        kf, r_kf, k_kf = krf.next()
        x1 = bank(7)[:, 256:288]
        x2 = bank(7)[:, 288:320]
        tt = krtmp
        P.op("dve", lambda e: e.tensor_tensor(out=tt[:, 0, :], in0=x1, in1=cm[:], op=ALU.mult), r=[R_bank[7], r_cm], w=[R_krtmp])
        P.op("dve", lambda e: e.tensor_tensor(out=tt[:, 1, :], in0=x2, in1=sm[:], op=ALU.mult), r=[R_bank[7], r_sm], w=[R_krtmp])
        P.op("dve", lambda e: e.tensor_tensor(out=tt[:, 2, :], in0=x2, in1=cm[:], op=ALU.mult), r=[R_bank[7], r_cm], w=[R_krtmp])
        P.op("dve", lambda e: e.tensor_tensor(out=tt[:, 3, :], in0=x1, in1=sm[:], op=ALU.mult), r=[R_bank[7], r_sm], w=[R_krtmp])
        P.op("dve", lambda e: e.tensor_tensor(out=kf[:, 0:32], in0=tt[:, 0, :], in1=tt[:, 1, :], op=ALU.subtract),
             r=[R_krtmp], w=[r_kf])
        P.op("dve", lambda e: e.tensor_tensor(out=kf[:, 32:64], in0=tt[:, 2, :], in1=tt[:, 3, :], op=ALU.add),
             r=[R_krtmp], w=[r_kf])
        P.dma("sp", lambda e: e.dma_start(out=kro_o[tok0:tok0 + 128, :], in_=kf[:]), r=[r_kf], key=k_kf)
        P.op("act", lambda e: e.activation(out=krbf[:], in_=kf[:], func=AF.Copy), r=[r_kf], w=[R_krtmp])
        for kc in range(2):
            P.op("pe", lambda e, kc=kc: e.transpose(out=T2[:, kc * 128:(kc + 1) * 128], in_=cbf[:, kc * 128:(kc + 1) * 128],
                                                    identity=ident[:]), r=[RK, R_const], w=[R_T])
        P.op("pe", lambda e: e.transpose(out=T2[0:64, 256:384], in_=krbf[:], identity=ident[:]), r=[R_krtmp, R_const], w=[R_T])
        Kd = KaugT_s if sample else KaugT
        c0 = 0 if sample else tok0
        P.op("dve", lambda e: e.tensor_copy(out=Kd[:, 0:2, c0:c0 + 128], in_=T2[:, 0:256].rearrange("p (k t) -> p k t", k=2)),
             r=[R_T], w=[RK])
        P.op("act", lambda e: e.activation(out=Kd[0:64, 2, c0:c0 + 128], in_=T2[0:64, 256:384], func=AF.Copy),
             r=[R_T], w=[RK])

    def mla_q1(h, cT, r_cT, sT, r_sT, Qdst, RQ, wqp, r_wqp):
        hb = (h % 2) * 256
        for kc in range(4):
            P.op("pe", lambda e, kc=kc: e.matmul(bank(6)[:, 0:128], lhsT=wqp[:, kc, hb:hb + 128], rhs=cqnT[:, kc, :],
                                                 start=(kc == 0), stop=(kc == 3)), r=[r_wqp, R_cqnT], w=[R_bank[6]])
        for kc in range(4):
            P.op("pe", lambda e, kc=kc: e.matmul(bank(6)[0:64, 128:256], lhsT=wqp[:, kc, hb + 128:hb + 192], rhs=cqnT[:, kc, :],
                                                 start=(kc == 0), stop=(kc == 3)), r=[r_wqp, R_cqnT], w=[R_bank[6]])
        for kc in range(4):
            P.op("pe", lambda e, kc=kc: e.matmul(bank(6)[0:64, 256:384], lhsT=wqp[:, kc, hb + 192:hb + 256], rhs=cqnT[:, kc, :],
                                                 start=(kc == 0), stop=(kc == 3)), r=[r_wqp, R_cqnT], w=[R_bank[6]])
        P.op("act", lambda e: e.activation(out=qnT[:], in_=bank(6)[:, 0:128], func=AF.Copy), r=[R_bank[6]], w=[R_qnT])
        P.op("dve", lambda e: e.tensor_tensor(out=qtmp[:, 0, :], in0=bank(6)[0:64, 128:256], in1=cT[:], op=ALU.mult),
             r=[R_bank[6], r_cT], w=[R_qtmp])
        P.op("dve", lambda e: e.tensor_tensor(out=qtmp[:, 1, :], in0=bank(6)[0:64, 256:384], in1=sT[:], op=ALU.mult),
             r=[R_bank[6], r_sT], w=[R_qtmp])
        P.op("dve", lambda e: e.tensor_tensor(out=Qdst[0:64, 2, :], in0=qtmp[:, 0, :], in1=qtmp[:, 1, :], op=ALU.add),
             r=[R_qtmp], w=[RQ])

    def mla_q2(h, Qdst, RQ):
        for ck in range(2):
            P.op("pe", lambda e, ck=ck: e.matmul(bank(7)[:, ck * 128:(ck + 1) * 128], lhsT=wukT[:, h, ck * 128:(ck + 1) * 128],
                                                 rhs=qnT[:], start=True, stop=True), r=[R_wukT, R_qnT], w=[R_bank[7]])
        P.op("act", lambda e: e.activation(out=Qdst[:, 0:2, :], in_=bank(7)[:, 0:256].rearrange("p (k t) -> p k t", k=2),
                                           func=AF.Copy, scale=SCALE), r=[R_bank[7]], w=[RQ])

    def mla_q(h, cT, r_cT, sT, r_sT, Qdst, RQ, wqp, r_wqp):
        mla_q1(h, cT, r_cT, sT, r_sT, Qdst, RQ, wqp, r_wqp)
        mla_q2(h, Qdst, RQ)

    def pa_S(T, h, Qs):
        nk = (T + 1) * 128
        ngrp = (nk + 511) // 512
        Sreg = ps[:, 1024:1024 + 2048]
        RS = [R_bank[2], R_bank[3], R_bank[4], R_bank[5]]
        Qt, r_Q = Qs
        for g in range(ngrp):
            k0 = g * 512
            kw = min(512, nk - k0)
            isdiag = (g == ngrp - 1)
            for c in range(3):
                rows = 128 if c < 2 else 64
                P.op("pe", lambda e, c=c, rows=rows, k0=k0, kw=kw, isdiag=isdiag: e.matmul(
                    Sreg[:, k0:k0 + kw], lhsT=Qt[0:rows, c, :], rhs=KaugT[0:rows, c, k0:k0 + kw],
                    start=(c == 0), stop=(c == 2 and not isdiag)), r=[r_Q, R_K], w=[RS[g]])
            if isdiag:
                P.op("pe", lambda e: e.matmul(Sreg[:, T * 128:(T + 1) * 128], lhsT=ident[:], rhs=cmask[:],
                                              start=False, stop=True), r=[R_const], w=[RS[g]])

    def pa_softmax(T, h):
        nk = (T + 1) * 128
        ngrp = (nk + 511) // 512
        Sreg = ps[:, 1024:1024 + 2048]
        used = [R_bank[2], R_bank[3], R_bank[4], R_bank[5]][:ngrp]
        P.op("dve", lambda e: e.tensor_reduce(out=st2[:, 0:1], in_=Sreg[:, 0:nk], axis=AX.X, op=ALU.max),
             r=used, w=[R_st2])
        P.op("dve", lambda e: e.tensor_scalar(out=st2[:, 0:1], in0=st2[:, 0:1], scalar1=-1.0, scalar2=None, op0=ALU.mult),
             r=[R_st2], w=[R_st2])
        P.op("act", lambda e: e.activation(out=Pf[:, 0:nk], in_=Sreg[:, 0:nk], func=AF.Exp, bias=st2[:, 0:1],
                                           accum_out=st2[:, 1:2]), r=used + [R_st2], w=[R_Pf, R_st2])
        P.op("dve", lambda e: e.reciprocal(out=st2[:, 1:2], in_=st2[:, 1:2]), r=[R_st2], w=[R_st2])
        P.op("act", lambda e: e.activation(out=Pn[:, 0:nk], in_=Pf[:, 0:nk], func=AF.Copy, scale=st2[:, 1:2]),
             r=[R_Pf, R_st2], w=[R_Pn])

    def pa_PT(T, h):
        nk = (T + 1) * 128
        for b in range(T + 1):
            P.op("pe", lambda e, b=b: e.transpose(out=T2[:, b * 128:(b + 1) * 128], in_=Pn[:, b * 128:(b + 1) * 128],
                                                  identity=ident[:]), r=[R_Pn, R_const], w=[R_T])
        P.op("dve", lambda e: e.tensor_copy(out=PT[:, 0:T + 1, :].rearrange("p b q -> p (b q)"), in_=T2[:, 0:nk]),
             r=[R_T], w=[R_PT])

    def pa_rest(T, h):
        for ck in range(2):
            for b in range(T + 1):
                P.op("pe", lambda e, ck=ck, b=b: e.matmul(bank(6)[:, ck * 128:(ck + 1) * 128],
                                                          lhsT=ckv_tm[:, b, ck * 128:(ck + 1) * 128], rhs=PT[:, b, :],
                                                          start=(b == 0), stop=(b == T)), r=[R_K, R_PT], w=[R_bank[6]])
        P.op("act", lambda e: e.activation(out=latT[:].rearrange("p k t -> p (k t)"), in_=bank(6)[:, 0:256], func=AF.Copy),
             r=[R_bank[6]], w=[R_latT])
        for ck in range(2):
            P.op("pe", lambda e, ck=ck: e.matmul(bank(7)[:, 0:128], lhsT=wuv[:, ck, h, :], rhs=latT[:, ck, :],
                                                 start=(ck == 0), stop=(ck == 1)), r=[R_wq, R_latT], w=[R_bank[7]])
        P.op("dve", lambda e: e.tensor_copy(out=mixT[:, 8 + h, :], in_=bank(7)[:, 0:128]), r=[R_bank[7]], w=[R_mix])

    def hg_norm_gate(h, oT, r_oT):
        Rh = R_hg[h // 4]
        P.op("act", lambda e: e.activation(out=osq[:], in_=oT, func=AF.Square), r=[r_oT], w=[R_osq])
        P.op("pe", lambda e: e.matmul(bank(2)[:, 0:128], lhsT=ones[:], rhs=osq[:], start=True, stop=True),
             r=[R_osq, R_const], w=[R_bank[2]])
        P.op("act", lambda e: e.activation(out=orst[:], in_=bank(2)[:, 0:128], func=AF.Ln, scale=1.0 / 128, bias=epsb[:, 0:1]),
             r=[R_bank[2], R_const], w=[R_orst])
        P.op("act", lambda e: e.activation(out=orst[:], in_=orst[:], func=AF.Exp, scale=-0.5), r=[R_orst], w=[R_orst])
        P.op("dve", lambda e: e.tensor_tensor(out=otmp[:], in0=oT, in1=orst[:], op=ALU.mult),
             r=[r_oT, R_orst], w=[R_otmp])
        P.op("dve", lambda e: e.scalar_tensor_tensor(out=mixT[:, h, :], in0=otmp[:], scalar=onormT[:, h:h + 1],
                                                     in1=gateT[:, h, :], op0=ALU.mult, op1=ALU.mult),
             r=[R_otmp, R_const, Rh], w=[R_mix])

    def hg_recurrence(T, sample):
        bd = bd8 if sample else bd64
        for h in range(H):
            P.op("pe", lambda e, h=h: e.transpose(out=T2[:, h * 128:(h + 1) * 128], in_=keT[:, h, :], identity=ident[:]),
                 r=[R_hg[h // 4], R_const], w=[R_T])
        P.op("dve", lambda e: e.tensor_copy(out=ketm[:].rearrange("p h k -> p (h k)"), in_=T2[:, 0:1024]),
             r=[R_T], w=[R_ketm])
        if not sample:
            for h in range(H):
                Rh = R_hg[h // 4]
                vh = vtm[:, h * 128:(h + 1) * 128]
                P.op("pe", lambda e, h=h: e.matmul(bank(5)[:, 0:128], lhsT=kiT[:, h, :], rhs=qdT[:, h, :], start=True, stop=True),
                     r=[Rh], w=[R_bank[5]])
                P.op("dve", lambda e: e.tensor_tensor(out=ATm[:], in0=bank(5)[:, 0:128], in1=bd[:], op=ALU.mult),
                     r=[R_bank[5], R_const], w=[R_ATm])
                P.op("pe", lambda e, vh=vh: e.matmul(bank(6)[:, 0:128], lhsT=vh, rhs=ATm[:], start=True, stop=False),
                     r=[R_v, R_ATm], w=[R_bank[6]])
                for c in range(2):
                    cs = slice(c * 64, (c + 1) * 64)
                    P.op("pe", lambda e, h=h, cs=cs, c=c: e.matmul(bank(6)[:, cs], lhsT=Sbf[:, h, :], rhs=qdT[:, h, cs],
                                                                   start=False, stop=(c == 1)), r=[R_S[h], Rh], w=[R_bank[6]])
                    P.op("pe", lambda e, h=h, cs=cs, vh=vh: e.matmul(bank(7)[:, 0:128], lhsT=ketm[cs, h, :], rhs=vh[cs, :],
                                                                     start=True, stop=True), r=[R_ketm, R_v], w=[R_bank[7]])
                    dcol = Edec[:, h, c * 64 + 63:c * 64 + 64]
                    P.op("dve", lambda e, h=h, dcol=dcol: e.scalar_tensor_tensor(out=Sst[:, h, :], in0=Sst[:, h, :], scalar=dcol,
                                                                                 in1=bank(7)[:, 0:128], op0=ALU.mult, op1=ALU.add),
                         r=[R_bank[7], Rh, R_S[h]], w=[R_S[h]])
                    P.op("act", lambda e, h=h: e.activation(out=Sbf[:, h, :], in_=Sst[:, h, :], func=AF.Copy),
                         r=[R_S[h]], w=[R_S[h]])
                hg_norm_gate(h, bank(6)[:, 0:128], R_bank[6])
            if T == 15:
                P.dma("sp", lambda e: e.dma_start(out=hgp_o.rearrange("h k v -> k h v"), in_=Sst[:]), r=R_S, key="hgp")
            return
        def oTap(h):
            return bank(3 + h // 4)[:, (h % 4) * 128:(h % 4 + 1) * 128]
        for h in range(H):
            Rh = R_hg[h // 4]
            vh = vtm[:, h * 128:(h + 1) * 128]
            P.op("pe", lambda e, h=h: e.matmul(bank(5)[:, 0:128], lhsT=kiT[:, h, :], rhs=qdT[:, h, :], start=True, stop=True),
                 r=[Rh], w=[R_bank[5]])
            P.op("dve", lambda e: e.tensor_tensor(out=ATm[:], in0=bank(5)[:, 0:128], in1=bd[:], op=ALU.mult),
                 r=[R_bank[5], R_const], w=[R_ATm])
            P.op("pe", lambda e, vh=vh, h=h: e.matmul(oTap(h), lhsT=vh, rhs=ATm[:], start=(h % 4 == 0), stop=False,
                                                      skip_group_check=True),
                 r=[R_v, R_ATm], w=[R_bank[3 + h // 4]])
        for b in range(NB):
            sl, r_sl, k_sl = Sload.next()
            slb, r_slb, _ = Sloadbf.next()
            sn, r_sn, k_sn = sl, r_sl, k_sl + "o"
            P.dma("sp", lambda e, sl=sl, b=b: e.dma_start(out=sl[:], in_=sth_d[b].rearrange("h k v -> k h v")), w=[r_sl], key=k_sl)
            P.op("act", lambda e, sl=sl, slb=slb: e.activation(out=slb[:].rearrange("p h v -> p (h v)"),
                                                               in_=sl[:].rearrange("p h v -> p (h v)"), func=AF.Copy),
                 r=[r_sl], w=[r_slb])
            for h in range(H):
                Rh = R_hg[h // 4]
                vh = vtm[:, h * 128:(h + 1) * 128]
                cs = slice(b * 8, (b + 1) * 8)
                P.op("pe", lambda e, h=h, cs=cs, slb=slb, b=b: e.matmul(oTap(h)[:, cs], lhsT=slb[:, h, :], rhs=qdT[:, h, cs],
                                                                        start=False, stop=(b == NB - 1), skip_group_check=True),
                     r=[r_slb, Rh], w=[R_bank[3 + h // 4]])
                km, r_km, _ = kemr.next()
                P.op("dve" if h % 2 == 0 else "pool",
                     lambda e, h=h, b=b, km=km: e.tensor_scalar(out=km[:], in0=ketm[:, h, :], scalar1=bmask[:, b:b + 1],
                                                                scalar2=None, op0=ALU.mult), r=[R_ketm, R_const], w=[r_km])
                bk = 6 + (h % 2)
                P.op("pe", lambda e, km=km, vh=vh, bk=bk: e.matmul(bank(bk)[:, 0:128], lhsT=km[:], rhs=vh, start=True, stop=True),
                     r=[r_km, R_v], w=[R_bank[bk]])
                dcol = Edec[:, h, b * 8 + 7:b * 8 + 8]
                P.op("dve", lambda e, h=h, dcol=dcol, sl=sl, sn=sn, bk=bk: e.scalar_tensor_tensor(
                    out=sn[:, h, :], in0=sl[:, h, :], scalar=dcol, in1=bank(bk)[:, 0:128], op0=ALU.mult, op1=ALU.add),
                    r=[R_bank[bk], Rh, r_sl], w=[r_sn])
            P.dma("sp", lambda e, sn=sn, b=b: e.dma_start(out=hgs_o[b].rearrange("h k v -> k h v"), in_=sn[:]), r=[r_sn], key=k_sn)
        for h in range(H):
            hg_norm_gate(h, oTap(h), R_bank[3 + h // 4])

    kiT = sb("kiT", [128, H, 128], BF16)
    kemr = Rot("kemr", [128, 128], BF16, 3)

    ae4 = sb("ae4", [128, 4, 130], F32)
    ca4 = sb("ca4", [128, 4, 128], F32)
    cb4 = sb("cb4", [128, 4, 128], F32)
    R_ae4 = Res("ae4")
    R_ca4 = Res("ca4")
    R_cb4 = Res("cb4")

    def ffn_group(pa, bA, bG):
        c0 = pa * 4
        P.op("act", lambda e: e.activation(out=ae4[:, :, 0:2], in_=convc[:, c0:c0 + 4, :], func=AF.Copy), r=[R_convc], w=[R_ae4])
        P.op("act", lambda e: e.activation(out=ae4[:, :, 2:130], in_=bank(bA).rearrange("p (c t) -> p c t", c=4), func=AF.Copy),
             r=[R_bank[bA]], w=[R_ae4])
        P.op("act", lambda e: e.activation(out=convc[:, c0:c0 + 4, :], in_=ae4[:, :, 128:130], func=AF.Copy), r=[R_ae4], w=[R_convc])
        for j in range(4):
            ci = c0 + j
            P.op("dve", lambda e, j=j, ci=ci: e.tensor_scalar(out=ca4[:, j, :], in0=ae4[:, j, 0:128], scalar1=cwT[:, 0, ci:ci + 1],
                                                              scalar2=cbT[:, ci:ci + 1], op0=ALU.mult, op1=ALU.add),
                 r=[R_ae4, R_const], w=[R_ca4])
            P.op("dve", lambda e, j=j, ci=ci: e.scalar_tensor_tensor(out=cb4[:, j, :], in0=ae4[:, j, 1:129], scalar=cwT[:, 1, ci:ci + 1],
                                                                     in1=ca4[:, j, :], op0=ALU.mult, op1=ALU.add),
                 r=[R_ae4, R_ca4, R_const], w=[R_cb4])
            P.op("dve", lambda e, j=j, ci=ci: e.scalar_tensor_tensor(out=ca4[:, j, :], in0=ae4[:, j, 2:130], scalar=cwT[:, 2, ci:ci + 1],
                                                                     in1=cb4[:, j, :], op0=ALU.mult, op1=ALU.add),
                 r=[R_ae4, R_cb4, R_const], w=[R_ca4])
        P.op("act", lambda e: e.activation(out=cb4[:].rearrange("p c t -> p (c t)"), in_=ca4[:].rearrange("p c t -> p (c t)"), func=AF.Silu),
             r=[R_ca4], w=[R_cb4])
        P.op("dve", lambda e: e.tensor_tensor(out=hT[:, c0:c0 + 4, :].rearrange("p c t -> p (c t)"), in0=cb4[:].rearrange("p c t -> p (c t)"),
                                              in1=bank(bG), op=ALU.mult), r=[R_cb4, R_bank[bG]], w=[R_hT])

    def ffn_chunk(ci, a_ap, r_a, g_ap, r_g, sample):
        ae, r_ae, _ = aext.next()
        ca, r_ca, _ = cva.next()
        cb_, r_cb, _ = cvb.next()
        if not sample:
            P.op("act", lambda e: e.activation(out=ae[:, 0:2], in_=convc[:, ci, :], func=AF.Copy), r=[R_convc], w=[r_ae])
            P.op("act", lambda e: e.activation(out=ae[:, 2:130], in_=a_ap, func=AF.Copy), r=[r_a], w=[r_ae])
            P.op("act", lambda e: e.activation(out=convc[:, ci, :], in_=ae[:, 128:130], func=AF.Copy), r=[r_ae], w=[R_convc])
            s0, s1, s2 = ae[:, 0:128], ae[:, 1:129], ae[:, 2:130]
            o1, o2 = ca[:], cb_[:]
        else:
            ae3 = ae[:].rearrange("p (b t) -> p b t", t=10)
            P.op("act", lambda e: e.activation(out=ae3[:, :, 0:2], in_=sconvT[:, ci, :].rearrange("p (b j) -> p b j", j=2),
                                               func=AF.Copy), r=[R_sconv], w=[r_ae])
            P.op("act", lambda e: e.activation(out=ae3[:, :, 2:10], in_=a_ap.rearrange("p (b t) -> p b t", t=8),
                                               func=AF.Copy), r=[r_a], w=[r_ae])
            s0, s1, s2 = ae3[:, :, 0:8], ae3[:, :, 1:9], ae3[:, :, 2:10]
            o1 = ca[:].rearrange("p (b t) -> p b t", t=8)
            o2 = cb_[:].rearrange("p (b t) -> p b t", t=8)
        P.op("dve", lambda e: e.tensor_scalar(out=o1, in0=s0, scalar1=cwT[:, 0, ci:ci + 1], scalar2=cbT[:, ci:ci + 1],
                                              op0=ALU.mult, op1=ALU.add), r=[r_ae, R_const], w=[r_ca])
        P.op("dve", lambda e: e.scalar_tensor_tensor(out=o2, in0=s1, scalar=cwT[:, 1, ci:ci + 1], in1=o1,
                                                     op0=ALU.mult, op1=ALU.add), r=[r_ae, r_ca, R_const], w=[r_cb])
        P.op("dve", lambda e: e.scalar_tensor_tensor(out=o1, in0=s2, scalar=cwT[:, 2, ci:ci + 1], in1=o2,
                                                     op0=ALU.mult, op1=ALU.add), r=[r_ae, r_cb, R_const], w=[r_ca])
        P.op("act", lambda e: e.activation(out=cb_[:], in_=ca[:], func=AF.Silu), r=[r_ca], w=[r_cb])
        P.op("dve", lambda e: e.tensor_tensor(out=hT[:, ci, :], in0=cb_[:], in1=g_ap, op=ALU.mult),
             r=[r_cb, r_g], w=[R_hT])

    R_negm = Res("negm")
    R_T67 = Res("T67", excl=True)
    R_l = Res("l_run")
    R_par = [Res("par0"), Res("par1")]

    def sample_attention():
        NGT = NGRP + 1
        TR = (bankbf(0, 2), bankbf(6, 2))
        R_TR = (R_T, R_T67)
        for b in range(NB):
            Qb = QaugT_all[:, :, b, :]
            P.op("dve", lambda e: e.memset(accs[:], 0.0), w=[R_acc])
            P.op("dve", lambda e: e.memset(fst[:, 0:2], 1e30), w=[R_negm])
            P.op("dve", lambda e: e.memset(fst[:, 2:3], 0.0), w=[R_l])
            gath = {}

            def issue_gather(G, b=b, gath=gath):
                gc, r_gc, k_gc = Gc.next()
                gr, r_gr, k_gr = Gr.next()
                col = b * 16 + G
                P.dma("pool", lambda e, gc=gc, col=col: e.indirect_dma_start(
                    out=gc[:].rearrange("p s c -> p (s c)"), out_offset=None, in_=ck_d,
                    in_offset=bass.IndirectOffsetOnAxis(ap=idx[:, col:col + 1], axis=0)), r=[R_idx], w=[r_gc], key=k_gc)
                P.dma("pool", lambda e, gr=gr, col=col: e.indirect_dma_start(
                    out=gr[:].rearrange("p s c -> p (s c)"), out_offset=None, in_=kr_d,
                    in_offset=bass.IndirectOffsetOnAxis(ap=idx[:, col:col + 1], axis=0)), r=[R_idx], w=[r_gr], key=k_gr)
                gath[G] = (gc, r_gc, gr, r_gr)

            issue_gather(0)
            info = {}
            for it in range(NGT + 2):
                g = it
                if g % 2 == 0 and g // 2 + 1 < 16:
                    issue_gather(g // 2 + 1)
                if g < NGT:
                    selfg = (g == NGRP)
                    bS = 2 + (g % 2)
                    if not selfg:
                        gc, r_gc, gr, r_gr = gath[g // 2]
                        so = 4 * (g % 2)
                        kt, r_kt, _ = KT.next()
                        Tg = TR[g % 2]
                        r_Tg = R_TR[g % 2]
                        for s_ in range(4):
                            for c in range(2):
                                P.op("pe", lambda e, s_=s_, c=c, gc=gc, Tg=Tg, so=so: e.transpose(
                                    out=Tg[:, c * 512 + s_ * 128:c * 512 + (s_ + 1) * 128], in_=gc[:, so + s_, c * 128:(c + 1) * 128],
                                    identity=ident[:]), r=[r_gc, R_const], w=[r_Tg])
                            P.op("pe", lambda e, s_=s_, gr=gr, Tg=Tg, so=so: e.transpose(out=Tg[0:64, 1024 + s_ * 128:1024 + (s_ + 1) * 128],
                                                                                         in_=gr[:, so + s_, :], identity=ident[:]), r=[r_gr, R_const], w=[r_Tg])
                        P.op("dve", lambda e, kt=kt, Tg=Tg: e.tensor_copy(out=kt[:, 0:2, :].rearrange("p c k -> p (c k)"), in_=Tg[:, 0:1024]),
                             r=[r_Tg], w=[r_kt])
                        P.op("act", lambda e, kt=kt, Tg=Tg: e.activation(out=kt[0:64, 2, :], in_=Tg[0:64, 1024:1536], func=AF.Copy),
                             r=[r_Tg], w=[r_kt])
                        nkeys, ktv, r_ktv, nsl = 512, kt, r_kt, 4
                        vfun = (lambda s_, gc=gc, so=so: gc[:, so + s_, :])
                        r_v = r_gc
                    else:
                        nkeys, ktv, r_ktv, nsl = 128, KaugT_s, R_Ks, 1
                        vfun = (lambda s_: ckvbf_s[:])
                        r_v = R_Ks
                    for c in range(3):
                        rows = 128 if c < 2 else 64
                        P.op("pe", lambda e, c=c, rows=rows, ktv=ktv, bS=bS, nkeys=nkeys, selfg=selfg, Qb=Qb: e.matmul(
                            bank(bS)[0:64, 0:nkeys], lhsT=Qb[0:rows, c], rhs=ktv[0:rows, c, 0:nkeys],
                            start=(c == 0), stop=(c == 2 and not selfg)), r=[R_Qall, r_ktv], w=[R_bank[bS]])
                    if selfg:
                        P.op("pe", lambda e, bS=bS, b=b: e.matmul(bank(bS)[0:64, 0:128], lhsT=ident[0:64, 0:64],
                                                                  rhs=smask[:, 120 - 8 * b:248 - 8 * b], start=False, stop=True),
                             r=[R_const], w=[R_bank[bS]])
                    info[g] = [bS, nkeys, vfun, r_v, nsl, None, None]
                g = it - 1
                if 0 <= g < NGT:
                    bS, nkeys, vfun, r_v, nsl, _, _ = info[g]
                    par = g % 2
                    Rp = R_par[par]
                    old = fst[:, (g + 1) % 2:(g + 1) % 2 + 1]
                    new_ = fst[:, g % 2:g % 2 + 1]
                    pb, r_pb, _ = Pbs.next()
                    P.op("dve", lambda e, pb=pb, bS=bS, nkeys=nkeys, old=old, new_=new_: e.tensor_scalar(
                        out=pb[:, 0:nkeys], in0=bank(bS)[0:64, 0:nkeys], scalar1=-1.0, scalar2=old,
                        op0=ALU.mult, op1=ALU.min, accum_out=new_), r=[R_bank[bS], R_negm], w=[r_pb, R_negm])
                    corr = fst[:, 3 + par:4 + par]
                    rsum = fst[:, 5 + par:6 + par]
                    P.op("act", lambda e, corr=corr, old=old, new_=new_: e.activation(out=corr, in_=old, func=AF.Exp, scale=-1.0, bias=new_),
                         r=[R_negm], w=[Rp])
                    P.op("act", lambda e, pb=pb, bS=bS, nkeys=nkeys, new_=new_, rsum=rsum: e.activation(
                        out=pb[:, 0:nkeys], in_=bank(bS)[0:64, 0:nkeys], func=AF.Exp, bias=new_, accum_out=rsum),
                        r=[R_bank[bS], R_negm], w=[r_pb, Rp])
                    P.op("dve", lambda e, corr=corr, rsum=rsum: e.scalar_tensor_tensor(out=fst[:, 2:3], in0=fst[:, 2:3], scalar=corr, in1=rsum,
                                                                                     op0=ALU.mult, op1=ALU.add), r=[Rp, R_l], w=[R_l])
                    pt_, r_pt, _ = PTs.next()
                    po = par * 256
                    for s_ in range(nsl):
                        P.op("pe", lambda e, s_=s_, pb=pb, po=po: e.transpose(out=bankbf(4)[:, po + s_ * 64:po + (s_ + 1) * 64], in_=pb[:, s_ * 128:(s_ + 1) * 128],
                                                                              identity=ident[0:64, 0:64]), r=[r_pb, R_const], w=[R_bank[4]])
                    P.op("dve", lambda e, pt_=pt_, nsl=nsl, po=po: e.tensor_copy(out=pt_[:, 0:nsl, :].rearrange("p s q -> p (s q)"),
                                                                                 in_=bankbf(4)[:, po:po + nsl * 64]), r=[R_bank[4]], w=[r_pt])
                    info[g][5] = pt_
                    info[g][6] = r_pt
                g = it - 2
                if 0 <= g < NGT:
                    bS, nkeys, vfun, r_v, nsl, pt_, r_pt = info.pop(g)
                    par = g % 2
                    vo = par * 256
                    corr = fst[:, 3 + par:4 + par]
                    for s_ in range(nsl):
                        P.op("pe", lambda e, s_=s_, pt_=pt_, vfun=vfun, nsl=nsl, vo=vo: e.matmul(
                            bank(5)[0:64, vo:vo + 256], lhsT=pt_[:, s_, :], rhs=vfun(s_), start=(s_ == 0), stop=(s_ == nsl - 1)),
                            r=[r_pt, r_v], w=[R_bank[5]])
                    P.op("dve", lambda e, corr=corr, vo=vo: e.scalar_tensor_tensor(out=accs[:], in0=accs[:], scalar=corr, in1=bank(5)[0:64, vo:vo + 256],
                                                                                 op0=ALU.mult, op1=ALU.add), r=[R_bank[5], R_par[par], R_acc], w=[R_acc])
            P.op("dve", lambda e: e.reciprocal(out=fst[:, 7:8], in_=fst[:, 2:3]), r=[R_l], w=[R_fst])
            P.op("act", lambda e: e.activation(out=lat_s[:], in_=accs[:], func=AF.Copy, scale=fst[:, 7:8]),
                 r=[R_acc, R_fst], w=[R_lats])
            for ck in range(2):
                P.op("pe", lambda e, ck=ck: e.transpose(out=bankbf(4)[:, 512 + ck * 64:512 + (ck + 1) * 64],
                                                        in_=lat_s[:, ck * 128:(ck + 1) * 128], identity=ident[0:64, 0:64]),
                     r=[R_lats, R_const], w=[R_bank[4]])
            for ck in range(2):
                P.op("dve", lambda e, b=b, ck=ck: e.tensor_copy(
                    out=latT_s[:, ck, :, b * 8:(b + 1) * 8],
                    in_=bankbf(4)[:, 512 + ck * 64:512 + (ck + 1) * 64].rearrange("p (h t) -> p h t", h=H)),
                    r=[R_bank[4]], w=[R_latTs])
        for h in range(H):
            for ck in range(2):
                P.op("pe", lambda e, ck=ck, h=h: e.matmul(bank(7)[:, 0:128], lhsT=wuv[:, ck, h, :], rhs=latT_s[:, ck, h, :],
                                                          start=(ck == 0), stop=(ck == 1)), r=[R_wq, R_latTs], w=[R_bank[7], R_T67])
            P.op("dve", lambda e, h=h: e.tensor_copy(out=mixT[:, 8 + h, :], in_=bank(7)[:, 0:128]), r=[R_bank[7], R_T67], w=[R_mix])

    bar = sb("bar", [128, 8], F32)

    def barrier():
        P.op("dve", lambda e: e.memset(bar[:], 0.0), w=list(Res.ALL))

    for T in range(min(nblk_run, 16)):
        block(T)
    if do_sample:
        barrier()
        panq[1] = True
        P.dma("sp", lambda e: e.dma_start(out=m8[:], in_=m8_d), w=[R_m], key="m8")
        for q in range(8):
            P.dma("sp", lambda e, q=q: e.dma_start(out=ptb[16 * q:16 * q + 16, :],
                                                   in_=ptq_d[q].partition_broadcast(16)), w=[R_idx], key="ptb")
        P.op("dve", lambda e: e.tensor_scalar(out=idx[:], in0=ptb[:], scalar1=16.0, scalar2=rcol[:, 0:1],
                                              op0=ALU.mult, op1=ALU.add), r=[R_idx, R_const], w=[R_idx])
        block(16)

    P.finalize()
    global LAST_PROG
    LAST_PROG = P
    with nc.Block() as blk:
        @blk.sync
        def _(e):
            P.emit("sp", e, final_waits=True)

        @blk.tensor
        def _(e):
            P.emit("pe", e)

        @blk.scalar
        def _(e):
            P.emit("act", e)

        @blk.vector
        def _(e):
            P.emit("dve", e)

        @blk.gpsimd
        def _(e):
            P.emit("pool", e)
    es.close()
    return nc


def _consts():
    bf = ml_dtypes.bfloat16
    c = {}
    c["c_ident"] = np.eye(128, dtype=np.float32).astype(bf)
    c["c_identf"] = np.eye(128, dtype=np.float32)
    c["c_ones"] = np.ones((128, 128), np.float32).astype(bf)
    q = np.arange(128)[:, None]
    k = np.arange(128)[None, :]
    c["c_cmask"] = np.where(k <= q, 0.0, NEG).astype(np.float32).astype(bf)
    s = np.arange(128)[:, None]
    t = np.arange(128)[None, :]
    c["c_bd64"] = ((s // 64 == t // 64) & (s <= t)).astype(np.float32).astype(bf)
    c["c_bd8"] = ((s // 8 == t // 8) & (s <= t)).astype(np.float32).astype(bf)
    j = np.arange(512)
    c["c_m64"] = np.broadcast_to((j % 64 != 0).astype(np.float32), (128, 512)).copy()
    c["c_m8"] = np.broadcast_to((j % 8 != 0).astype(np.float32), (128, 512)).copy()
    c["c_bmask"] = (np.arange(128)[:, None] // 8 == np.arange(NB)[None, :]).astype(np.float32)
    sm = np.full((64, 248), NEG, np.float32)
    qt = np.arange(64) % 8
    for kk in range(8):
        sm[:, 120 + kk] = np.where(kk <= qt, 0.0, NEG)
    c["c_smask"] = sm.astype(bf)
    c["c_rcol"] = (np.arange(128) % 16).astype(np.float32)[:, None]
    half = 32
    inv = (1.0 / (np.float32(10000.0) ** (np.arange(half, dtype=np.float32) / np.float32(half)))).astype(np.float32)
    pos = np.concatenate([np.arange(NTOK_P), np.tile(16384 + np.arange(8), NB)]).astype(np.float32)
    ang = (pos[:, None] * inv[None, :]).astype(np.float32)
    cs = np.cos(ang).astype(np.float32)
    sn = np.sin(ang).astype(np.float32)
    c["c_cosm"] = cs
    c["c_sinm"] = sn
    c["c_cosT"] = (np.concatenate([cs, cs], axis=1).T * np.float32(SCALE)).astype(np.float32).copy()
    c["c_sinT"] = (np.concatenate([-sn, sn], axis=1).T * np.float32(SCALE)).astype(np.float32).copy()
    return c


def make_in_map(inp, c, consts):
    f = np.ascontiguousarray
    npool = inp["cache_ckv"].shape[1]
    pt = inp["page_table"][NB * c:NB * (c + 1)]
    ptq = f(pt.reshape(NB, 16, 8).transpose(2, 0, 1).reshape(8, NB * 16)).astype(np.int32)
    m = {
        "x": f(np.concatenate([inp["x_prompt"][c], inp["x_sample"][NB * c:NB * (c + 1)].reshape(128, D)], axis=0)),
        "pl": f(np.concatenate([inp["p_prompt"][0, c], inp["p_sample"][0, NB * c:NB * (c + 1)].reshape(128, 256)], axis=0)),
        "cache_ckv": inp["cache_ckv"][0].reshape(npool * 16, 2048),
        "cache_krope": inp["cache_krope"][0].reshape(npool * 16, 512),
        "ptq": ptq,
        "state_hgrn": f(inp["state_hgrn"][0, NB * c:NB * (c + 1)]),
        "state_conv": f(inp["state_conv"][0, NB * c:NB * (c + 1)].reshape(NB * 2, DFF)),
        "w_in": inp["w_in"][0], "w_q_b": inp["w_q_b"][0], "w_kv_b": inp["w_kv_b"][0], "w_out": inp["w_out"][0],
        "w_up": inp["w_up"][0], "w_down": inp["w_down"][0], "w_ple_gate": inp["w_ple_gate"][0],
        "w_ple_proj": inp["w_ple_proj"][0],
        "norm_mix": inp["norm_mix"][0], "norm_ffn": inp["norm_ffn"][0], "norm_ple": inp["norm_ple"][0],
        "norm_final": inp["norm_final"], "hg_lower": inp["hg_lower"], "hg_onorm": inp["hg_onorm"][0],
        "mla_q_norm": inp["mla_q_norm"][0], "mla_kv_norm": inp["mla_kv_norm"][0],
        "conv_w": inp["conv_w"][0], "conv_b": inp["conv_b"][0],
    }
    m.update(consts)
    return m


def assemble(results, ncores):
    y = np.stack([r["y"] for r in results])
    ckv = np.stack([r["ckv_o"] for r in results])
    kr = np.stack([r["kr_o"] for r in results])
    y_prompt = y[:, :NTOK_P]
    y_sample = y[:, NTOK_P:].reshape(ncores * NB, 8, D)
    ckv_prompt = ckv[:, :NTOK_P][None]
    ckv_sample = ckv[:, NTOK_P:].reshape(ncores * NB, 8, 256)[None]
    kr_prompt = kr[:, :NTOK_P][None]
    kr_sample = kr[:, NTOK_P:].reshape(ncores * NB, 8, 64)[None]
    hg_p = np.stack([r["hg_p"] for r in results])[None]
    cv_p = np.stack([r["conv_p"] for r in results])[None]
    hg_s = np.concatenate([r["hg_s"] for r in results], axis=0)[None]
    cv_s = np.concatenate([r["conv_s"] for r in results], axis=0)[None]
    return tuple(np.ascontiguousarray(a, dtype=np.float32) for a in
                 (y_prompt, y_sample, ckv_prompt, kr_prompt, hg_p, cv_p, ckv_sample, kr_sample, hg_s, cv_s))


def kernel(**inputs):
    inp = {k: np.asarray(v) for k, v in inputs.items()}
    npool = inp["cache_ckv"].shape[1]
    consts = _consts()
    nc = build_program(npool)
    ncores = 8
    in_maps = [make_in_map(inp, c, consts) for c in range(ncores)]
    res = run_bass_kernel_spmd(nc, in_maps, core_ids=list(range(ncores)))
    return assemble(res.results, ncores)
```

```python
import contextlib
import os
import numpy as np
import ml_dtypes
import concourse.bass as bass
import concourse.mybir as mybir
from concourse.bass_utils import run_bass_kernel_spmd

F32 = mybir.dt.float32
BF16 = mybir.dt.bfloat16
I32 = mybir.dt.int32
AF = mybir.ActivationFunctionType
ALU = mybir.AluOpType
AX = mybir.AxisListType

D = 2048
NTOK_P = 2048
NTOK = 2176
NBLK = 17
H = 8
DFF = 5632
NFC = 44
INW = 4928
EPS = 1e-6
SCALE = 192.0 ** -0.5
NEG = -1e30
NPAGES = 128
NB = 16
NGRP = 32


class Res:
    __slots__ = ("name", "w", "rs", "excl")

    ALL = []

    def __init__(self, name, excl=False):
        self.name = name
        self.w = None
        self.rs = []
        self.excl = excl
        Res.ALL.append(self)


class Op:
    __slots__ = ("eng", "fn", "deps", "is_dma", "sem", "semval", "signal", "count", "waits", "pos", "phase")

    def __init__(self, eng, fn, deps, is_dma=False):
        self.eng = eng
        self.fn = fn
        self.deps = deps
        self.is_dma = is_dma
        self.sem = None
        self.semval = 0
        self.signal = False
        self.count = 0
        self.waits = []
        self.pos = 0


LAST_PROG = None
ENGS = ("pe", "act", "dve", "pool", "sp")


class Prog:
    def __init__(self, nc, es):
        self.nc = nc
        self.es = es
        self.streams = {e: [] for e in ENGS}
        self.ops = []
        self.dkeys = {}
        self.esem = {}
        self.phase = "pro"

    def _deps(self, r, w):
        deps = []
        for x in r:
            if x.w is not None:
                deps.append(x.w)
            if x.excl:
                deps.extend(x.rs)
        for x in w:
            if x.w is not None:
                deps.append(x.w)
            deps.extend(x.rs)
        return deps

    def _commit(self, op, r, w):
        for x in r:
            x.rs.append(op)
        for x in w:
            x.w = op
            x.rs = []
        op.pos = len(self.streams[op.eng])
        op.phase = self.phase
        self.streams[op.eng].append(op)
        self.ops.append(op)

    def op(self, eng, fn, r=(), w=()):
        op = Op(eng, fn, self._deps(r, w))
        self._commit(op, r, w)
        return op

    def dma(self, eng, fn, r=(), w=(), key=None):
        op = Op(eng, fn, self._deps(r, w), is_dma=True)
        ent = self.dkeys.get(key)
        if ent is None:
            sem = self.es.enter_context(self.nc.semaphore("d_" + key))
            ent = [sem, 0, None]
            self.dkeys[key] = ent
        if ent[2] is not None:
            op.deps.append(ent[2])
        ent[1] += 16
        ent[2] = op
        op.sem = ent[0]
        op.semval = ent[1]
        self._commit(op, r, w)
        return op

    def dma2(self, engs, fns, r=(), w=(), key=None):
        deps = self._deps(r, w)
        ent = self.dkeys.get(key)
        if ent is None:
            sem = self.es.enter_context(self.nc.semaphore("d_" + key))
            ent = [sem, 0, None]
            self.dkeys[key] = ent
        if ent[2] is not None:
            deps.append(ent[2])
        last = None
        for eng, fn in zip(engs, fns):
            op = Op(eng, fn, list(deps), is_dma=True)
            ent[1] += 16
            op.sem = ent[0]
            op.semval = ent[1]
            self._commit(op, r, w)
            last = op
        ent[2] = last
        return last

    def finalize(self):
        for e in ENGS:
            self.esem[e] = self.es.enter_context(self.nc.semaphore("e_" + e))
        for op in self.ops:
            best = {}
            for d in op.deps:
                if d.is_dma:
                    continue
                if d.eng == op.eng and op.eng == "pe" and not op.is_dma:
                    continue
                if d.eng not in best or best[d.eng].pos < d.pos:
                    best[d.eng] = d
            op.deps = [d for d in op.deps if d.is_dma] + list(best.values())
            for d in best.values():
                d.signal = True
        for e in ENGS:
            c = 0
            for op in self.streams[e]:
                if op.signal:
                    c += 1
                    op.count = c
        for op in self.ops:
            ws = {}
            for d in op.deps:
                if d.is_dma:
                    k = id(d.sem)
                    if k not in ws or ws[k][1] < d.semval:
                        ws[k] = (d.sem, d.semval)
                else:
                    if d.eng == op.eng and op.eng == "pe" and not op.is_dma:
                        continue
                    sem = self.esem[d.eng]
                    k = id(sem)
                    if k not in ws or ws[k][1] < d.count:
                        ws[k] = (sem, d.count)
            op.waits = list(ws.values())

    def emit(self, engname, eng, final_waits=False):
        seen = {}
        for op in self.streams[engname]:
            for sem, val in op.waits:
                k = id(sem)
                if seen.get(k, 0) >= val:
                    continue
                eng.wait_ge(sem, val)
                seen[k] = val
            inst = op.fn(eng)
            if op.is_dma:
                inst.then_inc(op.sem, 16)
            elif op.signal:
                inst.then_inc(self.esem[engname], 1)
        if final_waits:
            for key, ent in self.dkeys.items():
                if ent[1] > 0 and seen.get(id(ent[0]), 0) < ent[1]:
                    eng.wait_ge(ent[0], ent[1])
            for e in ENGS:
                if e == engname:
                    continue
                last = 0
                for op in self.streams[e]:
                    if op.signal:
                        last = op.count
                if last > 0 and seen.get(id(self.esem[e]), 0) < last:
                    eng.wait_ge(self.esem[e], last)


def build_program(npool, nblk_run=NBLK, do_sample=True):
    nc = bass.Bass("TRN2", target_bir_lowering=False)
    es = contextlib.ExitStack()
    P = Prog(nc, es)
    Res.ALL = []

    def din(name, shape, dt=F32):
        return nc.dram_tensor(name, list(shape), dt, kind="ExternalInput").ap()

    def dout(name, shape, dt=F32):
        return nc.dram_tensor(name, list(shape), dt, kind="ExternalOutput").ap()

    def dscr(name, shape, dt=BF16):
        return nc.dram_tensor(name, list(shape), dt).ap()

    def sb(name, shape, dt):
        return es.enter_context(nc.sbuf_tensor(name, list(shape), dt))

    ARENA_BYTES = 58 * 1024
    arena = sb("arena", [128, ARENA_BYTES // 2], BF16)

    class Carver:
        def __init__(self):
            self.off = 0

        def __call__(self, name, shape, dt):
            esz = 2 if dt == BF16 else 4
            n = 1
            for d_ in shape[1:]:
                n *= d_
            nb = n * esz
            a = self.off
            self.off += (nb + 31) // 32 * 32
            assert self.off <= ARENA_BYTES, (name, self.off)
            ap = arena[0:shape[0], a // 2:(a + nb) // 2]
            if dt != BF16:
                ap = ap.bitcast(dt)
            if len(shape) == 3:
                ap = ap.rearrange("p (a b) -> p a b", a=shape[1])
            elif len(shape) == 4:
                ap = ap.rearrange("p (a b c) -> p a b c", a=shape[1], b=shape[2])
            return ap

    cvP = Carver()
    cvS = Carver()

    x_d = din("x", [NTOK, D])
    p_d = din("pl", [NTOK, 256])
    ck_d = din("cache_ckv", [npool * 16, 2048])
    kr_d = din("cache_krope", [npool * 16, 512])
    ptq_d = din("ptq", [8, NB * 16], I32)
    sth_d = din("state_hgrn", [NB, H, 128, 128])
    stc_d = din("state_conv", [NB * 2, DFF])
    w_in_d = din("w_in", [D, INW])
    w_qb_d = din("w_q_b", [512, 1536])
    w_kvb_d = din("w_kv_b", [256, 2048])
    w_out_d = din("w_out", [D, D])
    w_up_d = din("w_up", [D, 2 * DFF])
    w_down_d = din("w_down", [DFF, D])
    w_pg_d = din("w_ple_gate", [D, D])
    w_pp_d = din("w_ple_proj", [256, D])
    nmix_d = din("norm_mix", [D])
    nffn_d = din("norm_ffn", [D])
    nple_d = din("norm_ple", [D])
    nfin_d = din("norm_final", [D])
    hgl_d = din("hg_lower", [2, 1024])
    onorm_d = din("hg_onorm", [1024])
    qnorm_d = din("mla_q_norm", [512])
    kvnorm_d = din("mla_kv_norm", [256])
    convw_d = din("conv_w", [3, DFF])
    convb_d = din("conv_b", [DFF])
    ident_d = din("c_ident", [128, 128], BF16)
    identf_d = din("c_identf", [128, 128], F32)
    ones_d = din("c_ones", [128, 128], BF16)
    cmask_d = din("c_cmask", [128, 128], BF16)
    bd64_d = din("c_bd64", [128, 128], BF16)
    bd8_d = din("c_bd8", [128, 128], BF16)
    m64_d = din("c_m64", [128, 512], F32)
    m8_d = din("c_m8", [128, 512], F32)
    bmask_d = din("c_bmask", [128, NB], F32)
    smask_d = din("c_smask", [64, 248], BF16)
    rcol_d = din("c_rcol", [128, 1], F32)
    cosT_d = din("c_cosT", [64, NTOK])
    sinT_d = din("c_sinT", [64, NTOK])
    cosm_d = din("c_cosm", [NTOK, 32])
    sinm_d = din("c_sinm", [NTOK, 32])

    y_o = dout("y", [NTOK, D])
    ckv_o = dout("ckv_o", [NTOK, 256])
    kro_o = dout("kr_o", [NTOK, 64])
    hgp_o = dout("hg_p", [H, 128, 128])
    cvp_o = dout("conv_p", [2, DFF])
    hgs_o = dout("hg_s", [NB, H, 128, 128])
    cvs_o = dout("conv_s", [NB, 2, DFF])
    DBG = os.environ.get("KDBG", "0") == "1"
    if DBG:
        dbg1 = dout("dbg1", [128, 2048], BF16)
        dbg2 = dout("dbg2", [64, 264], F32)
        dbg3 = dout("dbg3", [128, NB * 16], I32)
        dbg4 = dout("dbg4", [128, 4 * 256], BF16)

    wb_in = dscr("wb_in", [10, 128, 16, 512])
    wb_up = dscr("wb_up", [22, 128, 16, 512])
    wb_dn = dscr("wb_dn", [16, 128, 11, 512])
    wb_out = dscr("wb_out", [4, 128, 16, 512])
    wb_pg = dscr("wb_pg", [4, 128, 16, 512])
    wb_pp = dscr("wb_pp", [4, 128, 2, 512])
    wb_q = dscr("wb_q", [4, 128, 4, 512])
    R_wb = {n: Res(n) for n in ("in", "up", "dn", "out", "pg", "pp", "q")}

    ps = es.enter_context(nc.psum_tensor("ps", [128, 4096], F32))
    R_bank = [Res("bank%d" % i, excl=True) for i in range(8)]

    def bank(i, n=1):
        return ps[:, i * 512:(i + n) * 512]

    def bankbf(i, n=1):
        return ps[:, i * 512:(i + n) * 512].bitcast(BF16)

    ident = sb("ident", [128, 128], BF16)
    identf = sb("identf", [128, 128], F32)
    ones = sb("ones", [128, 128], BF16)
    cmask = sb("cmask", [128, 128], BF16)
    bd64 = sb("bd64", [128, 128], BF16)
    bd8 = sb("bd8", [128, 128], BF16)
    bmask = sb("bmask", [128, NB], F32)
    smask = sb("smask", [64, 248], BF16)
    rcol = sb("rcol", [128, 1], F32)
    gT = sb("gT", [128, 3, 16], F32)
    hgl = sb("hgl", [128, 2, H], F32)
    lbT = sb("lbT", [128, H], F32)
    omlT = sb("omlT", [128, H], F32)
    nomlT = sb("nomlT", [128, H], F32)
    onormT = sb("onormT", [128, H], F32)
    qnorm_bc = sb("qnorm_bc", [128, 512], F32)
    kvnorm_bc = sb("kvnorm_bc", [128, 256], F32)
    cwT = sb("cwT", [128, 3, NFC], F32)
    cbT = sb("cbT", [128, NFC], F32)
    cvW = Carver()
    cvW.off = 52 * 1024
    wuk = cvW("wuk", [128, 2, H, 128], BF16)
    wukT = sb("wukT", [128, H, 256], BF16)
    wuv = sb("wuv", [128, 2, H, 128], BF16)
    R_const = Res("const")
    R_wq = Res("wq")
    R_wukT = Res("wukT")
    R_lb = Res("lb")

    def cload(dst, src, key="const", eng="sp", w=None):
        P.dma(eng, lambda e, dst=dst, src=src: e.dma_start(out=dst, in_=src, allow_slow_non_contiguous=True), w=[w or R_const], key=key)

    with nc.allow_non_contiguous_dma(reason="small constant layouts"):
        cload(ident[:], ident_d)
        cload(identf[:], identf_d)
        cload(ones[:], ones_d)
        cload(cmask[:], cmask_d)
        cload(bd64[:], bd64_d)
        cload(bd8[:], bd8_d)
        cload(bmask[:], bmask_d)
        cload(smask[:], smask_d)
        cload(rcol[:], rcol_d)
        cload(gT[:, 0, :], nmix_d.rearrange("(k p) -> p k", p=128))
        cload(gT[:, 1, :], nffn_d.rearrange("(k p) -> p k", p=128))
        cload(gT[:, 2, :], nple_d.rearrange("(k p) -> p k", p=128))
        cload(hgl[:, 0, :], hgl_d[0].rearrange("(h p) -> p h", p=128))
        cload(hgl[:, 1, :], hgl_d[1].rearrange("(h p) -> p h", p=128))
        cload(onormT[:], onorm_d.rearrange("(h p) -> p h", p=128))
        cload(qnorm_bc[:], qnorm_d.partition_broadcast(128))
        cload(kvnorm_bc[:], kvnorm_d.partition_broadcast(128))
        cload(cwT[:], convw_d.rearrange("j (c p) -> p j c", p=128))
        cload(cbT[:], convb_d.rearrange("(c p) -> p c", p=128))
        for kc in range(4):
            wqv = w_qb_d[kc * 128:(kc + 1) * 128, :].rearrange("p (h c) -> p h c", c=192)
            for pn in range(4):
                dstv = wb_q[pn, :, kc, :].rearrange("p (hh c) -> p hh c", c=256)
                srcv = wqv[:, 2 * pn:2 * pn + 2, :]
                for (d0, d1, s0, s1) in ((0, 192, 0, 192), (192, 224, 160, 192), (224, 256, 128, 160)):
                    P.dma("pool", lambda e, d=dstv[:, :, d0:d1], s_=srcv[:, :, s0:s1]: e.dma_start(out=d, in_=s_, allow_slow_non_contiguous=True),
                          w=[R_wb["q"]], key="cast_q")
        for kc in range(2):
            wkv = w_kvb_d[kc * 128:(kc + 1) * 128, :].rearrange("p (h c) -> p h c", c=256)
            cload(wuk[:, kc, :, :], wkv[:, :, 0:128], key="wq", eng="pool", w=R_wq)
            cload(wuv[:, kc, :, :], wkv[:, :, 128:256], key="wq", eng="pool", w=R_wq)

    dlb = sb("dlb", [128, H], F32)
    P.op("dve", lambda e: e.tensor_tensor(out=dlb[:], in0=hgl[:, 0, :], in1=hgl[:, 1, :], op=ALU.subtract),
         r=[R_const], w=[R_lb])
    P.op("act", lambda e: e.activation(out=lbT[:], in_=dlb[:], func=AF.Sigmoid), r=[R_lb], w=[R_lb])
    P.op("dve", lambda e: e.tensor_scalar(out=omlT[:], in0=lbT[:], scalar1=-1.0, scalar2=1.0,
                                          op0=ALU.mult, op1=ALU.add), r=[R_lb], w=[R_lb])
    P.op("dve", lambda e: e.tensor_scalar(out=nomlT[:], in0=lbT[:], scalar1=1.0, scalar2=-1.0,
                                          op0=ALU.mult, op1=ALU.add), r=[R_lb], w=[R_lb])

    T2 = bankbf(0, 2)
    R_T = Res("Tbanks", excl=True)
    for h in range(H):
        for kc in range(2):
            P.op("pe", lambda e, h=h, kc=kc: e.transpose(out=T2[:, (h * 2 + kc) * 128:(h * 2 + kc + 1) * 128],
                                                         in_=wuk[:, kc, h, :], identity=ident[:]),
                 r=[R_wq, R_const], w=[R_T])
    P.op("dve", lambda e: e.tensor_copy(out=wukT[:].rearrange("p h c -> p (h c)"), in_=T2[:, 0:2048]),
         r=[R_T], w=[R_wukT])

    def cast_panels(dst, src, ncols_total, kc_n, name, col0=0, kbase=0, pidx0=0, npan=None):
        npan = npan if npan is not None else (ncols_total + 511) // 512
        for j in range(npan):
            c0 = col0 + j * 512
            cw = min(512, col0 + ncols_total - c0)
            srcv = src[kbase:kbase + kc_n * 128, c0:c0 + cw].rearrange("(k p) c -> p k c", p=128)
            P.dma("pool", lambda e, d=dst[pidx0 + j, :, :, 0:cw], s=srcv: e.dma_start(out=d, in_=s),
                  w=[R_wb[name]], key="cast_" + name)

    cast_panels(wb_in, w_in_d, INW, 16, "in")
    cast_panels(wb_out, w_out_d, D, 16, "out")
    cast_panels(wb_up, w_up_d, 2 * DFF, 16, "up")
    for cb in range(4):
        for kg in range(4):
            srcv = w_down_d[kg * 1408:(kg + 1) * 1408, cb * 512:(cb + 1) * 512].rearrange("(k p) c -> p k c", p=128)
            P.dma("pool", lambda e, d=wb_dn[cb * 4 + kg], s=srcv: e.dma_start(out=d, in_=s),
                  w=[R_wb["dn"]], key="cast_dn")
    cast_panels(wb_pg, w_pg_d, D, 16, "pg")
    cast_panels(wb_pp, w_pp_d, D, 2, "pp")

    class Rot:
        def __init__(self, name, shape, dt, n, alloc=None):
            alloc = alloc or sb
            self.t = [alloc("%s%d" % (name, i), shape, dt) for i in range(n)]
            self.r = [Res("%s%d" % (name, i)) for i in range(n)]
            self.i = 0
            self.n = n
            self.name = name

        def next(self):
            k = self.i % self.n
            self.i += 1
            return self.t[k], self.r[k], "%s%d" % (self.name, k)

    panels = Rot("pan", [128, 16, 512], BF16, 3)

    panq = [0, False]

    def load_panel(src_panel, kc_n, rname, cw=512):
        t, r, key = panels.next()
        panq[0] += 1
        if kc_n >= 4:
            h1 = kc_n // 2
            e2 = "sp" if panq[1] else "pool"
            P.dma2(("sp", e2),
                   (lambda e, t=t, s=src_panel, cw=cw, h1=h1: e.dma_start(out=t[:, 0:h1, 0:cw], in_=s[:, 0:h1, 0:cw]),
                    lambda e, t=t, s=src_panel, cw=cw, h1=h1, kc_n=kc_n: e.dma_start(out=t[:, h1:kc_n, 0:cw], in_=s[:, h1:kc_n, 0:cw])),
                   r=[R_wb[rname]], w=[r], key=key)
        else:
            P.dma("sp", lambda e, t=t, s=src_panel, kc_n=kc_n, cw=cw: e.dma_start(out=t[:, 0:kc_n, 0:cw], in_=s[:, 0:kc_n, 0:cw]),
                  r=[R_wb[rname]], w=[r], key=key)
        return t, r

    xt = sb("xt", [128, D], F32)
    R_x = Res("x")
    xsq = sb("xsq", [128, D], BF16)
    R_xsq = Res("xsq")
    st1 = sb("st1", [128, 8], F32)
    R_st = Res("st1")
    xnT = sb("xnT", [128, 16, 128], BF16)
    R_xnT = Res("xnT")
    mixT = sb("mixT", [128, 16, 128], BF16)
    R_mix = Res("mixT")
    hT = sb("hT", [128, NFC, 128], BF16)
    R_hT = Res("hT")

    m64 = cvP("m64", [128, 512], F32)
    m8 = cvS("m8", [128, 512], F32)
    R_m = Res("scanmask")
    P.dma("sp", lambda e: e.dma_start(out=m64[:], in_=m64_d), w=[R_m], key="m64")
    KaugT = cvP("KaugT", [128, 3, NTOK_P], BF16)
    ckv_tm = cvP("ckv_tm", [128, 16, 256], BF16)
    R_K = Res("Kcache")
    Sst = cvP("Sst", [128, H, 128], F32)
    Sbf = cvP("Sbf", [128, H, 128], BF16)
    R_S = [Res("S%d" % h) for h in range(H)]
    convc = cvP("convc", [128, NFC, 2], F32)
    R_convc = Res("convc")
    P.op("dve", lambda e: e.memset(Sst[:], 0.0), w=R_S)
    P.op("dve", lambda e: e.memset(Sbf[:], 0.0), w=R_S)
    P.op("dve", lambda e: e.memset(convc[:], 0.0), w=[R_convc])
    P.op("dve", lambda e: e.memset(KaugT[:], 0.0), w=[R_K])

    def rstd_from_ss(ss_ap, out_ap, n, rs, ws):
        P.op("act", lambda e: e.activation(out=out_ap, in_=ss_ap, func=AF.Ln, scale=1.0 / n, bias=epsb[:, 0:1]),
             r=rs, w=ws)
        P.op("act", lambda e: e.activation(out=out_ap, in_=out_ap, func=AF.Exp, scale=-0.5), r=ws, w=ws)

    epsb = sb("epsb", [128, 1], F32)
    P.op("dve", lambda e: e.memset(epsb[:], EPS), w=[R_const])

    def norm_to_xnT(gi):
        P.op("act", lambda e: e.activation(out=xsq[:], in_=xt[:], func=AF.Square, accum_out=st1[:, 0:1]),
             r=[R_x], w=[R_xsq, R_st])
        rstd_from_ss(st1[:, 0:1], st1[:, 1:2], D, [R_st, R_const], [R_st])
        P.op("act", lambda e: e.activation(out=xsq[:], in_=xt[:], func=AF.Copy, scale=st1[:, 1:2]),
             r=[R_x, R_st], w=[R_xsq])
        for kc in range(16):
            P.op("pe", lambda e, kc=kc: e.transpose(out=T2[:, kc * 128:(kc + 1) * 128], in_=xsq[:, kc * 128:(kc + 1) * 128],
                                                    identity=ident[:]), r=[R_xsq, R_const], w=[R_T])
        for kc in range(16):
            eng = "dve" if kc % 2 == 0 else "act"
            if eng == "dve":
                P.op("dve", lambda e, kc=kc: e.tensor_scalar(out=xnT[:, kc, :], in0=T2[:, kc * 128:(kc + 1) * 128],
                                                             scalar1=gT[:, gi, kc:kc + 1], scalar2=None, op0=ALU.mult),
                     r=[R_T, R_const], w=[R_xnT])
            else:
                P.op("act", lambda e, kc=kc: e.activation(out=xnT[:, kc, :], in_=T2[:, kc * 128:(kc + 1) * 128],
                                                          func=AF.Copy, scale=gT[:, gi, kc:kc + 1]),
                     r=[R_T, R_const], w=[R_xnT])

    def mm_fm(out_ap, pan, c0, ncol, act, kcn, rs, ws, rows=128):
        for kc in range(kcn):
            P.op("pe", lambda e, kc=kc: e.matmul(out_ap, lhsT=pan[:, kc, c0:c0 + ncol], rhs=act[:, kc, :],
                                                 start=(kc == 0), stop=(kc == kcn - 1)), r=rs, w=ws)

    def mm_tm(out_ap, act, pan, c0, ncol, kcn, rs, ws, kc0=0, first=True, last=True, pk0=0):
        for kc in range(kcn):
            P.op("pe", lambda e, kc=kc: e.matmul(out_ap, lhsT=act[:, kc0 + kc, :], rhs=pan[:, pk0 + kc, c0:c0 + ncol],
                                                 start=(first and kc == 0), stop=(last and kc == kcn - 1)), r=rs, w=ws)

    vtm = sb("vtm", [128, 1024], BF16)
    R_v = Res("vtm")
    qdT = sb("qdT", [128, H, 128], BF16)
    keT = sb("keT", [128, H, 128], BF16)
    gateT = sb("gateT", [128, H, 128], BF16)
    Edec = sb("Edec", [128, H, 128], F32)
    R_hg = [Res("hg%d" % i) for i in range(2)]
    tA = [sb("tA%d" % i, [128, 512], F32) for i in range(5)]
    R_tA = [Res("tA%d" % i) for i in range(5)]
    tA.append(tA[1])
    R_tA.append(R_tA[1])
    ketm = sb("ketm", [128, H, 128], BF16)
    R_ketm = Res("ketm")
    ATm = sb("ATm", [128, 128], BF16)
    R_ATm = Res("ATm")
    osq = sb("osq", [128, 128], BF16)
    R_osq = Res("osq")
    orst = sb("orst", [128, 128], F32)
    R_orst = Res("orst")
    otmp = sb("otmp", [128, 128], F32)
    R_otmp = Res("otmp")
    cqn = sb("cqn", [128, 512], BF16)
    R_cqn = Res("cqn")
    cqnT = sb("cqnT", [128, 4, 128], BF16)
    R_cqnT = Res("cqnT")
    ckvf = Rot("ckvf", [128, 256], F32, 2)
    krf = Rot("krf", [128, 64], F32, 2)
    krtmp = sb("krtmp", [128, 6, 32], F32)
    R_krtmp = Res("krtmp")
    krbf = sb("krbf", [128, 64], BF16)
    ckvbf_s = cvS("ckvbf_s", [128, 256], BF16)
    KaugT_s = cvS("KaugT_s", [128, 3, 128], BF16)
    R_Ks = Res("Ks")
    cosm = Rot("cosm", [128, 32], F32, 2)
    sinm = Rot("sinm", [128, 32], F32, 2)
    cosT = Rot("cosT", [64, 128], F32, 2)
    sinT = Rot("sinT", [64, 128], F32, 2)
    qnT = sb("qnT", [128, 128], BF16)
    R_qnT = Res("qnT")
    QaugT = cvP("QaugT", [128, 3, 128], BF16)
    R_Q = Res("QaugT")
    QaugT_b = cvP("QaugT_b", [128, 3, 128], BF16)
    R_Qb = Res("QaugT_b")
    st2 = sb("st2", [128, 2], F32)
    R_st2 = Res("st2")
    QaugT_all = cvS("QaugT_all", [128, 3, NB, 64], BF16)
    QaugT_s = cvS("QaugT_s", [128, 3, 128], BF16)
    R_Qs = Res("QaugT_s")
    R_Qall = Res("Qall")
    qtmp = sb("qtmp", [64, 2, 128], F32)
    R_qtmp = Res("qtmp")
    Pf = cvP("Pf", [128, 2048], F32)
    R_Pf = Res("Pf")
    Pn = cvP("Pn", [128, 2048], BF16)
    R_Pn = Res("Pn")
    PT = cvP("PT", [128, 16, 128], BF16)
    R_PT = Res("PT")
    latT = cvP("latT", [128, 2, 128], BF16)
    R_latT = Res("latT")
    ptile = sb("ptile", [128, 256], F32)
    R_ptile = Res("ptile")
    pbf = sb("pbf", [128, 256], BF16)
    R_pbf = Res("pbf")
    pT = sb("pT", [128, 2, 128], BF16)
    R_pT = Res("pT")
    aext = Rot("aext", [128, 160], F32, 1)
    cva = Rot("cva", [128, 128], F32, 1)
    cvb = Rot("cvb", [128, 128], F32, 1)
    atmr = Rot("atmr", [128, 512], F32, 1)
    sconvT = cvS("sconvT", [128, NFC, 32], F32)
    R_sconv = Res("sconv")
    R_sctm = Res("sctm")
    sgt = Rot("sgt", [128, 512], F32, 1)

    ptb = cvS("ptb", [128, NB * 16], I32)
    idx = cvS("idx", [128, NB * 16], I32)
    R_idx = Res("idx")
    accs = cvS("accs", [64, 256], F32)
    fst = cvS("fst", [64, 8], F32)
    R_acc = Res("acc")
    R_fst = Res("fst")
    lat_s = cvS("lat_s", [64, 256], BF16)
    R_lats = Res("lat_s")
    latT_s = cvS("latT_s", [128, 2, H, 128], BF16)
    R_latTs = Res("latT_s")
    cvS2 = Carver()
    cvS2.off = cvS.off
    cvS3 = Carver()
    cvS3.off = cvS.off
    Sload = Rot("Sload", [128, H, 128], F32, 2, cvS)
    Sloadbf = Rot("Sloadbf", [128, H, 128], BF16, 2, cvS)
    Gc = Rot("Gc", [128, 8, 256], BF16, 3, cvS2)
    Gr = Rot("Gr", [128, 8, 64], BF16, 3, cvS2)
    KT = Rot("KT", [128, 3, 512], BF16, 3, cvS2)
    Pbs = Rot("Pbs", [64, 512], BF16, 2, cvS2)
    PTs = Rot("PTs", [128, 4, 64], BF16, 2, cvS2)
    sctm = cvS3("sctm", [32, 2048], F32)

    def block(T):
        sample = (T == 16)
        tok0 = T * 128
        P.dma("sp", lambda e: e.dma_start(out=xt[:], in_=x_d[tok0:tok0 + 128, :]), w=[R_x], key="x")
        cm, r_cm, k_cm = cosm.next()
        sm, r_sm, k_sm = sinm.next()
        cT, r_cT, k_cT = cosT.next()
        sT, r_sT, k_sT = sinT.next()
        P.dma("sp", lambda e: e.dma_start(out=cm[:], in_=cosm_d[tok0:tok0 + 128, :]), w=[r_cm], key=k_cm)
        P.dma("sp", lambda e: e.dma_start(out=sm[:], in_=sinm_d[tok0:tok0 + 128, :]), w=[r_sm], key=k_sm)
        P.dma("sp", lambda e: e.dma_start(out=cT[:], in_=cosT_d[:, tok0:tok0 + 128]), w=[r_cT], key=k_cT)
        P.dma("sp", lambda e: e.dma_start(out=sT[:], in_=sinT_d[:, tok0:tok0 + 128]), w=[r_sT], key=k_sT)

        STG = int(os.environ.get("KSTAGE", "99"))
        P.phase = "%d:norm1" % T
        norm_to_xnT(0)
        if STG <= 1:
            return
        P.phase = "%d:w_in" % T
        for half in range(2):
            for j, (pidx, bk) in enumerate(((half, 2), (2 + half, 3), (6 + half, 4))):
                pan, rp = load_panel(wb_in[pidx], 16, "in")
                for hh in range(4):
                    mm_fm(bank(bk)[:, hh * 128:(hh + 1) * 128], pan, hh * 128, 128, xnT, 16,
                          [rp, R_xnT], [R_bank[bk]])
            hg_elementwise(half, sample)
        if STG <= 2:
            return
        for j in range(2):
            pan, rp = load_panel(wb_in[4 + j], 16, "in")
            mm_tm(bank(5), xnT, pan, 0, 512, 16, [rp, R_xnT], [R_bank[5]])
            P.op("act", lambda e, j=j: e.activation(out=vtm[:, j * 512:(j + 1) * 512], in_=bank(5), func=AF.Copy),
                 r=[R_bank[5]], w=[R_v])
        pan, rp = load_panel(wb_in[8], 16, "in")
        mm_tm(bank(6), xnT, pan, 0, 512, 16, [rp, R_xnT], [R_bank[6]])
        pan, rp = load_panel(wb_in[9], 16, "in", cw=320)
        mm_tm(bank(7)[:, 0:320], xnT, pan, 0, 320, 16, [rp, R_xnT], [R_bank[7]])
        P.phase = "%d:latents" % T
        mla_latents(T, sample, cm, r_cm, sm, r_sm)
        if STG <= 3:
            return
        P.phase = "%d:hgrec" % T
        hg_recurrence(T, sample)
        P.phase = "%d:attn" % T
        if STG <= 4:
            return
        if sample:
            barrier()
            for h in range(H):
                if h % 2 == 0:
                    wqp, r_wqp = load_panel(wb_q[h // 2], 4, "q")
                mla_q(h, cT, r_cT, sT, r_sT, QaugT_s[:], R_Qs, wqp, r_wqp)
                for c in range(3):
                    rows = 128 if c < 2 else 64
                    P.op("dve" if c != 1 else "act",
                         (lambda e, c=c, rows=rows, h=h: e.tensor_copy(
                             out=QaugT_all[0:rows, c, :, h * 8:(h + 1) * 8],
                             in_=QaugT_s[0:rows, c, :].rearrange("p (b t) -> p b t", t=8))) if c != 1 else
                         (lambda e, c=c, rows=rows, h=h: e.activation(
                             out=QaugT_all[0:rows, c, :, h * 8:(h + 1) * 8],
                             in_=QaugT_s[0:rows, c, :].rearrange("p (b t) -> p b t", t=8), func=AF.Copy)),
                         r=[R_Qs], w=[R_Qall])
            sample_attention()
        else:
            Qs2 = [(QaugT, R_Q), (QaugT_b, R_Qb)]
            wqp, r_wqp = load_panel(wb_q[0], 4, "q")
            mla_q(0, cT, r_cT, sT, r_sT, QaugT[:], R_Q, wqp, r_wqp)
            pa_S(T, 0, Qs2[0])
            for h in range(H):
                pa_softmax(T, h)
                if h + 1 < H:
                    if (h + 1) % 2 == 0:
                        wqp, r_wqp = load_panel(wb_q[(h + 1) // 2], 4, "q")
                    Qn, r_Qn = Qs2[(h + 1) % 2]
                    mla_q1(h + 1, cT, r_cT, sT, r_sT, Qn, r_Qn, wqp, r_wqp)
                pa_PT(T, h)
                if h + 1 < H:
                    mla_q2(h + 1, Qn, r_Qn)
                pa_rest(T, h)
                if h + 1 < H:
                    pa_S(T, h + 1, Qs2[(h + 1) % 2])
        if DBG and sample:
            P.dma("sp", lambda e: e.dma_start(out=dbg1, in_=mixT[:].rearrange("p k t -> p (k t)")), r=[R_mix], key="dbg1")
            P.dma("sp", lambda e: e.dma_start(out=dbg2[:, 0:256], in_=accs[:]), r=[R_acc], key="dbg2")
            P.dma("sp", lambda e: e.dma_start(out=dbg2[:, 256:264], in_=fst[:]), r=[R_fst], key="dbg2")
            P.dma("sp", lambda e: e.dma_start(out=dbg3, in_=idx[:]), r=[R_idx], key="dbg3")
            pass
        if STG <= 5:
            return
        P.phase = "%d:w_out" % T
        for cb in range(4):
            pan, rp = load_panel(wb_out[cb], 16, "out")
            bk = 5 + (cb % 2)
            mm_tm(bank(bk), mixT, pan, 0, 512, 16, [rp, R_mix], [R_bank[bk]])
            P.op("dve", lambda e, cb=cb, bk=bk: e.tensor_tensor(out=xt[:, cb * 512:(cb + 1) * 512],
                                                                in0=xt[:, cb * 512:(cb + 1) * 512], in1=bank(bk), op=ALU.add),
                 r=[R_bank[bk], R_x], w=[R_x])

        if STG <= 6:
            return
        P.phase = "%d:ffn_up" % T
        norm_to_xnT(1)
        need_atm = sample or T == 15
        if sample:
            barrier()
            for g3 in range(3):
                n = 16 if g3 < 2 else 12
                P.dma("sp", lambda e, g3=g3, n=n: e.dma_start(out=sctm[:, 0:n * 128], in_=stc_d[:, g3 * 2048:g3 * 2048 + n * 128]),
                      w=[R_sctm], key="sctm")
                for i in range(n):
                    P.op("pe", lambda e, i=i, g3=g3: e.transpose(out=bank(2 + g3)[:, i * 32:(i + 1) * 32],
                                                                 in_=sctm[:, i * 128:(i + 1) * 128], identity=identf[0:32, 0:32]),
                         r=[R_sctm, R_const], w=[R_bank[2 + g3]])
                P.op("dve", lambda e, g3=g3, n=n: e.tensor_copy(out=sconvT[:, g3 * 16:g3 * 16 + n, :].rearrange("p c j -> p (c j)"),
                                                                in_=bank(2 + g3)[:, 0:n * 32]),
                     r=[R_bank[2 + g3]], w=[R_sconv])
        for pa in range(11):
            pana, rpa = load_panel(wb_up[pa], 16, "up")
            pang, rpg = load_panel(wb_up[11 + pa], 16, "up")
            if need_atm:
                mm_tm(bank(7), xnT, pana, 0, 512, 16, [rpa, R_xnT], [R_bank[7]])
                at_, r_at, k_at = atmr.next()
                P.op("act", lambda e, at_=at_: e.activation(out=at_[:], in_=bank(7), func=AF.Copy),
                     r=[R_bank[7]], w=[r_at])
                if sample:
                    for t in range(2):
                        P.dma("sp", lambda e, t=t, at_=at_, pa=pa: e.dma_start(out=cvs_o[:, t, pa * 512:(pa + 1) * 512],
                                                                               in_=at_[6 + t:128:8, :]), r=[r_at], key=k_at)
                else:
                    P.dma("sp", lambda e, at_=at_, pa=pa: e.dma_start(out=cvp_o[:, pa * 512:(pa + 1) * 512], in_=at_[126:128, :]),
                          r=[r_at], key=k_at)
            bA = 2 + (pa % 2)
            bG = 4 + (pa % 2)
            for j in range(4):
                mm_fm(bank(bA)[:, j * 128:(j + 1) * 128], pana, j * 128, 128, xnT, 16, [rpa, R_xnT], [R_bank[bA]])
            for j in range(4):
                mm_fm(bank(bG)[:, j * 128:(j + 1) * 128], pang, j * 128, 128, xnT, 16, [rpg, R_xnT], [R_bank[bG]])
            if sample:
                for j in range(4):
                    ffn_chunk(pa * 4 + j, bank(bA)[:, j * 128:(j + 1) * 128], R_bank[bA],
                              bank(bG)[:, j * 128:(j + 1) * 128], R_bank[bG], True)
            else:
                ffn_group(pa, bA, bG)
        P.phase = "%d:ffn_dn" % T
        for cb in range(4):
            bk = 5 + (cb % 2)
            for kg in range(4):
                pan, rp = load_panel(wb_dn[cb * 4 + kg], 11, "dn")
                mm_tm(bank(bk), hT, pan, 0, 512, 11, [rp, R_hT], [R_bank[bk]], kc0=kg * 11,
                      first=(kg == 0), last=(kg == 3))
            P.op("dve", lambda e, cb=cb, bk=bk: e.tensor_tensor(out=xt[:, cb * 512:(cb + 1) * 512],
                                                                in0=xt[:, cb * 512:(cb + 1) * 512], in1=bank(bk), op=ALU.add),
                 r=[R_bank[bk], R_x], w=[R_x])

        if STG <= 7:
            return
        P.phase = "%d:ple" % T
        norm_to_xnT(2)
        P.dma("sp", lambda e: e.dma_start(out=ptile[:], in_=p_d[tok0:tok0 + 128, :]), w=[R_ptile], key="ptile")
        P.op("dve", lambda e: e.tensor_copy(out=pbf[:], in_=ptile[:]), r=[R_ptile], w=[R_pbf])
        for kc in range(2):
            P.op("pe", lambda e, kc=kc: e.transpose(out=T2[:, kc * 128:(kc + 1) * 128], in_=pbf[:, kc * 128:(kc + 1) * 128],
                                                    identity=ident[:]), r=[R_pbf, R_const], w=[R_T])
        P.op("dve", lambda e: e.tensor_copy(out=pT[:].rearrange("p k t -> p (k t)"), in_=T2[:, 0:256]), r=[R_T], w=[R_pT])
        for cb in range(4):
            pan, rp = load_panel(wb_pg[cb], 16, "pg")
            pan2, rp2 = load_panel(wb_pp[cb], 2, "pp")
            bk = 5 + (cb % 2)
            mm_tm(bank(bk), xnT, pan, 0, 512, 16, [rp, R_xnT], [R_bank[bk]])
            mm_tm(bank(7), pT, pan2, 0, 512, 2, [rp2, R_pT], [R_bank[7]])
            sg, r_sg, _ = sgt.next()
            P.op("act", lambda e, sg=sg, bk=bk: e.activation(out=sg[:], in_=bank(bk), func=AF.Sigmoid),
                 r=[R_bank[bk]], w=[r_sg])
            P.op("dve", lambda e, sg=sg: e.tensor_tensor(out=sg[:], in0=sg[:], in1=bank(7), op=ALU.mult),
                 r=[R_bank[7], r_sg], w=[r_sg])
            P.op("dve", lambda e, sg=sg, cb=cb: e.tensor_tensor(out=xt[:, cb * 512:(cb + 1) * 512],
                                                                in0=xt[:, cb * 512:(cb + 1) * 512], in1=sg[:], op=ALU.add),
                 r=[r_sg, R_x], w=[R_x])

        if STG <= 8:
            return
        P.op("act", lambda e: e.activation(out=xsq[:], in_=xt[:], func=AF.Square, accum_out=st1[:, 0:1]),
             r=[R_x], w=[R_xsq, R_st])
        rstd_from_ss(st1[:, 0:1], st1[:, 1:2], D, [R_st, R_const], [R_st])
        gt_, r_gt, k_gt = panels.next()
        gfin = gt_[:, 0:8, :].rearrange("p k c -> p (k c)").bitcast(F32)
        P.dma("sp", lambda e, gfin=gfin: e.dma_start(out=gfin, in_=nfin_d.partition_broadcast(128)), w=[r_gt], key=k_gt)
        P.op("dve", lambda e, gfin=gfin: e.scalar_tensor_tensor(out=xt[:], in0=xt[:], scalar=st1[:, 1:2], in1=gfin,
                                                                op0=ALU.mult, op1=ALU.mult),
             r=[R_x, R_st, r_gt], w=[R_x])
        P.dma("sp", lambda e: e.dma_start(out=y_o[tok0:tok0 + 128, :], in_=xt[:]), r=[R_x], key="yout")

    def hg_elementwise(half, sample):
        h0 = half * 4
        R = R_hg[half]
        qs, sg, ff, kk, cum, E = tA
        rq, rsg, rff, rkk, rcum, rE = R_tA
        mask = m8 if sample else m64
        hsl = slice(h0, h0 + 4)

        def v3(t):
            return t[:].rearrange("p (h t) -> p h t", h=4)

        P.op("act", lambda e: e.activation(out=qs[:], in_=bank(2), func=AF.Silu), r=[R_bank[2]], w=[rq])
        P.op("act", lambda e: e.activation(out=sg[:], in_=bank(3), func=AF.Sigmoid), r=[R_bank[3]], w=[rsg])
        P.op("act", lambda e: e.activation(out=gateT[:, hsl, :].rearrange("p h t -> p (h t)"), in_=bank(4), func=AF.Sigmoid),
             r=[R_bank[4]], w=[R])
        for hh in range(4):
            h = h0 + hh
            P.op("dve", lambda e, hh=hh, h=h: e.tensor_scalar(out=ff[:, hh * 128:(hh + 1) * 128], in0=sg[:, hh * 128:(hh + 1) * 128],
                                                              scalar1=omlT[:, h:h + 1], scalar2=lbT[:, h:h + 1],
                                                              op0=ALU.mult, op1=ALU.add), r=[rsg, R_lb], w=[rff])
            P.op("dve", lambda e, hh=hh, h=h: e.tensor_scalar(out=kk[:, hh * 128:(hh + 1) * 128], in0=sg[:, hh * 128:(hh + 1) * 128],
                                                              scalar1=nomlT[:, h:h + 1], scalar2=omlT[:, h:h + 1],
                                                              op0=ALU.mult, op1=ALU.add), r=[rsg, R_lb], w=[rkk])
        P.op("act", lambda e: e.activation(out=ff[:], in_=ff[:], func=AF.Ln), r=[rff], w=[rff])
        P.op("dve", lambda e: e.tensor_tensor_scan(out=cum[:], data0=mask[:], data1=ff[:], initial=0.0,
                                                   op0=ALU.mult, op1=ALU.add), r=[rff, R_m], w=[rcum])
        P.op("act", lambda e: e.activation(out=Edec[:, hsl, :].rearrange("p h t -> p (h t)"), in_=cum[:], func=AF.Exp),
             r=[rcum], w=[R])
        P.op("act", lambda e: e.activation(out=E[:], in_=cum[:], func=AF.Exp, scale=-1.0), r=[rcum], w=[rE])
        P.op("dve", lambda e: e.tensor_tensor(out=qdT[:, hsl, :].rearrange("p h t -> p (h t)"), in0=qs[:],
                                              in1=Edec[:, hsl, :].rearrange("p h t -> p (h t)"), op=ALU.mult),
             r=[rq, R], w=[R])
        P.op("dve", lambda e: e.tensor_tensor(out=kk[:], in0=kk[:], in1=E[:], op=ALU.mult), r=[rkk, rE], w=[rkk])
        P.op("act", lambda e: e.activation(out=kiT[:, hsl, :].rearrange("p h t -> p (h t)"), in_=kk[:], func=AF.Copy),
             r=[rkk], w=[R])
        seg = 8 if sample else 64
        ns = 512 // seg
        kk3 = kk[:].rearrange("p (s t) -> p s t", t=seg)
        ed3 = Edec[:, hsl, :].rearrange("p h (s t) -> p (h s) t", t=seg)
        P.op("dve", lambda e: e.tensor_tensor(out=keT[:, hsl, :].rearrange("p h (s t) -> p (h s) t", t=seg), in0=kk3,
                                              in1=ed3[:, :, seg - 1:seg].to_broadcast([128, ns, seg]), op=ALU.mult),
             r=[rkk, R], w=[R])

    def mla_latents(T, sample, cm, r_cm, sm, r_sm):
        tok0 = T * 128
        P.op("act", lambda e: e.activation(out=xsq[:, 0:512], in_=bank(6), func=AF.Square, accum_out=st1[:, 2:3]),
             r=[R_bank[6]], w=[R_xsq, R_st])
        rstd_from_ss(st1[:, 2:3], st1[:, 3:4], 512, [R_st, R_const], [R_st])
        P.op("dve", lambda e: e.scalar_tensor_tensor(out=cqn[:], in0=bank(6), scalar=st1[:, 3:4], in1=qnorm_bc[:],
                                                     op0=ALU.mult, op1=ALU.mult), r=[R_bank[6], R_st, R_const], w=[R_cqn])
        for kc in range(4):
            P.op("pe", lambda e, kc=kc: e.transpose(out=T2[:, kc * 128:(kc + 1) * 128], in_=cqn[:, kc * 128:(kc + 1) * 128],
                                                    identity=ident[:]), r=[R_cqn, R_const], w=[R_T])
        P.op("dve", lambda e: e.tensor_copy(out=cqnT[:].rearrange("p k t -> p (k t)"), in_=T2[:, 0:512]),
             r=[R_T], w=[R_cqnT])
        cf, r_cf, k_cf = ckvf.next()
        P.op("act", lambda e: e.activation(out=xsq[:, 512:768], in_=bank(7)[:, 0:256], func=AF.Square, accum_out=st1[:, 4:5]),
             r=[R_bank[7]], w=[R_xsq, R_st])
        rstd_from_ss(st1[:, 4:5], st1[:, 5:6], 256, [R_st, R_const], [R_st])
        P.op("dve", lambda e: e.scalar_tensor_tensor(out=cf[:], in0=bank(7)[:, 0:256], scalar=st1[:, 5:6], in1=kvnorm_bc[:],
                                                     op0=ALU.mult, op1=ALU.mult), r=[R_bank[7], R_st, R_const], w=[r_cf])
        P.dma("sp", lambda e: e.dma_start(out=ckv_o[tok0:tok0 + 128, :], in_=cf[:]), r=[r_cf], key=k_cf)
        cbf = ckvbf_s[:] if sample else ckv_tm[:, T, :]
        RK = R_Ks if sample else R_K
        P.op("act", lambda e: e.activation(out=cbf, in_=cf[:], func=AF.Copy), r=[r_cf], w=[RK])
        kf, r_kf, k_kf = krf.next()
        x1 = bank(7)[:, 256:288]
        x2 = bank(7)[:, 288:320]
        tt = krtmp
        P.op("dve", lambda e: e.tensor_tensor(out=tt[:, 0, :], in0=x1, in1=cm[:], op=ALU.mult), r=[R_bank[7], r_cm], w=[R_krtmp])
        P.op("dve", lambda e: e.tensor_tensor(out=tt[:, 1, :], in0=x2, in1=sm[:], op=ALU.mult), r=[R_bank[7], r_sm], w=[R_krtmp])
        P.op("dve", lambda e: e.tensor_tensor(out=tt[:, 2, :], in0=x2, in1=cm[:], op=ALU.mult), r=[R_bank[7], r_cm], w=[R_krtmp])
        P.op("dve", lambda e: e.tensor_tensor(out=tt[:, 3, :], in0=x1, in1=sm[:], op=ALU.mult), r=[R_bank[7], r_sm], w=[R_krtmp])
        P.op("dve", lambda e: e.tensor_tensor(out=kf[:, 0:32], in0=tt[:, 0, :], in1=tt[:, 1, :], op=ALU.subtract),
             r=[R_krtmp], w=[r_kf])
        P.op("dve", lambda e: e.tensor_tensor(out=kf[:, 32:64], in0=tt[:, 2, :], in1=tt[:, 3, :], op=ALU.add),
             r=[R_krtmp], w=[r_kf])
        P.dma("sp", lambda e: e.dma_start(out=kro_o[tok0:tok0 + 128, :], in_=kf[:]), r=[r_kf], key=k_kf)
        P.op("act", lambda e: e.activation(out=krbf[:], in_=kf[:], func=AF.Copy), r=[r_kf], w=[R_krtmp])
        for kc in range(2):
            P.op("pe", lambda e, kc=kc: e.transpose(out=T2[:, kc * 128:(kc + 1) * 128], in_=cbf[:, kc * 128:(kc + 1) * 128],
                                                    identity=ident[:]), r=[RK, R_const], w=[R_T])
        P.op("pe", lambda e: e.transpose(out=T2[0:64, 256:384], in_=krbf[:], identity=ident[:]), r=[R_krtmp, R_const], w=[R_T])
        Kd = KaugT_s if sample else KaugT
        c0 = 0 if sample else tok0
        P.op("dve", lambda e: e.tensor_copy(out=Kd[:, 0:2, c0:c0 + 128], in_=T2[:, 0:256].rearrange("p (k t) -> p k t", k=2)),
             r=[R_T], w=[RK])
        P.op("act", lambda e: e.activation(out=Kd[0:64, 2, c0:c0 + 128], in_=T2[0:64, 256:384], func=AF.Copy),
             r=[R_T], w=[RK])

    def mla_q1(h, cT, r_cT, sT, r_sT, Qdst, RQ, wqp, r_wqp):
        hb = (h % 2) * 256
        for kc in range(4):
            P.op("pe", lambda e, kc=kc: e.matmul(bank(6)[:, 0:128], lhsT=wqp[:, kc, hb:hb + 128], rhs=cqnT[:, kc, :],
                                                 start=(kc == 0), stop=(kc == 3)), r=[r_wqp, R_cqnT], w=[R_bank[6]])
        for kc in range(4):
            P.op("pe", lambda e, kc=kc: e.matmul(bank(6)[0:64, 128:256], lhsT=wqp[:, kc, hb + 128:hb + 192], rhs=cqnT[:, kc, :],
                                                 start=(kc == 0), stop=(kc == 3)), r=[r_wqp, R_cqnT], w=[R_bank[6]])
        for kc in range(4):
            P.op("pe", lambda e, kc=kc: e.matmul(bank(6)[0:64, 256:384], lhsT=wqp[:, kc, hb + 192:hb + 256], rhs=cqnT[:, kc, :],
                                                 start=(kc == 0), stop=(kc == 3)), r=[r_wqp, R_cqnT], w=[R_bank[6]])
        P.op("act", lambda e: e.activation(out=qnT[:], in_=bank(6)[:, 0:128], func=AF.Copy), r=[R_bank[6]], w=[R_qnT])
        P.op("dve", lambda e: e.tensor_tensor(out=qtmp[:, 0, :], in0=bank(6)[0:64, 128:256], in1=cT[:], op=ALU.mult),
             r=[R_bank[6], r_cT], w=[R_qtmp])
        P.op("dve", lambda e: e.tensor_tensor(out=qtmp[:, 1, :], in0=bank(6)[0:64, 256:384], in1=sT[:], op=ALU.mult),
             r=[R_bank[6], r_sT], w=[R_qtmp])
        P.op("dve", lambda e: e.tensor_tensor(out=Qdst[0:64, 2, :], in0=qtmp[:, 0, :], in1=qtmp[:, 1, :], op=ALU.add),
             r=[R_qtmp], w=[RQ])

    def mla_q2(h, Qdst, RQ):
        for ck in range(2):
            P.op("pe", lambda e, ck=ck: e.matmul(bank(7)[:, ck * 128:(ck + 1) * 128], lhsT=wukT[:, h, ck * 128:(ck + 1) * 128],
                                                 rhs=qnT[:], start=True, stop=True), r=[R_wukT, R_qnT], w=[R_bank[7]])
        P.op("act", lambda e: e.activation(out=Qdst[:, 0:2, :], in_=bank(7)[:, 0:256].rearrange("p (k t) -> p k t", k=2),
                                           func=AF.Copy, scale=SCALE), r=[R_bank[7]], w=[RQ])

    def mla_q(h, cT, r_cT, sT, r_sT, Qdst, RQ, wqp, r_wqp):
        mla_q1(h, cT, r_cT, sT, r_sT, Qdst, RQ, wqp, r_wqp)
        mla_q2(h, Qdst, RQ)

    def pa_S(T, h, Qs):
        nk = (T + 1) * 128
        ngrp = (nk + 511) // 512
        Sreg = ps[:, 1024:1024 + 2048]
        RS = [R_bank[2], R_bank[3], R_bank[4], R_bank[5]]
        Qt, r_Q = Qs
        for g in range(ngrp):
            k0 = g * 512
            kw = min(512, nk - k0)
            isdiag = (g == ngrp - 1)
            for c in range(3):
                rows = 128 if c < 2 else 64
                P.op("pe", lambda e, c=c, rows=rows, k0=k0, kw=kw, isdiag=isdiag: e.matmul(
                    Sreg[:, k0:k0 + kw], lhsT=Qt[0:rows, c, :], rhs=KaugT[0:rows, c, k0:k0 + kw],
                    start=(c == 0), stop=(c == 2 and not isdiag)), r=[r_Q, R_K], w=[RS[g]])
            if isdiag:
                P.op("pe", lambda e: e.matmul(Sreg[:, T * 128:(T + 1) * 128], lhsT=ident[:], rhs=cmask[:],
                                              start=False, stop=True), r=[R_const], w=[RS[g]])

    def pa_softmax(T, h):
        nk = (T + 1) * 128
        ngrp = (nk + 511) // 512
        Sreg = ps[:, 1024:1024 + 2048]
        used = [R_bank[2], R_bank[3], R_bank[4], R_bank[5]][:ngrp]
        P.op("dve", lambda e: e.tensor_reduce(out=st2[:, 0:1], in_=Sreg[:, 0:nk], axis=AX.X, op=ALU.max),
             r=used, w=[R_st2])
        P.op("dve", lambda e: e.tensor_scalar(out=st2[:, 0:1], in0=st2[:, 0:1], scalar1=-1.0, scalar2=None, op0=ALU.mult),
             r=[R_st2], w=[R_st2])
        P.op("act", lambda e: e.activation(out=Pf[:, 0:nk], in_=Sreg[:, 0:nk], func=AF.Exp, bias=st2[:, 0:1],
                                           accum_out=st2[:, 1:2]), r=used + [R_st2], w=[R_Pf, R_st2])
        P.op("dve", lambda e: e.reciprocal(out=st2[:, 1:2], in_=st2[:, 1:2]), r=[R_st2], w=[R_st2])
        P.op("act", lambda e: e.activation(out=Pn[:, 0:nk], in_=Pf[:, 0:nk], func=AF.Copy, scale=st2[:, 1:2]),
             r=[R_Pf, R_st2], w=[R_Pn])

    def pa_PT(T, h):
        nk = (T + 1) * 128
        for b in range(T + 1):
            P.op("pe", lambda e, b=b: e.transpose(out=T2[:, b * 128:(b + 1) * 128], in_=Pn[:, b * 128:(b + 1) * 128],
                                                  identity=ident[:]), r=[R_Pn, R_const], w=[R_T])
        P.op("dve", lambda e: e.tensor_copy(out=PT[:, 0:T + 1, :].rearrange("p b q -> p (b q)"), in_=T2[:, 0:nk]),
             r=[R_T], w=[R_PT])

    def pa_rest(T, h):
        for ck in range(2):
            for b in range(T + 1):
                P.op("pe", lambda e, ck=ck, b=b: e.matmul(bank(6)[:, ck * 128:(ck + 1) * 128],
                                                          lhsT=ckv_tm[:, b, ck * 128:(ck + 1) * 128], rhs=PT[:, b, :],
                                                          start=(b == 0), stop=(b == T)), r=[R_K, R_PT], w=[R_bank[6]])
        P.op("act", lambda e: e.activation(out=latT[:].rearrange("p k t -> p (k t)"), in_=bank(6)[:, 0:256], func=AF.Copy),
             r=[R_bank[6]], w=[R_latT])
        for ck in range(2):
            P.op("pe", lambda e, ck=ck: e.matmul(bank(7)[:, 0:128], lhsT=wuv[:, ck, h, :], rhs=latT[:, ck, :],
                                                 start=(ck == 0), stop=(ck == 1)), r=[R_wq, R_latT], w=[R_bank[7]])
        P.op("dve", lambda e: e.tensor_copy(out=mixT[:, 8 + h, :], in_=bank(7)[:, 0:128]), r=[R_bank[7]], w=[R_mix])

    def hg_norm_gate(h, oT, r_oT):
        Rh = R_hg[h // 4]
        P.op("act", lambda e: e.activation(out=osq[:], in_=oT, func=AF.Square), r=[r_oT], w=[R_osq])
        P.op("pe", lambda e: e.matmul(bank(2)[:, 0:128], lhsT=ones[:], rhs=osq[:], start=True, stop=True),
             r=[R_osq, R_const], w=[R_bank[2]])
        P.op("act", lambda e: e.activation(out=orst[:], in_=bank(2)[:, 0:128], func=AF.Ln, scale=1.0 / 128, bias=epsb[:, 0:1]),
             r=[R_bank[2], R_const], w=[R_orst])
        P.op("act", lambda e: e.activation(out=orst[:], in_=orst[:], func=AF.Exp, scale=-0.5), r=[R_orst], w=[R_orst])
        P.op("dve", lambda e: e.tensor_tensor(out=otmp[:], in0=oT, in1=orst[:], op=ALU.mult),
             r=[r_oT, R_orst], w=[R_otmp])
        P.op("dve", lambda e: e.scalar_tensor_tensor(out=mixT[:, h, :], in0=otmp[:], scalar=onormT[:, h:h + 1],
                                                     in1=gateT[:, h, :], op0=ALU.mult, op1=ALU.mult),
             r=[R_otmp, R_const, Rh], w=[R_mix])

    def hg_recurrence(T, sample):
        bd = bd8 if sample else bd64
        for h in range(H):
            P.op("pe", lambda e, h=h: e.transpose(out=T2[:, h * 128:(h + 1) * 128], in_=keT[:, h, :], identity=ident[:]),
                 r=[R_hg[h // 4], R_const], w=[R_T])
        P.op("dve", lambda e: e.tensor_copy(out=ketm[:].rearrange("p h k -> p (h k)"), in_=T2[:, 0:1024]),
             r=[R_T], w=[R_ketm])
        if not sample:
            for h in range(H):
                Rh = R_hg[h // 4]
                vh = vtm[:, h * 128:(h + 1) * 128]
                P.op("pe", lambda e, h=h: e.matmul(bank(5)[:, 0:128], lhsT=kiT[:, h, :], rhs=qdT[:, h, :], start=True, stop=True),
                     r=[Rh], w=[R_bank[5]])
                P.op("dve", lambda e: e.tensor_tensor(out=ATm[:], in0=bank(5)[:, 0:128], in1=bd[:], op=ALU.mult),
                     r=[R_bank[5], R_const], w=[R_ATm])
                P.op("pe", lambda e, vh=vh: e.matmul(bank(6)[:, 0:128], lhsT=vh, rhs=ATm[:], start=True, stop=False),
                     r=[R_v, R_ATm], w=[R_bank[6]])
                for c in range(2):
                    cs = slice(c * 64, (c + 1) * 64)
                    P.op("pe", lambda e, h=h, cs=cs, c=c: e.matmul(bank(6)[:, cs], lhsT=Sbf[:, h, :], rhs=qdT[:, h, cs],
                                                                   start=False, stop=(c == 1)), r=[R_S[h], Rh], w=[R_bank[6]])
                    P.op("pe", lambda e, h=h, cs=cs, vh=vh: e.matmul(bank(7)[:, 0:128], lhsT=ketm[cs, h, :], rhs=vh[cs, :],
                                                                     start=True, stop=True), r=[R_ketm, R_v], w=[R_bank[7]])
                    dcol = Edec[:, h, c * 64 + 63:c * 64 + 64]
                    P.op("dve", lambda e, h=h, dcol=dcol: e.scalar_tensor_tensor(out=Sst[:, h, :], in0=Sst[:, h, :], scalar=dcol,
                                                                                 in1=bank(7)[:, 0:128], op0=ALU.mult, op1=ALU.add),
                         r=[R_bank[7], Rh, R_S[h]], w=[R_S[h]])
                    P.op("act", lambda e, h=h: e.activation(out=Sbf[:, h, :], in_=Sst[:, h, :], func=AF.Copy),
                         r=[R_S[h]], w=[R_S[h]])
                hg_norm_gate(h, bank(6)[:, 0:128], R_bank[6])
            if T == 15:
                P.dma("sp", lambda e: e.dma_start(out=hgp_o.rearrange("h k v -> k h v"), in_=Sst[:]), r=R_S, key="hgp")
            return
        def oTap(h):
            return bank(3 + h // 4)[:, (h % 4) * 128:(h % 4 + 1) * 128]
        for h in range(H):
            Rh = R_hg[h // 4]
            vh = vtm[:, h * 128:(h + 1) * 128]
            P.op("pe", lambda e, h=h: e.matmul(bank(5)[:, 0:128], lhsT=kiT[:, h, :], rhs=qdT[:, h, :], start=True, stop=True),
                 r=[Rh], w=[R_bank[5]])
            P.op("dve", lambda e: e.tensor_tensor(out=ATm[:], in0=bank(5)[:, 0:128], in1=bd[:], op=ALU.mult),
                 r=[R_bank[5], R_const], w=[R_ATm])
            P.op("pe", lambda e, vh=vh, h=h: e.matmul(oTap(h), lhsT=vh, rhs=ATm[:], start=(h % 4 == 0), stop=False,
                                                      skip_group_check=True),
                 r=[R_v, R_ATm], w=[R_bank[3 + h // 4]])
        for b in range(NB):
            sl, r_sl, k_sl = Sload.next()
            slb, r_slb, _ = Sloadbf.next()
            sn, r_sn, k_sn = sl, r_sl, k_sl + "o"
            P.dma("sp", lambda e, sl=sl, b=b: e.dma_start(out=sl[:], in_=sth_d[b].rearrange("h k v -> k h v")), w=[r_sl], key=k_sl)
            P.op("act", lambda e, sl=sl, slb=slb: e.activation(out=slb[:].rearrange("p h v -> p (h v)"),
                                                               in_=sl[:].rearrange("p h v -> p (h v)"), func=AF.Copy),
                 r=[r_sl], w=[r_slb])
            for h in range(H):
                Rh = R_hg[h // 4]
                vh = vtm[:, h * 128:(h + 1) * 128]
                cs = slice(b * 8, (b + 1) * 8)
                P.op("pe", lambda e, h=h, cs=cs, slb=slb, b=b: e.matmul(oTap(h)[:, cs], lhsT=slb[:, h, :], rhs=qdT[:, h, cs],
                                                                        start=False, stop=(b == NB - 1), skip_group_check=True),
                     r=[r_slb, Rh], w=[R_bank[3 + h // 4]])
                km, r_km, _ = kemr.next()
                P.op("dve" if h % 2 == 0 else "pool",
                     lambda e, h=h, b=b, km=km: e.tensor_scalar(out=km[:], in0=ketm[:, h, :], scalar1=bmask[:, b:b + 1],
                                                                scalar2=None, op0=ALU.mult), r=[R_ketm, R_const], w=[r_km])
                bk = 6 + (h % 2)
                P.op("pe", lambda e, km=km, vh=vh, bk=bk: e.matmul(bank(bk)[:, 0:128], lhsT=km[:], rhs=vh, start=True, stop=True),
                     r=[r_km, R_v], w=[R_bank[bk]])
                dcol = Edec[:, h, b * 8 + 7:b * 8 + 8]
                P.op("dve", lambda e, h=h, dcol=dcol, sl=sl, sn=sn, bk=bk: e.scalar_tensor_tensor(
                    out=sn[:, h, :], in0=sl[:, h, :], scalar=dcol, in1=bank(bk)[:, 0:128], op0=ALU.mult, op1=ALU.add),
                    r=[R_bank[bk], Rh, r_sl], w=[r_sn])
            P.dma("sp", lambda e, sn=sn, b=b: e.dma_start(out=hgs_o[b].rearrange("h k v -> k h v"), in_=sn[:]), r=[r_sn], key=k_sn)
        for h in range(H):
            hg_norm_gate(h, oTap(h), R_bank[3 + h // 4])

    kiT = sb("kiT", [128, H, 128], BF16)
    kemr = Rot("kemr", [128, 128], BF16, 3)

    ae4 = sb("ae4", [128, 4, 130], F32)
    ca4 = sb("ca4", [128, 4, 128], F32)
    cb4 = sb("cb4", [128, 4, 128], F32)
    R_ae4 = Res("ae4")
    R_ca4 = Res("ca4")
    R_cb4 = Res("cb4")

    def ffn_group(pa, bA, bG):
        c0 = pa * 4
        P.op("act", lambda e: e.activation(out=ae4[:, :, 0:2], in_=convc[:, c0:c0 + 4, :], func=AF.Copy), r=[R_convc], w=[R_ae4])
        P.op("act", lambda e: e.activation(out=ae4[:, :, 2:130], in_=bank(bA).rearrange("p (c t) -> p c t", c=4), func=AF.Copy),
             r=[R_bank[bA]], w=[R_ae4])
        P.op("act", lambda e: e.activation(out=convc[:, c0:c0 + 4, :], in_=ae4[:, :, 128:130], func=AF.Copy), r=[R_ae4], w=[R_convc])
        for j in range(4):
            ci = c0 + j
            P.op("dve", lambda e, j=j, ci=ci: e.tensor_scalar(out=ca4[:, j, :], in0=ae4[:, j, 0:128], scalar1=cwT[:, 0, ci:ci + 1],
                                                              scalar2=cbT[:, ci:ci + 1], op0=ALU.mult, op1=ALU.add),
                 r=[R_ae4, R_const], w=[R_ca4])
            P.op("dve", lambda e, j=j, ci=ci: e.scalar_tensor_tensor(out=cb4[:, j, :], in0=ae4[:, j, 1:129], scalar=cwT[:, 1, ci:ci + 1],
                                                                     in1=ca4[:, j, :], op0=ALU.mult, op1=ALU.add),
                 r=[R_ae4, R_ca4, R_const], w=[R_cb4])
            P.op("dve", lambda e, j=j, ci=ci: e.scalar_tensor_tensor(out=ca4[:, j, :], in0=ae4[:, j, 2:130], scalar=cwT[:, 2, ci:ci + 1],
                                                                     in1=cb4[:, j, :], op0=ALU.mult, op1=ALU.add),
                 r=[R_ae4, R_cb4, R_const], w=[R_ca4])
        P.op("act", lambda e: e.activation(out=cb4[:].rearrange("p c t -> p (c t)"), in_=ca4[:].rearrange("p c t -> p (c t)"), func=AF.Silu),
             r=[R_ca4], w=[R_cb4])
        P.op("dve", lambda e: e.tensor_tensor(out=hT[:, c0:c0 + 4, :].rearrange("p c t -> p (c t)"), in0=cb4[:].rearrange("p c t -> p (c t)"),
                                              in1=bank(bG), op=ALU.mult), r=[R_cb4, R_bank[bG]], w=[R_hT])

    def ffn_chunk(ci, a_ap, r_a, g_ap, r_g, sample):
        ae, r_ae, _ = aext.next()
        ca, r_ca, _ = cva.next()
        cb_, r_cb, _ = cvb.next()
        if not sample:
            P.op("act", lambda e: e.activation(out=ae[:, 0:2], in_=convc[:, ci, :], func=AF.Copy), r=[R_convc], w=[r_ae])
            P.op("act", lambda e: e.activation(out=ae[:, 2:130], in_=a_ap, func=AF.Copy), r=[r_a], w=[r_ae])
            P.op("act", lambda e: e.activation(out=convc[:, ci, :], in_=ae[:, 128:130], func=AF.Copy), r=[r_ae], w=[R_convc])
            s0, s1, s2 = ae[:, 0:128], ae[:, 1:129], ae[:, 2:130]
            o1, o2 = ca[:], cb_[:]
        else:
            ae3 = ae[:].rearrange("p (b t) -> p b t", t=10)
            P.op("act", lambda e: e.activation(out=ae3[:, :, 0:2], in_=sconvT[:, ci, :].rearrange("p (b j) -> p b j", j=2),
                                               func=AF.Copy), r=[R_sconv], w=[r_ae])
            P.op("act", lambda e: e.activation(out=ae3[:, :, 2:10], in_=a_ap.rearrange("p (b t) -> p b t", t=8),
                                               func=AF.Copy), r=[r_a], w=[r_ae])
            s0, s1, s2 = ae3[:, :, 0:8], ae3[:, :, 1:9], ae3[:, :, 2:10]
            o1 = ca[:].rearrange("p (b t) -> p b t", t=8)
            o2 = cb_[:].rearrange("p (b t) -> p b t", t=8)
        P.op("dve", lambda e: e.tensor_scalar(out=o1, in0=s0, scalar1=cwT[:, 0, ci:ci + 1], scalar2=cbT[:, ci:ci + 1],
                                              op0=ALU.mult, op1=ALU.add), r=[r_ae, R_const], w=[r_ca])
        P.op("dve", lambda e: e.scalar_tensor_tensor(out=o2, in0=s1, scalar=cwT[:, 1, ci:ci + 1], in1=o1,
                                                     op0=ALU.mult, op1=ALU.add), r=[r_ae, r_ca, R_const], w=[r_cb])
        P.op("dve", lambda e: e.scalar_tensor_tensor(out=o1, in0=s2, scalar=cwT[:, 2, ci:ci + 1], in1=o2,
                                                     op0=ALU.mult, op1=ALU.add), r=[r_ae, r_cb, R_const], w=[r_ca])
        P.op("act", lambda e: e.activation(out=cb_[:], in_=ca[:], func=AF.Silu), r=[r_ca], w=[r_cb])
        P.op("dve", lambda e: e.tensor_tensor(out=hT[:, ci, :], in0=cb_[:], in1=g_ap, op=ALU.mult),
             r=[r_cb, r_g], w=[R_hT])

    R_negm = Res("negm")
    R_T67 = Res("T67", excl=True)
    R_l = Res("l_run")
    R_par = [Res("par0"), Res("par1")]

    def sample_attention():
        NGT = NGRP + 1
        TR = (bankbf(0, 2), bankbf(6, 2))
        R_TR = (R_T, R_T67)
        for b in range(NB):
            Qb = QaugT_all[:, :, b, :]
            P.op("dve", lambda e: e.memset(accs[:], 0.0), w=[R_acc])
            P.op("dve", lambda e: e.memset(fst[:, 0:2], 1e30), w=[R_negm])
            P.op("dve", lambda e: e.memset(fst[:, 2:3], 0.0), w=[R_l])
            gath = {}

            def issue_gather(G, b=b, gath=gath):
                gc, r_gc, k_gc = Gc.next()
                gr, r_gr, k_gr = Gr.next()
                col = b * 16 + G
                P.dma("pool", lambda e, gc=gc, col=col: e.indirect_dma_start(
                    out=gc[:].rearrange("p s c -> p (s c)"), out_offset=None, in_=ck_d,
                    in_offset=bass.IndirectOffsetOnAxis(ap=idx[:, col:col + 1], axis=0)), r=[R_idx], w=[r_gc], key=k_gc)
                P.dma("pool", lambda e, gr=gr, col=col: e.indirect_dma_start(
                    out=gr[:].rearrange("p s c -> p (s c)"), out_offset=None, in_=kr_d,
                    in_offset=bass.IndirectOffsetOnAxis(ap=idx[:, col:col + 1], axis=0)), r=[R_idx], w=[r_gr], key=k_gr)
                gath[G] = (gc, r_gc, gr, r_gr)

            issue_gather(0)
            info = {}
            for it in range(NGT + 2):
                g = it
                if g % 2 == 0 and g // 2 + 1 < 16:
                    issue_gather(g // 2 + 1)
                if g < NGT:
                    selfg = (g == NGRP)
                    bS = 2 + (g % 2)
                    if not selfg:
                        gc, r_gc, gr, r_gr = gath[g // 2]
                        so = 4 * (g % 2)
                        kt, r_kt, _ = KT.next()
                        Tg = TR[g % 2]
                        r_Tg = R_TR[g % 2]
                        for s_ in range(4):
                            for c in range(2):
                                P.op("pe", lambda e, s_=s_, c=c, gc=gc, Tg=Tg, so=so: e.transpose(
                                    out=Tg[:, c * 512 + s_ * 128:c * 512 + (s_ + 1) * 128], in_=gc[:, so + s_, c * 128:(c + 1) * 128],
                                    identity=ident[:]), r=[r_gc, R_const], w=[r_Tg])
                            P.op("pe", lambda e, s_=s_, gr=gr, Tg=Tg, so=so: e.transpose(out=Tg[0:64, 1024 + s_ * 128:1024 + (s_ + 1) * 128],
                                                                                         in_=gr[:, so + s_, :], identity=ident[:]), r=[r_gr, R_const], w=[r_Tg])
                        P.op("dve", lambda e, kt=kt, Tg=Tg: e.tensor_copy(out=kt[:, 0:2, :].rearrange("p c k -> p (c k)"), in_=Tg[:, 0:1024]),
                             r=[r_Tg], w=[r_kt])
                        P.op("act", lambda e, kt=kt, Tg=Tg: e.activation(out=kt[0:64, 2, :], in_=Tg[0:64, 1024:1536], func=AF.Copy),
                             r=[r_Tg], w=[r_kt])
                        nkeys, ktv, r_ktv, nsl = 512, kt, r_kt, 4
                        vfun = (lambda s_, gc=gc, so=so: gc[:, so + s_, :])
                        r_v = r_gc
                    else:
                        nkeys, ktv, r_ktv, nsl = 128, KaugT_s, R_Ks, 1
                        vfun = (lambda s_: ckvbf_s[:])
                        r_v = R_Ks
                    for c in range(3):
                        rows = 128 if c < 2 else 64
                        P.op("pe", lambda e, c=c, rows=rows, ktv=ktv, bS=bS, nkeys=nkeys, selfg=selfg, Qb=Qb: e.matmul(
                            bank(bS)[0:64, 0:nkeys], lhsT=Qb[0:rows, c], rhs=ktv[0:rows, c, 0:nkeys],
                            start=(c == 0), stop=(c == 2 and not selfg)), r=[R_Qall, r_ktv], w=[R_bank[bS]])
                    if selfg:
                        P.op("pe", lambda e, bS=bS, b=b: e.matmul(bank(bS)[0:64, 0:128], lhsT=ident[0:64, 0:64],
                                                                  rhs=smask[:, 120 - 8 * b:248 - 8 * b], start=False, stop=True),
                             r=[R_const], w=[R_bank[bS]])
                    info[g] = [bS, nkeys, vfun, r_v, nsl, None, None]
                g = it - 1
                if 0 <= g < NGT:
                    bS, nkeys, vfun, r_v, nsl, _, _ = info[g]
                    par = g % 2
                    Rp = R_par[par]
                    old = fst[:, (g + 1) % 2:(g + 1) % 2 + 1]
                    new_ = fst[:, g % 2:g % 2 + 1]
                    pb, r_pb, _ = Pbs.next()
                    P.op("dve", lambda e, pb=pb, bS=bS, nkeys=nkeys, old=old, new_=new_: e.tensor_scalar(
                        out=pb[:, 0:nkeys], in0=bank(bS)[0:64, 0:nkeys], scalar1=-1.0, scalar2=old,
                        op0=ALU.mult, op1=ALU.min, accum_out=new_), r=[R_bank[bS], R_negm], w=[r_pb, R_negm])
                    corr = fst[:, 3 + par:4 + par]
                    rsum = fst[:, 5 + par:6 + par]
                    P.op("act", lambda e, corr=corr, old=old, new_=new_: e.activation(out=corr, in_=old, func=AF.Exp, scale=-1.0, bias=new_),
                         r=[R_negm], w=[Rp])
                    P.op("act", lambda e, pb=pb, bS=bS, nkeys=nkeys, new_=new_, rsum=rsum: e.activation(
                        out=pb[:, 0:nkeys], in_=bank(bS)[0:64, 0:nkeys], func=AF.Exp, bias=new_, accum_out=rsum),
                        r=[R_bank[bS], R_negm], w=[r_pb, Rp])
                    P.op("dve", lambda e, corr=corr, rsum=rsum: e.scalar_tensor_tensor(out=fst[:, 2:3], in0=fst[:, 2:3], scalar=corr, in1=rsum,
                                                                                     op0=ALU.mult, op1=ALU.add), r=[Rp, R_l], w=[R_l])
                    pt_, r_pt, _ = PTs.next()
                    po = par * 256
                    for s_ in range(nsl):
                        P.op("pe", lambda e, s_=s_, pb=pb, po=po: e.transpose(out=bankbf(4)[:, po + s_ * 64:po + (s_ + 1) * 64], in_=pb[:, s_ * 128:(s_ + 1) * 128],
                                                                              identity=ident[0:64, 0:64]), r=[r_pb, R_const], w=[R_bank[4]])
                    P.op("dve", lambda e, pt_=pt_, nsl=nsl, po=po: e.tensor_copy(out=pt_[:, 0:nsl, :].rearrange("p s q -> p (s q)"),
                                                                                 in_=bankbf(4)[:, po:po + nsl * 64]), r=[R_bank[4]], w=[r_pt])
                    info[g][5] = pt_
                    info[g][6] = r_pt
                g = it - 2
                if 0 <= g < NGT:
                    bS, nkeys, vfun, r_v, nsl, pt_, r_pt = info.pop(g)
                    par = g % 2
                    vo = par * 256
                    corr = fst[:, 3 + par:4 + par]
                    for s_ in range(nsl):
                        P.op("pe", lambda e, s_=s_, pt_=pt_, vfun=vfun, nsl=nsl, vo=vo: e.matmul(
                            bank(5)[0:64, vo:vo + 256], lhsT=pt_[:, s_, :], rhs=vfun(s_), start=(s_ == 0), stop=(s_ == nsl - 1)),
                            r=[r_pt, r_v], w=[R_bank[5]])
                    P.op("dve", lambda e, corr=corr, vo=vo: e.scalar_tensor_tensor(out=accs[:], in0=accs[:], scalar=corr, in1=bank(5)[0:64, vo:vo + 256],
                                                                                 op0=ALU.mult, op1=ALU.add), r=[R_bank[5], R_par[par], R_acc], w=[R_acc])
            P.op("dve", lambda e: e.reciprocal(out=fst[:, 7:8], in_=fst[:, 2:3]), r=[R_l], w=[R_fst])
            P.op("act", lambda e: e.activation(out=lat_s[:], in_=accs[:], func=AF.Copy, scale=fst[:, 7:8]),
                 r=[R_acc, R_fst], w=[R_lats])
            for ck in range(2):
                P.op("pe", lambda e, ck=ck: e.transpose(out=bankbf(4)[:, 512 + ck * 64:512 + (ck + 1) * 64],
                                                        in_=lat_s[:, ck * 128:(ck + 1) * 128], identity=ident[0:64, 0:64]),
                     r=[R_lats, R_const], w=[R_bank[4]])
            for ck in range(2):
                P.op("dve", lambda e, b=b, ck=ck: e.tensor_copy(
                    out=latT_s[:, ck, :, b * 8:(b + 1) * 8],
                    in_=bankbf(4)[:, 512 + ck * 64:512 + (ck + 1) * 64].rearrange("p (h t) -> p h t", h=H)),
                    r=[R_bank[4]], w=[R_latTs])
        for h in range(H):
            for ck in range(2):
                P.op("pe", lambda e, ck=ck, h=h: e.matmul(bank(7)[:, 0:128], lhsT=wuv[:, ck, h, :], rhs=latT_s[:, ck, h, :],
                                                          start=(ck == 0), stop=(ck == 1)), r=[R_wq, R_latTs], w=[R_bank[7], R_T67])
            P.op("dve", lambda e, h=h: e.tensor_copy(out=mixT[:, 8 + h, :], in_=bank(7)[:, 0:128]), r=[R_bank[7], R_T67], w=[R_mix])

    bar = sb("bar", [128, 8], F32)

    def barrier():
        P.op("dve", lambda e: e.memset(bar[:], 0.0), w=list(Res.ALL))

    for T in range(min(nblk_run, 16)):
        block(T)
    if do_sample:
        barrier()
        panq[1] = True
        P.dma("sp", lambda e: e.dma_start(out=m8[:], in_=m8_d), w=[R_m], key="m8")
        for q in range(8):
            P.dma("sp", lambda e, q=q: e.dma_start(out=ptb[16 * q:16 * q + 16, :],
                                                   in_=ptq_d[q].partition_broadcast(16)), w=[R_idx], key="ptb")
        P.op("dve", lambda e: e.tensor_scalar(out=idx[:], in0=ptb[:], scalar1=16.0, scalar2=rcol[:, 0:1],
                                              op0=ALU.mult, op1=ALU.add), r=[R_idx, R_const], w=[R_idx])
        block(16)

    P.finalize()
    global LAST_PROG
    LAST_PROG = P
    with nc.Block() as blk:
        @blk.sync
        def _(e):
            P.emit("sp", e, final_waits=True)

        @blk.tensor
        def _(e):
            P.emit("pe", e)

        @blk.scalar
        def _(e):
            P.emit("act", e)

        @blk.vector
        def _(e):
            P.emit("dve", e)

        @blk.gpsimd
        def _(e):
            P.emit("pool", e)
    es.close()
    return nc


def _consts():
    bf = ml_dtypes.bfloat16
    c = {}
    c["c_ident"] = np.eye(128, dtype=np.float32).astype(bf)
    c["c_identf"] = np.eye(128, dtype=np.float32)
    c["c_ones"] = np.ones((128, 128), np.float32).astype(bf)
    q = np.arange(128)[:, None]
    k = np.arange(128)[None, :]
    c["c_cmask"] = np.where(k <= q, 0.0, NEG).astype(np.float32).astype(bf)
    s = np.arange(128)[:, None]
    t = np.arange(128)[None, :]
    c["c_bd64"] = ((s // 64 == t // 64) & (s <= t)).astype(np.float32).astype(bf)
    c["c_bd8"] = ((s // 8 == t // 8) & (s <= t)).astype(np.float32).astype(bf)
    j = np.arange(512)
    c["c_m64"] = np.broadcast_to((j % 64 != 0).astype(np.float32), (128, 512)).copy()
    c["c_m8"] = np.broadcast_to((j % 8 != 0).astype(np.float32), (128, 512)).copy()
    c["c_bmask"] = (np.arange(128)[:, None] // 8 == np.arange(NB)[None, :]).astype(np.float32)
    sm = np.full((64, 248), NEG, np.float32)
    qt = np.arange(64) % 8
    for kk in range(8):
        sm[:, 120 + kk] = np.where(kk <= qt, 0.0, NEG)
    c["c_smask"] = sm.astype(bf)
    c["c_rcol"] = (np.arange(128) % 16).astype(np.float32)[:, None]
    half = 32
    inv = (1.0 / (np.float32(10000.0) ** (np.arange(half, dtype=np.float32) / np.float32(half)))).astype(np.float32)
    pos = np.concatenate([np.arange(NTOK_P), np.tile(16384 + np.arange(8), NB)]).astype(np.float32)
    ang = (pos[:, None] * inv[None, :]).astype(np.float32)
    cs = np.cos(ang).astype(np.float32)
    sn = np.sin(ang).astype(np.float32)
    c["c_cosm"] = cs
    c["c_sinm"] = sn
    c["c_cosT"] = (np.concatenate([cs, cs], axis=1).T * np.float32(SCALE)).astype(np.float32).copy()
    c["c_sinT"] = (np.concatenate([-sn, sn], axis=1).T * np.float32(SCALE)).astype(np.float32).copy()
    return c


def make_in_map(inp, c, consts):
    f = np.ascontiguousarray
    npool = inp["cache_ckv"].shape[1]
    pt = inp["page_table"][NB * c:NB * (c + 1)]
    ptq = f(pt.reshape(NB, 16, 8).transpose(2, 0, 1).reshape(8, NB * 16)).astype(np.int32)
    m = {
        "x": f(np.concatenate([inp["x_prompt"][c], inp["x_sample"][NB * c:NB * (c + 1)].reshape(128, D)], axis=0)),
        "pl": f(np.concatenate([inp["p_prompt"][0, c], inp["p_sample"][0, NB * c:NB * (c + 1)].reshape(128, 256)], axis=0)),
        "cache_ckv": inp["cache_ckv"][0].reshape(npool * 16, 2048),
        "cache_krope": inp["cache_krope"][0].reshape(npool * 16, 512),
        "ptq": ptq,
        "state_hgrn": f(inp["state_hgrn"][0, NB * c:NB * (c + 1)]),
        "state_conv": f(inp["state_conv"][0, NB * c:NB * (c + 1)].reshape(NB * 2, DFF)),
        "w_in": inp["w_in"][0], "w_q_b": inp["w_q_b"][0], "w_kv_b": inp["w_kv_b"][0], "w_out": inp["w_out"][0],
        "w_up": inp["w_up"][0], "w_down": inp["w_down"][0], "w_ple_gate": inp["w_ple_gate"][0],
        "w_ple_proj": inp["w_ple_proj"][0],
        "norm_mix": inp["norm_mix"][0], "norm_ffn": inp["norm_ffn"][0], "norm_ple": inp["norm_ple"][0],
        "norm_final": inp["norm_final"], "hg_lower": inp["hg_lower"], "hg_onorm": inp["hg_onorm"][0],
        "mla_q_norm": inp["mla_q_norm"][0], "mla_kv_norm": inp["mla_kv_norm"][0],
        "conv_w": inp["conv_w"][0], "conv_b": inp["conv_b"][0],
    }
    m.update(consts)
    return m


def assemble(results, ncores):
    y = np.stack([r["y"] for r in results])
    ckv = np.stack([r["ckv_o"] for r in results])
    kr = np.stack([r["kr_o"] for r in results])
    y_prompt = y[:, :NTOK_P]
    y_sample = y[:, NTOK_P:].reshape(ncores * NB, 8, D)
    ckv_prompt = ckv[:, :NTOK_P][None]
    ckv_sample = ckv[:, NTOK_P:].reshape(ncores * NB, 8, 256)[None]
    kr_prompt = kr[:, :NTOK_P][None]
    kr_sample = kr[:, NTOK_P:].reshape(ncores * NB, 8, 64)[None]
    hg_p = np.stack([r["hg_p"] for r in results])[None]
    cv_p = np.stack([r["conv_p"] for r in results])[None]
    hg_s = np.concatenate([r["hg_s"] for r in results], axis=0)[None]
    cv_s = np.concatenate([r["conv_s"] for r in results], axis=0)[None]
    return tuple(np.ascontiguousarray(a, dtype=np.float32) for a in
                 (y_prompt, y_sample, ckv_prompt, kr_prompt, hg_p, cv_p, ckv_sample, kr_sample, hg_s, cv_s))


def kernel(**inputs):
    inp = {k: np.asarray(v) for k, v in inputs.items()}
    npool = inp["cache_ckv"].shape[1]
    consts = _consts()
    nc = build_program(npool)
    ncores = 8
    in_maps = [make_in_map(inp, c, consts) for c in range(ncores)]
    res = run_bass_kernel_spmd(nc, in_maps, core_ids=list(range(ncores)))
    return assemble(res.results, ncores)
```

```python
import contextlib
import os
import numpy as np
import ml_dtypes
import concourse.bass as bass
import concourse.mybir as mybir
from concourse.bass_utils import run_bass_kernel_spmd

F32 = mybir.dt.float32
BF16 = mybir.dt.bfloat16
I32 = mybir.dt.int32
AF = mybir.ActivationFunctionType
ALU = mybir.AluOpType
AX = mybir.AxisListType

D = 2048
NTOK_P = 2048
NTOK = 2176
NBLK = 17
H = 8
DFF = 5632
NFC = 44
INW = 4928
EPS = 1e-6
SCALE = 192.0 ** -0.5
NEG = -1e30
NPAGES = 128
NB = 16
NGRP = 32


class Res:
    __slots__ = ("name", "w", "rs", "excl")

    ALL = []

    def __init__(self, name, excl=False):
        self.name = name
        self.w = None
        self.rs = []
        self.excl = excl
        Res.ALL.append(self)


class Op:
    __slots__ = ("eng", "fn", "deps", "is_dma", "sem", "semval", "signal", "count", "waits", "pos", "phase")

    def __init__(self, eng, fn, deps, is_dma=False):
        self.eng = eng
        self.fn = fn
        self.deps = deps
        self.is_dma = is_dma
        self.sem = None
        self.semval = 0
        self.signal = False
        self.count = 0
        self.waits = []
        self.pos = 0


LAST_PROG = None
ENGS = ("pe", "act", "dve", "pool", "sp")


class Prog:
    def __init__(self, nc, es):
        self.nc = nc
        self.es = es
        self.streams = {e: [] for e in ENGS}
        self.ops = []
        self.dkeys = {}
        self.esem = {}
        self.phase = "pro"

    def _deps(self, r, w):
        deps = []
        for x in r:
            if x.w is not None:
                deps.append(x.w)
            if x.excl:
                deps.extend(x.rs)
        for x in w:
            if x.w is not None:
                deps.append(x.w)
            deps.extend(x.rs)
        return deps

    def _commit(self, op, r, w):
        for x in r:
            x.rs.append(op)
        for x in w:
            x.w = op
            x.rs = []
        op.pos = len(self.streams[op.eng])
        op.phase = self.phase
        self.streams[op.eng].append(op)
        self.ops.append(op)

    def op(self, eng, fn, r=(), w=()):
        op = Op(eng, fn, self._deps(r, w))
        self._commit(op, r, w)
        return op

    def dma(self, eng, fn, r=(), w=(), key=None):
        op = Op(eng, fn, self._deps(r, w), is_dma=True)
        ent = self.dkeys.get(key)
        if ent is None:
            sem = self.es.enter_context(self.nc.semaphore("d_" + key))
            ent = [sem, 0, None]
            self.dkeys[key] = ent
        if ent[2] is not None:
            op.deps.append(ent[2])
        ent[1] += 16
        ent[2] = op
        op.sem = ent[0]
        op.semval = ent[1]
        self._commit(op, r, w)
        return op

    def dma2(self, engs, fns, r=(), w=(), key=None):
        deps = self._deps(r, w)
        ent = self.dkeys.get(key)
        if ent is None:
            sem = self.es.enter_context(self.nc.semaphore("d_" + key))
            ent = [sem, 0, None]
            self.dkeys[key] = ent
        if ent[2] is not None:
            deps.append(ent[2])
        last = None
        for eng, fn in zip(engs, fns):
            op = Op(eng, fn, list(deps), is_dma=True)
            ent[1] += 16
            op.sem = ent[0]
            op.semval = ent[1]
            self._commit(op, r, w)
            last = op
        ent[2] = last
        return last

    def finalize(self):
        for e in ENGS:
            self.esem[e] = self.es.enter_context(self.nc.semaphore("e_" + e))
        for op in self.ops:
            best = {}
            for d in op.deps:
                if d.is_dma:
                    continue
                if d.eng == op.eng and op.eng == "pe" and not op.is_dma:
                    continue
                if d.eng not in best or best[d.eng].pos < d.pos:
                    best[d.eng] = d
            op.deps = [d for d in op.deps if d.is_dma] + list(best.values())
            for d in best.values():
                d.signal = True
        for e in ENGS:
            c = 0
            for op in self.streams[e]:
                if op.signal:
                    c += 1
                    op.count = c
        for op in self.ops:
            ws = {}
            for d in op.deps:
                if d.is_dma:
                    k = id(d.sem)
                    if k not in ws or ws[k][1] < d.semval:
                        ws[k] = (d.sem, d.semval)
                else:
                    if d.eng == op.eng and op.eng == "pe" and not op.is_dma:
                        continue
                    sem = self.esem[d.eng]
                    k = id(sem)
                    if k not in ws or ws[k][1] < d.count:
                        ws[k] = (sem, d.count)
            op.waits = list(ws.values())

    def emit(self, engname, eng, final_waits=False):
        seen = {}
        for op in self.streams[engname]:
            for sem, val in op.waits:
                k = id(sem)
                if seen.get(k, 0) >= val:
                    continue
                eng.wait_ge(sem, val)
                seen[k] = val
            inst = op.fn(eng)
            if op.is_dma:
                inst.then_inc(op.sem, 16)
            elif op.signal:
                inst.then_inc(self.esem[engname], 1)
        if final_waits:
            for key, ent in self.dkeys.items():
                if ent[1] > 0 and seen.get(id(ent[0]), 0) < ent[1]:
                    eng.wait_ge(ent[0], ent[1])
            for e in ENGS:
                if e == engname:
                    continue
                last = 0
                for op in self.streams[e]:
                    if op.signal:
                        last = op.count
                if last > 0 and seen.get(id(self.esem[e]), 0) < last:
                    eng.wait_ge(self.esem[e], last)


def build_program(npool, nblk_run=NBLK, do_sample=True):
    nc = bass.Bass("TRN2", target_bir_lowering=False)
    es = contextlib.ExitStack()
    P = Prog(nc, es)
    Res.ALL = []

    def din(name, shape, dt=F32):
        return nc.dram_tensor(name, list(shape), dt, kind="ExternalInput").ap()

    def dout(name, shape, dt=F32):
        return nc.dram_tensor(name, list(shape), dt, kind="ExternalOutput").ap()

    def dscr(name, shape, dt=BF16):
        return nc.dram_tensor(name, list(shape), dt).ap()

    def sb(name, shape, dt):
        return es.enter_context(nc.sbuf_tensor(name, list(shape), dt))

    ARENA_BYTES = 58 * 1024
    arena = sb("arena", [128, ARENA_BYTES // 2], BF16)

    class Carver:
        def __init__(self):
            self.off = 0

        def __call__(self, name, shape, dt):
            esz = 2 if dt == BF16 else 4
            n = 1
            for d_ in shape[1:]:
                n *= d_
            nb = n * esz
            a = self.off
            self.off += (nb + 31) // 32 * 32
            assert self.off <= ARENA_BYTES, (name, self.off)
            ap = arena[0:shape[0], a // 2:(a + nb) // 2]
            if dt != BF16:
                ap = ap.bitcast(dt)
            if len(shape) == 3:
                ap = ap.rearrange("p (a b) -> p a b", a=shape[1])
            elif len(shape) == 4:
                ap = ap.rearrange("p (a b c) -> p a b c", a=shape[1], b=shape[2])
            return ap

    cvP = Carver()
    cvS = Carver()

    x_d = din("x", [NTOK, D])
    p_d = din("pl", [NTOK, 256])
    ck_d = din("cache_ckv", [npool * 16, 2048])
    kr_d = din("cache_krope", [npool * 16, 512])
    ptq_d = din("ptq", [8, NB * 16], I32)
    sth_d = din("state_hgrn", [NB, H, 128, 128])
    stc_d = din("state_conv", [NB * 2, DFF])
    w_in_d = din("w_in", [D, INW])
    w_qb_d = din("w_q_b", [512, 1536])
    w_kvb_d = din("w_kv_b", [256, 2048])
    w_out_d = din("w_out", [D, D])
    w_up_d = din("w_up", [D, 2 * DFF])
    w_down_d = din("w_down", [DFF, D])
    w_pg_d = din("w_ple_gate", [D, D])
    w_pp_d = din("w_ple_proj", [256, D])
    nmix_d = din("norm_mix", [D])
    nffn_d = din("norm_ffn", [D])
    nple_d = din("norm_ple", [D])
    nfin_d = din("norm_final", [D])
    hgl_d = din("hg_lower", [2, 1024])
    onorm_d = din("hg_onorm", [1024])
    qnorm_d = din("mla_q_norm", [512])
    kvnorm_d = din("mla_kv_norm", [256])
    convw_d = din("conv_w", [3, DFF])
    convb_d = din("conv_b", [DFF])
    ident_d = din("c_ident", [128, 128], BF16)
    identf_d = din("c_identf", [128, 128], F32)
    ones_d = din("c_ones", [128, 128], BF16)
    cmask_d = din("c_cmask", [128, 128], BF16)
    bd64_d = din("c_bd64", [128, 128], BF16)
    bd8_d = din("c_bd8", [128, 128], BF16)
    m64_d = din("c_m64", [128, 512], F32)
    m8_d = din("c_m8", [128, 512], F32)
    bmask_d = din("c_bmask", [128, NB], F32)
    smask_d = din("c_smask", [64, 248], BF16)
    rcol_d = din("c_rcol", [128, 1], F32)
    cosT_d = din("c_cosT", [64, NTOK])
    sinT_d = din("c_sinT", [64, NTOK])
    cosm_d = din("c_cosm", [NTOK, 32])
    sinm_d = din("c_sinm", [NTOK, 32])

    y_o = dout("y", [NTOK, D])
    ckv_o = dout("ckv_o", [NTOK, 256])
    kro_o = dout("kr_o", [NTOK, 64])
    hgp_o = dout("hg_p", [H, 128, 128])
    cvp_o = dout("conv_p", [2, DFF])
    hgs_o = dout("hg_s", [NB, H, 128, 128])
    cvs_o = dout("conv_s", [NB, 2, DFF])
    DBG = os.environ.get("KDBG", "0") == "1"
    if DBG:
        dbg1 = dout("dbg1", [128, 2048], BF16)
        dbg2 = dout("dbg2", [64, 264], F32)
        dbg3 = dout("dbg3", [128, NB * 16], I32)
        dbg4 = dout("dbg4", [128, 4 * 256], BF16)

    wb_in = dscr("wb_in", [10, 128, 16, 512])
    wb_up = dscr("wb_up", [22, 128, 16, 512])
    wb_dn = dscr("wb_dn", [16, 128, 11, 512])
    wb_out = dscr("wb_out", [4, 128, 16, 512])
    wb_pg = dscr("wb_pg", [4, 128, 16, 512])
    wb_pp = dscr("wb_pp", [4, 128, 2, 512])
    wb_q = dscr("wb_q", [4, 128, 4, 512])
    R_wb = {}
    for n_, cnt_ in (("in", 10), ("up", 22), ("dn", 16), ("out", 4), ("pg", 4), ("pp", 4), ("q", 4)):
        for i_ in range(cnt_):
            R_wb[(n_, i_)] = Res("wb_%s%d" % (n_, i_))

    ps = es.enter_context(nc.psum_tensor("ps", [128, 4096], F32))
    R_bank = [Res("bank%d" % i, excl=True) for i in range(8)]

    def bank(i, n=1):
        return ps[:, i * 512:(i + n) * 512]

    def bankbf(i, n=1):
        return ps[:, i * 512:(i + n) * 512].bitcast(BF16)

    ident = sb("ident", [128, 128], BF16)
    identf = sb("identf", [128, 128], F32)
    ones = sb("ones", [128, 128], BF16)
    cmask = sb("cmask", [128, 128], BF16)
    bd64 = sb("bd64", [128, 128], BF16)
    bd8 = sb("bd8", [128, 128], BF16)
    bmask = sb("bmask", [128, NB], F32)
    smask = sb("smask", [64, 248], BF16)
    rcol = sb("rcol", [128, 1], F32)
    gT = sb("gT", [128, 3, 16], F32)
    hgl = sb("hgl", [128, 2, H], F32)
    lbT = sb("lbT", [128, H], F32)
    omlT = sb("omlT", [128, H], F32)
    nomlT = sb("nomlT", [128, H], F32)
    onormT = sb("onormT", [128, H], F32)
    qnorm_bc = sb("qnorm_bc", [128, 512], F32)
    kvnorm_bc = sb("kvnorm_bc", [128, 256], F32)
    cwT = sb("cwT", [128, 3, NFC], F32)
    cbT = sb("cbT", [128, NFC], F32)
    cvW = Carver()
    cvW.off = 52 * 1024
    wuk = cvW("wuk", [128, 2, H, 128], BF16)
    wukT = sb("wukT", [128, H, 256], BF16)
    wuv = sb("wuv", [128, 2, H, 128], BF16)
    R_const = Res("const")
    R_wq = Res("wq")
    R_wukT = Res("wukT")
    R_lb = Res("lb")

    def cload(dst, src, key="const", eng="sp", w=None):
        P.dma(eng, lambda e, dst=dst, src=src: e.dma_start(out=dst, in_=src, allow_slow_non_contiguous=True), w=[w or R_const], key=key)

    with nc.allow_non_contiguous_dma(reason="small constant layouts"):
        cload(ident[:], ident_d)
        cload(identf[:], identf_d)
        cload(ones[:], ones_d)
        cload(cmask[:], cmask_d)
        cload(bd64[:], bd64_d)
        cload(bd8[:], bd8_d)
        cload(bmask[:], bmask_d)
        cload(smask[:], smask_d)
        cload(rcol[:], rcol_d)
        cload(gT[:, 0, :], nmix_d.rearrange("(k p) -> p k", p=128))
        cload(gT[:, 1, :], nffn_d.rearrange("(k p) -> p k", p=128))
        cload(gT[:, 2, :], nple_d.rearrange("(k p) -> p k", p=128))
        cload(hgl[:, 0, :], hgl_d[0].rearrange("(h p) -> p h", p=128))
        cload(hgl[:, 1, :], hgl_d[1].rearrange("(h p) -> p h", p=128))
        cload(onormT[:], onorm_d.rearrange("(h p) -> p h", p=128))
        cload(qnorm_bc[:], qnorm_d.partition_broadcast(128))
        cload(kvnorm_bc[:], kvnorm_d.partition_broadcast(128))
        cload(cwT[:], convw_d.rearrange("j (c p) -> p j c", p=128))
        cload(cbT[:], convb_d.rearrange("(c p) -> p c", p=128))
        for kc in range(4):
            wqv = w_qb_d[kc * 128:(kc + 1) * 128, :].rearrange("p (h c) -> p h c", c=192)
            for pn in range(4):
                dstv = wb_q[pn, :, kc, :].rearrange("p (hh c) -> p hh c", c=256)
                srcv = wqv[:, 2 * pn:2 * pn + 2, :]
                for (d0, d1, s0, s1) in ((0, 192, 0, 192), (192, 224, 160, 192), (224, 256, 128, 160)):
                    P.dma("pool", lambda e, d=dstv[:, :, d0:d1], s_=srcv[:, :, s0:s1]: e.dma_start(out=d, in_=s_, allow_slow_non_contiguous=True),
                          w=[R_wb[("q", pn)]], key="cast_q%d" % (pn % 2))
        for kc in range(2):
            wkv = w_kvb_d[kc * 128:(kc + 1) * 128, :].rearrange("p (h c) -> p h c", c=256)
            cload(wuk[:, kc, :, :], wkv[:, :, 0:128], key="wq", eng="pool", w=R_wq)
            cload(wuv[:, kc, :, :], wkv[:, :, 128:256], key="wq", eng="pool", w=R_wq)

    dlb = sb("dlb", [128, H], F32)
    P.op("dve", lambda e: e.tensor_tensor(out=dlb[:], in0=hgl[:, 0, :], in1=hgl[:, 1, :], op=ALU.subtract),
         r=[R_const], w=[R_lb])
    P.op("act", lambda e: e.activation(out=lbT[:], in_=dlb[:], func=AF.Sigmoid), r=[R_lb], w=[R_lb])
    P.op("dve", lambda e: e.tensor_scalar(out=omlT[:], in0=lbT[:], scalar1=-1.0, scalar2=1.0,
                                          op0=ALU.mult, op1=ALU.add), r=[R_lb], w=[R_lb])
    P.op("dve", lambda e: e.tensor_scalar(out=nomlT[:], in0=lbT[:], scalar1=1.0, scalar2=-1.0,
                                          op0=ALU.mult, op1=ALU.add), r=[R_lb], w=[R_lb])

    T2 = bankbf(0, 2)
    R_T = Res("Tbanks", excl=True)
    for h in range(H):
        for kc in range(2):
            P.op("pe", lambda e, h=h, kc=kc: e.transpose(out=T2[:, (h * 2 + kc) * 128:(h * 2 + kc + 1) * 128],
                                                         in_=wuk[:, kc, h, :], identity=ident[:]),
                 r=[R_wq, R_const], w=[R_T])
    P.op("dve", lambda e: e.tensor_copy(out=wukT[:].rearrange("p h c -> p (h c)"), in_=T2[:, 0:2048]),
         r=[R_T], w=[R_wukT])

    def cast_panels(dst, src, ncols_total, kc_n, name, col0=0, kbase=0, pidx0=0, npan=None):
        npan = npan if npan is not None else (ncols_total + 511) // 512
        order = list(range(npan))
        if name == "up":
            order = [x for pa_ in range(11) for x in (pa_, 11 + pa_)]
        for j in order:
            c0 = col0 + j * 512
            cw = min(512, col0 + ncols_total - c0)
            srcv = src[kbase:kbase + kc_n * 128, c0:c0 + cw].rearrange("(k p) c -> p k c", p=128)
            P.dma("pool", lambda e, d=dst[pidx0 + j, :, :, 0:cw], s=srcv: e.dma_start(out=d, in_=s),
                  w=[R_wb[(name, pidx0 + j)]], key="cast_%s%d" % (name, j % 2))

    cast_panels(wb_in, w_in_d, INW, 16, "in")
    cast_panels(wb_out, w_out_d, D, 16, "out")
    cast_panels(wb_up, w_up_d, 2 * DFF, 16, "up")
    for cb in range(4):
        for kg in range(4):
            srcv = w_down_d[kg * 1408:(kg + 1) * 1408, cb * 512:(cb + 1) * 512].rearrange("(k p) c -> p k c", p=128)
            P.dma("pool", lambda e, d=wb_dn[cb * 4 + kg], s=srcv: e.dma_start(out=d, in_=s),
                  w=[R_wb[("dn", cb * 4 + kg)]], key="cast_dn%d" % (kg % 2))
    cast_panels(wb_pg, w_pg_d, D, 16, "pg")
    cast_panels(wb_pp, w_pp_d, D, 2, "pp")

    class Rot:
        def __init__(self, name, shape, dt, n, alloc=None):
            alloc = alloc or sb
            self.t = [alloc("%s%d" % (name, i), shape, dt) for i in range(n)]
            self.r = [Res("%s%d" % (name, i)) for i in range(n)]
            self.i = 0
            self.n = n
            self.name = name

        def next(self):
            k = self.i % self.n
            self.i += 1
            return self.t[k], self.r[k], "%s%d" % (self.name, k)

    panels = Rot("pan", [128, 16, 512], BF16, 3)

    panq = [0, False]

    def load_panel(wb_t, pidx_, kc_n, rname, cw=512):
        src_panel = wb_t[pidx_]
        t, r, key = panels.next()
        panq[0] += 1
        if kc_n >= 4:
            h1 = kc_n // 2
            e2 = "sp" if panq[1] else "pool"
            P.dma2(("sp", e2),
                   (lambda e, t=t, s=src_panel, cw=cw, h1=h1: e.dma_start(out=t[:, 0:h1, 0:cw], in_=s[:, 0:h1, 0:cw]),
                    lambda e, t=t, s=src_panel, cw=cw, h1=h1, kc_n=kc_n: e.dma_start(out=t[:, h1:kc_n, 0:cw], in_=s[:, h1:kc_n, 0:cw])),
                   r=[R_wb[(rname, pidx_)]], w=[r], key=key)
        else:
            P.dma("sp", lambda e, t=t, s=src_panel, kc_n=kc_n, cw=cw: e.dma_start(out=t[:, 0:kc_n, 0:cw], in_=s[:, 0:kc_n, 0:cw]),
                  r=[R_wb[(rname, pidx_)]], w=[r], key=key)
        return t, r

    xt = sb("xt", [128, D], F32)
    R_x = Res("x")
    xsq = sb("xsq", [128, D], BF16)
    R_xsq = Res("xsq")
    st1 = sb("st1", [128, 8], F32)
    R_st = Res("st1")
    xnT = sb("xnT", [128, 16, 128], BF16)
    R_xnT = Res("xnT")
    mixT = sb("mixT", [128, 16, 128], BF16)
    R_mix = Res("mixT")
    hT = sb("hT", [128, NFC, 128], BF16)
    R_hT = Res("hT")

    m64 = cvP("m64", [128, 512], F32)
    m8 = cvS("m8", [128, 512], F32)
    R_m = Res("scanmask")
    P.dma("sp", lambda e: e.dma_start(out=m64[:], in_=m64_d), w=[R_m], key="m64")
    KaugT = cvP("KaugT", [128, 3, NTOK_P], BF16)
    ckv_tm = cvP("ckv_tm", [128, 16, 256], BF16)
    R_K = Res("Kcache")
    Sst = cvP("Sst", [128, H, 128], F32)
    Sbf = cvP("Sbf", [128, H, 128], BF16)
    R_S = [Res("S%d" % h) for h in range(H)]
    convc = cvP("convc", [128, NFC, 2], F32)
    R_convc = Res("convc")
    P.op("dve", lambda e: e.memset(Sst[:], 0.0), w=R_S)
    P.op("dve", lambda e: e.memset(Sbf[:], 0.0), w=R_S)
    P.op("dve", lambda e: e.memset(convc[:], 0.0), w=[R_convc])
    P.op("dve", lambda e: e.memset(KaugT[:], 0.0), w=[R_K])

    def rstd_from_ss(ss_ap, out_ap, n, rs, ws):
        P.op("act", lambda e: e.activation(out=out_ap, in_=ss_ap, func=AF.Ln, scale=1.0 / n, bias=epsb[:, 0:1]),
             r=rs, w=ws)
        P.op("act", lambda e: e.activation(out=out_ap, in_=out_ap, func=AF.Exp, scale=-0.5), r=ws, w=ws)

    epsb = sb("epsb", [128, 1], F32)
    P.op("dve", lambda e: e.memset(epsb[:], EPS), w=[R_const])

    def norm_to_xnT(gi):
        P.op("act", lambda e: e.activation(out=xsq[:], in_=xt[:], func=AF.Square, accum_out=st1[:, 0:1]),
             r=[R_x], w=[R_xsq, R_st])
        rstd_from_ss(st1[:, 0:1], st1[:, 1:2], D, [R_st, R_const], [R_st])
        P.op("act", lambda e: e.activation(out=xsq[:], in_=xt[:], func=AF.Copy, scale=st1[:, 1:2]),
             r=[R_x, R_st], w=[R_xsq])
        for kc in range(16):
            P.op("pe", lambda e, kc=kc: e.transpose(out=T2[:, kc * 128:(kc + 1) * 128], in_=xsq[:, kc * 128:(kc + 1) * 128],
                                                    identity=ident[:]), r=[R_xsq, R_const], w=[R_T])
        for kc in range(16):
            eng = "dve" if kc % 2 == 0 else "act"
            if eng == "dve":
                P.op("dve", lambda e, kc=kc: e.tensor_scalar(out=xnT[:, kc, :], in0=T2[:, kc * 128:(kc + 1) * 128],
                                                             scalar1=gT[:, gi, kc:kc + 1], scalar2=None, op0=ALU.mult),
                     r=[R_T, R_const], w=[R_xnT])
            else:
                P.op("act", lambda e, kc=kc: e.activation(out=xnT[:, kc, :], in_=T2[:, kc * 128:(kc + 1) * 128],
                                                          func=AF.Copy, scale=gT[:, gi, kc:kc + 1]),
                     r=[R_T, R_const], w=[R_xnT])

    def mm_fm(out_ap, pan, c0, ncol, act, kcn, rs, ws, rows=128):
        for kc in range(kcn):
            P.op("pe", lambda e, kc=kc: e.matmul(out_ap, lhsT=pan[:, kc, c0:c0 + ncol], rhs=act[:, kc, :],
                                                 start=(kc == 0), stop=(kc == kcn - 1)), r=rs, w=ws)

    def mm_tm(out_ap, act, pan, c0, ncol, kcn, rs, ws, kc0=0, first=True, last=True, pk0=0):
        for kc in range(kcn):
            P.op("pe", lambda e, kc=kc: e.matmul(out_ap, lhsT=act[:, kc0 + kc, :], rhs=pan[:, pk0 + kc, c0:c0 + ncol],
                                                 start=(first and kc == 0), stop=(last and kc == kcn - 1)), r=rs, w=ws)

    vtm = sb("vtm", [128, 1024], BF16)
    R_v = Res("vtm")
    qdT = sb("qdT", [128, H, 128], BF16)
    keT = sb("keT", [128, H, 128], BF16)
    gateT = sb("gateT", [128, H, 128], BF16)
    Edec = sb("Edec", [128, H, 128], F32)
    R_hg = [Res("hg%d" % i) for i in range(2)]
    tA = [sb("tA%d" % i, [128, 512], F32) for i in range(5)]
    R_tA = [Res("tA%d" % i) for i in range(5)]
    tA.append(tA[1])
    R_tA.append(R_tA[1])
    ketm = sb("ketm", [128, H, 128], BF16)
    R_ketm = Res("ketm")
    ATm = sb("ATm", [128, 128], BF16)
    R_ATm = Res("ATm")
    osq = sb("osq", [128, 128], BF16)
    R_osq = Res("osq")
    orst = sb("orst", [128, 128], F32)
    R_orst = Res("orst")
    otmp = sb("otmp", [128, 128], F32)
    R_otmp = Res("otmp")
    cqn = sb("cqn", [128, 512], BF16)
    R_cqn = Res("cqn")
    cqnT = sb("cqnT", [128, 4, 128], BF16)
    R_cqnT = Res("cqnT")
    ckvf = Rot("ckvf", [128, 256], F32, 2)
    krf = Rot("krf", [128, 64], F32, 2)
    krtmp = sb("krtmp", [128, 6, 32], F32)
    R_krtmp = Res("krtmp")
    krbf = sb("krbf", [128, 64], BF16)
    ckvbf_s = cvS("ckvbf_s", [128, 256], BF16)
    KaugT_s = cvS("KaugT_s", [128, 3, 128], BF16)
    R_Ks = Res("Ks")
    cosm = Rot("cosm", [128, 32], F32, 2)
    sinm = Rot("sinm", [128, 32], F32, 2)
    cosT = Rot("cosT", [64, 128], F32, 2)
    sinT = Rot("sinT", [64, 128], F32, 2)
    qnT = sb("qnT", [128, 128], BF16)
    R_qnT = Res("qnT")
    QaugT = cvP("QaugT", [128, 3, 128], BF16)
    R_Q = Res("QaugT")
    QaugT_b = cvP("QaugT_b", [128, 3, 128], BF16)
    R_Qb = Res("QaugT_b")
    st2 = sb("st2", [128, 2], F32)
    R_st2 = Res("st2")
    QaugT_all = cvS("QaugT_all", [128, 3, NB, 64], BF16)
    QaugT_s = cvS("QaugT_s", [128, 3, 128], BF16)
    R_Qs = Res("QaugT_s")
    R_Qall = Res("Qall")
    qtmp = sb("qtmp", [64, 2, 128], F32)
    R_qtmp = Res("qtmp")
    Pf = cvP("Pf", [128, 2048], F32)
    R_Pf = Res("Pf")
    Pn = cvP("Pn", [128, 2048], BF16)
    R_Pn = Res("Pn")
    PT = cvP("PT", [128, 16, 128], BF16)
    R_PT = Res("PT")
    latT = cvP("latT", [128, 2, 128], BF16)
    R_latT = Res("latT")
    ptile = sb("ptile", [128, 256], F32)
    R_ptile = Res("ptile")
    pbf = sb("pbf", [128, 256], BF16)
    R_pbf = Res("pbf")
    pT = sb("pT", [128, 2, 128], BF16)
    R_pT = Res("pT")
    aext = Rot("aext", [128, 160], F32, 1)
    cva = Rot("cva", [128, 128], F32, 1)
    cvb = Rot("cvb", [128, 128], F32, 1)
    atmr = Rot("atmr", [128, 512], F32, 1)
    sconvT = cvS("sconvT", [128, NFC, 32], F32)
    R_sconv = Res("sconv")
    R_sctm = Res("sctm")
    sgt = Rot("sgt", [128, 512], F32, 1)

    ptb = cvS("ptb", [128, NB * 16], I32)
    idx = cvS("idx", [128, NB * 16], I32)
    R_idx = Res("idx")
    accs = cvS("accs", [64, 256], F32)
    fst = cvS("fst", [64, 8], F32)
    R_acc = Res("acc")
    R_fst = Res("fst")
    lat_s = cvS("lat_s", [64, 256], BF16)
    R_lats = Res("lat_s")
    latT_s = cvS("latT_s", [128, 2, H, 128], BF16)
    R_latTs = Res("latT_s")
    cvS2 = Carver()
    cvS2.off = cvS.off
    cvS3 = Carver()
    cvS3.off = cvS.off
    Sload = Rot("Sload", [128, H, 128], F32, 2, cvS)
    Sloadbf = Rot("Sloadbf", [128, H, 128], BF16, 2, cvS)
    Gc = Rot("Gc", [128, 8, 256], BF16, 3, cvS2)
    Gr = Rot("Gr", [128, 8, 64], BF16, 3, cvS2)
    KT = Rot("KT", [128, 3, 512], BF16, 3, cvS2)
    Pbs = Rot("Pbs", [64, 512], BF16, 2, cvS2)
    PTs = Rot("PTs", [128, 4, 64], BF16, 2, cvS2)
    sctm = cvS3("sctm", [32, 2048], F32)

    def block(T):
        sample = (T == 16)
        tok0 = T * 128
        P.dma("sp", lambda e: e.dma_start(out=xt[:], in_=x_d[tok0:tok0 + 128, :]), w=[R_x], key="x")
        cm, r_cm, k_cm = cosm.next()
        sm, r_sm, k_sm = sinm.next()
        cT, r_cT, k_cT = cosT.next()
        sT, r_sT, k_sT = sinT.next()
        P.dma("sp", lambda e: e.dma_start(out=cm[:], in_=cosm_d[tok0:tok0 + 128, :]), w=[r_cm], key=k_cm)
        P.dma("sp", lambda e: e.dma_start(out=sm[:], in_=sinm_d[tok0:tok0 + 128, :]), w=[r_sm], key=k_sm)
        P.dma("sp", lambda e: e.dma_start(out=cT[:], in_=cosT_d[:, tok0:tok0 + 128]), w=[r_cT], key=k_cT)
        P.dma("sp", lambda e: e.dma_start(out=sT[:], in_=sinT_d[:, tok0:tok0 + 128]), w=[r_sT], key=k_sT)

        STG = int(os.environ.get("KSTAGE", "99"))
        P.phase = "%d:norm1" % T
        norm_to_xnT(0)
        if STG <= 1:
            return
        P.phase = "%d:w_in" % T
        for half in range(2):
            for j, (pidx, bk) in enumerate(((half, 2), (2 + half, 3), (6 + half, 4))):
                pan, rp = load_panel(wb_in, pidx, 16, "in")
                for hh in range(4):
                    mm_fm(bank(bk)[:, hh * 128:(hh + 1) * 128], pan, hh * 128, 128, xnT, 16,
                          [rp, R_xnT], [R_bank[bk]])
            hg_elementwise(half, sample)
        if STG <= 2:
            return
        for j in range(2):
            pan, rp = load_panel(wb_in, 4 + j, 16, "in")
            mm_tm(bank(5), xnT, pan, 0, 512, 16, [rp, R_xnT], [R_bank[5]])
            P.op("act", lambda e, j=j: e.activation(out=vtm[:, j * 512:(j + 1) * 512], in_=bank(5), func=AF.Copy),
                 r=[R_bank[5]], w=[R_v])
        pan, rp = load_panel(wb_in, 8, 16, "in")
        mm_tm(bank(6), xnT, pan, 0, 512, 16, [rp, R_xnT], [R_bank[6]])
        pan, rp = load_panel(wb_in, 9, 16, "in", cw=320)
        mm_tm(bank(7)[:, 0:320], xnT, pan, 0, 320, 16, [rp, R_xnT], [R_bank[7]])
        P.phase = "%d:latents" % T
        mla_latents(T, sample, cm, r_cm, sm, r_sm)
        if STG <= 3:
            return
        P.phase = "%d:hgrec" % T
        hg_recurrence(T, sample)
        P.phase = "%d:attn" % T
        if STG <= 4:
            return
        if sample:
            barrier()
            for h in range(H):
                if h % 2 == 0:
                    wqp, r_wqp = load_panel(wb_q, h // 2, 4, "q")
                mla_q(h, cT, r_cT, sT, r_sT, QaugT_s[:], R_Qs, wqp, r_wqp)
                for c in range(3):
                    rows = 128 if c < 2 else 64
                    P.op("dve" if c != 1 else "act",
                         (lambda e, c=c, rows=rows, h=h: e.tensor_copy(
                             out=QaugT_all[0:rows, c, :, h * 8:(h + 1) * 8],
                             in_=QaugT_s[0:rows, c, :].rearrange("p (b t) -> p b t", t=8))) if c != 1 else
                         (lambda e, c=c, rows=rows, h=h: e.activation(
                             out=QaugT_all[0:rows, c, :, h * 8:(h + 1) * 8],
                             in_=QaugT_s[0:rows, c, :].rearrange("p (b t) -> p b t", t=8), func=AF.Copy)),
                         r=[R_Qs], w=[R_Qall])
            sample_attention()
        else:
            Qs2 = [(QaugT, R_Q), (QaugT_b, R_Qb)]
            wqp, r_wqp = load_panel(wb_q, 0, 4, "q")
            mla_q(0, cT, r_cT, sT, r_sT, QaugT[:], R_Q, wqp, r_wqp)
            pa_S(T, 0, Qs2[0])
            for h in range(H):
                pa_softmax(T, h)
                if h + 1 < H:
                    if (h + 1) % 2 == 0:
                        wqp, r_wqp = load_panel(wb_q, (h + 1) // 2, 4, "q")
                    Qn, r_Qn = Qs2[(h + 1) % 2]
                    mla_q1(h + 1, cT, r_cT, sT, r_sT, Qn, r_Qn, wqp, r_wqp)
                pa_PT(T, h)
                if h + 1 < H:
                    mla_q2(h + 1, Qn, r_Qn)
                pa_rest(T, h)
                if h + 1 < H:
                    pa_S(T, h + 1, Qs2[(h + 1) % 2])
        if DBG and sample:
            P.dma("sp", lambda e: e.dma_start(out=dbg1, in_=mixT[:].rearrange("p k t -> p (k t)")), r=[R_mix], key="dbg1")
            P.dma("sp", lambda e: e.dma_start(out=dbg2[:, 0:256], in_=accs[:]), r=[R_acc], key="dbg2")
            P.dma("sp", lambda e: e.dma_start(out=dbg2[:, 256:264], in_=fst[:]), r=[R_fst], key="dbg2")
            P.dma("sp", lambda e: e.dma_start(out=dbg3, in_=idx[:]), r=[R_idx], key="dbg3")
            pass
        if STG <= 5:
            return
        P.phase = "%d:w_out" % T
        for cb in range(4):
            pan, rp = load_panel(wb_out, cb, 16, "out")
            bk = 5 + (cb % 2)
            mm_tm(bank(bk), mixT, pan, 0, 512, 16, [rp, R_mix], [R_bank[bk]])
            P.op("dve", lambda e, cb=cb, bk=bk: e.tensor_tensor(out=xt[:, cb * 512:(cb + 1) * 512],
                                                                in0=xt[:, cb * 512:(cb + 1) * 512], in1=bank(bk), op=ALU.add),
                 r=[R_bank[bk], R_x], w=[R_x])

        if STG <= 6:
            return
        P.phase = "%d:ffn_up" % T
        norm_to_xnT(1)
        need_atm = sample or T == 15
        if sample:
            barrier()
            for g3 in range(3):
                n = 16 if g3 < 2 else 12
                P.dma("sp", lambda e, g3=g3, n=n: e.dma_start(out=sctm[:, 0:n * 128], in_=stc_d[:, g3 * 2048:g3 * 2048 + n * 128]),
                      w=[R_sctm], key="sctm")
                for i in range(n):
                    P.op("pe", lambda e, i=i, g3=g3: e.transpose(out=bank(2 + g3)[:, i * 32:(i + 1) * 32],
                                                                 in_=sctm[:, i * 128:(i + 1) * 128], identity=identf[0:32, 0:32]),
                         r=[R_sctm, R_const], w=[R_bank[2 + g3]])
                P.op("dve", lambda e, g3=g3, n=n: e.tensor_copy(out=sconvT[:, g3 * 16:g3 * 16 + n, :].rearrange("p c j -> p (c j)"),
                                                                in_=bank(2 + g3)[:, 0:n * 32]),
                     r=[R_bank[2 + g3]], w=[R_sconv])
        for pa in range(11):
            pana, rpa = load_panel(wb_up, pa, 16, "up")
            pang, rpg = load_panel(wb_up, 11 + pa, 16, "up")
            if need_atm:
                mm_tm(bank(7), xnT, pana, 0, 512, 16, [rpa, R_xnT], [R_bank[7]])
                at_, r_at, k_at = atmr.next()
                P.op("act", lambda e, at_=at_: e.activation(out=at_[:], in_=bank(7), func=AF.Copy),
                     r=[R_bank[7]], w=[r_at])
                if sample:
                    for t in range(2):
                        P.dma("sp", lambda e, t=t, at_=at_, pa=pa: e.dma_start(out=cvs_o[:, t, pa * 512:(pa + 1) * 512],
                                                                               in_=at_[6 + t:128:8, :]), r=[r_at], key=k_at)
                else:
                    P.dma("sp", lambda e, at_=at_, pa=pa: e.dma_start(out=cvp_o[:, pa * 512:(pa + 1) * 512], in_=at_[126:128, :]),
                          r=[r_at], key=k_at)
            bA = 2 + (pa % 2)
            bG = 4 + (pa % 2)
            for j in range(4):
                mm_fm(bank(bA)[:, j * 128:(j + 1) * 128], pana, j * 128, 128, xnT, 16, [rpa, R_xnT], [R_bank[bA]])
            for j in range(4):
                mm_fm(bank(bG)[:, j * 128:(j + 1) * 128], pang, j * 128, 128, xnT, 16, [rpg, R_xnT], [R_bank[bG]])
            if sample:
                for j in range(4):
                    ffn_chunk(pa * 4 + j, bank(bA)[:, j * 128:(j + 1) * 128], R_bank[bA],
                              bank(bG)[:, j * 128:(j + 1) * 128], R_bank[bG], True)
            else:
                ffn_group(pa, bA, bG)
        P.phase = "%d:ffn_dn" % T
        for cb in range(4):
            bk = 5 + (cb % 2)
            for kg in range(4):
                pan, rp = load_panel(wb_dn, cb * 4 + kg, 11, "dn")
                mm_tm(bank(bk), hT, pan, 0, 512, 11, [rp, R_hT], [R_bank[bk]], kc0=kg * 11,
                      first=(kg == 0), last=(kg == 3))
            P.op("dve", lambda e, cb=cb, bk=bk: e.tensor_tensor(out=xt[:, cb * 512:(cb + 1) * 512],
                                                                in0=xt[:, cb * 512:(cb + 1) * 512], in1=bank(bk), op=ALU.add),
                 r=[R_bank[bk], R_x], w=[R_x])

        if STG <= 7:
            return
        P.phase = "%d:ple" % T
        norm_to_xnT(2)
        P.dma("sp", lambda e: e.dma_start(out=ptile[:], in_=p_d[tok0:tok0 + 128, :]), w=[R_ptile], key="ptile")
        P.op("dve", lambda e: e.tensor_copy(out=pbf[:], in_=ptile[:]), r=[R_ptile], w=[R_pbf])
        for kc in range(2):
            P.op("pe", lambda e, kc=kc: e.transpose(out=T2[:, kc * 128:(kc + 1) * 128], in_=pbf[:, kc * 128:(kc + 1) * 128],
                                                    identity=ident[:]), r=[R_pbf, R_const], w=[R_T])
        P.op("dve", lambda e: e.tensor_copy(out=pT[:].rearrange("p k t -> p (k t)"), in_=T2[:, 0:256]), r=[R_T], w=[R_pT])
        for cb in range(4):
            pan, rp = load_panel(wb_pg, cb, 16, "pg")
            pan2, rp2 = load_panel(wb_pp, cb, 2, "pp")
            bk = 5 + (cb % 2)
            mm_tm(bank(bk), xnT, pan, 0, 512, 16, [rp, R_xnT], [R_bank[bk]])
            mm_tm(bank(7), pT, pan2, 0, 512, 2, [rp2, R_pT], [R_bank[7]])
            sg, r_sg, _ = sgt.next()
            P.op("act", lambda e, sg=sg, bk=bk: e.activation(out=sg[:], in_=bank(bk), func=AF.Sigmoid),
                 r=[R_bank[bk]], w=[r_sg])
            P.op("dve", lambda e, sg=sg: e.tensor_tensor(out=sg[:], in0=sg[:], in1=bank(7), op=ALU.mult),
                 r=[R_bank[7], r_sg], w=[r_sg])
            P.op("dve", lambda e, sg=sg, cb=cb: e.tensor_tensor(out=xt[:, cb * 512:(cb + 1) * 512],
                                                                in0=xt[:, cb * 512:(cb + 1) * 512], in1=sg[:], op=ALU.add),
                 r=[r_sg, R_x], w=[R_x])

        if STG <= 8:
            return
        P.op("act", lambda e: e.activation(out=xsq[:], in_=xt[:], func=AF.Square, accum_out=st1[:, 0:1]),
             r=[R_x], w=[R_xsq, R_st])
        rstd_from_ss(st1[:, 0:1], st1[:, 1:2], D, [R_st, R_const], [R_st])
        gt_, r_gt, k_gt = panels.next()
        gfin = gt_[:, 0:8, :].rearrange("p k c -> p (k c)").bitcast(F32)
        P.dma("sp", lambda e, gfin=gfin: e.dma_start(out=gfin, in_=nfin_d.partition_broadcast(128)), w=[r_gt], key=k_gt)
        P.op("dve", lambda e, gfin=gfin: e.scalar_tensor_tensor(out=xt[:], in0=xt[:], scalar=st1[:, 1:2], in1=gfin,
                                                                op0=ALU.mult, op1=ALU.mult),
             r=[R_x, R_st, r_gt], w=[R_x])
        P.dma("sp", lambda e: e.dma_start(out=y_o[tok0:tok0 + 128, :], in_=xt[:]), r=[R_x], key="yout")

    def hg_elementwise(half, sample):
        h0 = half * 4
        R = R_hg[half]
        qs, sg, ff, kk, cum, E = tA
        rq, rsg, rff, rkk, rcum, rE = R_tA
        mask = m8 if sample else m64
        hsl = slice(h0, h0 + 4)

        def v3(t):
            return t[:].rearrange("p (h t) -> p h t", h=4)

        P.op("act", lambda e: e.activation(out=qs[:], in_=bank(2), func=AF.Silu), r=[R_bank[2]], w=[rq])
        P.op("act", lambda e: e.activation(out=sg[:], in_=bank(3), func=AF.Sigmoid), r=[R_bank[3]], w=[rsg])
        P.op("act", lambda e: e.activation(out=gateT[:, hsl, :].rearrange("p h t -> p (h t)"), in_=bank(4), func=AF.Sigmoid),
             r=[R_bank[4]], w=[R])
        for hh in range(4):
            h = h0 + hh
            P.op("dve", lambda e, hh=hh, h=h: e.tensor_scalar(out=ff[:, hh * 128:(hh + 1) * 128], in0=sg[:, hh * 128:(hh + 1) * 128],
                                                              scalar1=omlT[:, h:h + 1], scalar2=lbT[:, h:h + 1],
                                                              op0=ALU.mult, op1=ALU.add), r=[rsg, R_lb], w=[rff])
            P.op("dve", lambda e, hh=hh, h=h: e.tensor_scalar(out=kk[:, hh * 128:(hh + 1) * 128], in0=sg[:, hh * 128:(hh + 1) * 128],
                                                              scalar1=nomlT[:, h:h + 1], scalar2=omlT[:, h:h + 1],
                                                              op0=ALU.mult, op1=ALU.add), r=[rsg, R_lb], w=[rkk])
        P.op("act", lambda e: e.activation(out=ff[:], in_=ff[:], func=AF.Ln), r=[rff], w=[rff])
        P.op("dve", lambda e: e.tensor_tensor_scan(out=cum[:], data0=mask[:], data1=ff[:], initial=0.0,
                                                   op0=ALU.mult, op1=ALU.add), r=[rff, R_m], w=[rcum])
        P.op("act", lambda e: e.activation(out=Edec[:, hsl, :].rearrange("p h t -> p (h t)"), in_=cum[:], func=AF.Exp),
             r=[rcum], w=[R])
        P.op("act", lambda e: e.activation(out=E[:], in_=cum[:], func=AF.Exp, scale=-1.0), r=[rcum], w=[rE])
        P.op("dve", lambda e: e.tensor_tensor(out=qdT[:, hsl, :].rearrange("p h t -> p (h t)"), in0=qs[:],
                                              in1=Edec[:, hsl, :].rearrange("p h t -> p (h t)"), op=ALU.mult),
             r=[rq, R], w=[R])
        P.op("dve", lambda e: e.tensor_tensor(out=kk[:], in0=kk[:], in1=E[:], op=ALU.mult), r=[rkk, rE], w=[rkk])
        P.op("act", lambda e: e.activation(out=kiT[:, hsl, :].rearrange("p h t -> p (h t)"), in_=kk[:], func=AF.Copy),
             r=[rkk], w=[R])
        seg = 8 if sample else 64
        ns = 512 // seg
        kk3 = kk[:].rearrange("p (s t) -> p s t", t=seg)
        ed3 = Edec[:, hsl, :].rearrange("p h (s t) -> p (h s) t", t=seg)
        P.op("dve", lambda e: e.tensor_tensor(out=keT[:, hsl, :].rearrange("p h (s t) -> p (h s) t", t=seg), in0=kk3,
                                              in1=ed3[:, :, seg - 1:seg].to_broadcast([128, ns, seg]), op=ALU.mult),
             r=[rkk, R], w=[R])

    def mla_latents(T, sample, cm, r_cm, sm, r_sm):
        tok0 = T * 128
        P.op("act", lambda e: e.activation(out=xsq[:, 0:512], in_=bank(6), func=AF.Square, accum_out=st1[:, 2:3]),
             r=[R_bank[6]], w=[R_xsq, R_st])
        rstd_from_ss(st1[:, 2:3], st1[:, 3:4], 512, [R_st, R_const], [R_st])
        P.op("dve", lambda e: e.scalar_tensor_tensor(out=cqn[:], in0=bank(6), scalar=st1[:, 3:4], in1=qnorm_bc[:],
                                                     op0=ALU.mult, op1=ALU.mult), r=[R_bank[6], R_st, R_const], w=[R_cqn])
        for kc in range(4):
            P.op("pe", lambda e, kc=kc: e.transpose(out=T2[:, kc * 128:(kc + 1) * 128], in_=cqn[:, kc * 128:(kc + 1) * 128],
                                                    identity=ident[:]), r=[R_cqn, R_const], w=[R_T])
        P.op("dve", lambda e: e.tensor_copy(out=cqnT[:].rearrange("p k t -> p (k t)"), in_=T2[:, 0:512]),
             r=[R_T], w=[R_cqnT])
        cf, r_cf, k_cf = ckvf.next()
        P.op("act", lambda e: e.activation(out=xsq[:, 512:768], in_=bank(7)[:, 0:256], func=AF.Square, accum_out=st1[:, 4:5]),
             r=[R_bank[7]], w=[R_xsq, R_st])
        rstd_from_ss(st1[:, 4:5], st1[:, 5:6], 256, [R_st, R_const], [R_st])
        P.op("dve", lambda e: e.scalar_tensor_tensor(out=cf[:], in0=bank(7)[:, 0:256], scalar=st1[:, 5:6], in1=kvnorm_bc[:],
                                                     op0=ALU.mult, op1=ALU.mult), r=[R_bank[7], R_st, R_const], w=[r_cf])
        P.dma("sp", lambda e: e.dma_start(out=ckv_o[tok0:tok0 + 128, :], in_=cf[:]), r=[r_cf], key=k_cf)
        cbf = ckvbf_s[:] if sample else ckv_tm[:, T, :]
        RK = R_Ks if sample else R_K
        P.op("act", lambda e: e.activation(out=cbf, in_=cf[:], func=AF.Copy), r=[r_cf], w=[RK])
        kf, r_kf, k_kf = krf.next()
        x1 = bank(7)[:, 256:288]
        x2 = bank(7)[:, 288:320]
        tt = krtmp
        P.op("dve", lambda e: e.tensor_tensor(out=tt[:, 0, :], in0=x1, in1=cm[:], op=ALU.mult), r=[R_bank[7], r_cm], w=[R_krtmp])
        P.op("dve", lambda e: e.tensor_tensor(out=tt[:, 1, :], in0=x2, in1=sm[:], op=ALU.mult), r=[R_bank[7], r_sm], w=[R_krtmp])
        P.op("dve", lambda e: e.tensor_tensor(out=tt[:, 2, :], in0=x2, in1=cm[:], op=ALU.mult), r=[R_bank[7], r_cm], w=[R_krtmp])
        P.op("dve", lambda e: e.tensor_tensor(out=tt[:, 3, :], in0=x1, in1=sm[:], op=ALU.mult), r=[R_bank[7], r_sm], w=[R_krtmp])
        P.op("dve", lambda e: e.tensor_tensor(out=kf[:, 0:32], in0=tt[:, 0, :], in1=tt[:, 1, :], op=ALU.subtract),
             r=[R_krtmp], w=[r_kf])
        P.op("dve", lambda e: e.tensor_tensor(out=kf[:, 32:64], in0=tt[:, 2, :], in1=tt[:, 3, :], op=ALU.add),
             r=[R_krtmp], w=[r_kf])
        P.dma("sp", lambda e: e.dma_start(out=kro_o[tok0:tok0 + 128, :], in_=kf[:]), r=[r_kf], key=k_kf)
        P.op("act", lambda e: e.activation(out=krbf[:], in_=kf[:], func=AF.Copy), r=[r_kf], w=[R_krtmp])
        for kc in range(2):
            P.op("pe", lambda e, kc=kc: e.transpose(out=T2[:, kc * 128:(kc + 1) * 128], in_=cbf[:, kc * 128:(kc + 1) * 128],
                                                    identity=ident[:]), r=[RK, R_const], w=[R_T])
        P.op("pe", lambda e: e.transpose(out=T2[0:64, 256:384], in_=krbf[:], identity=ident[:]), r=[R_krtmp, R_const], w=[R_T])
        Kd = KaugT_s if sample else KaugT
        c0 = 0 if sample else tok0
        P.op("dve", lambda e: e.tensor_copy(out=Kd[:, 0:2, c0:c0 + 128], in_=T2[:, 0:256].rearrange("p (k t) -> p k t", k=2)),
             r=[R_T], w=[RK])
        P.op("act", lambda e: e.activation(out=Kd[0:64, 2, c0:c0 + 128], in_=T2[0:64, 256:384], func=AF.Copy),
             r=[R_T], w=[RK])

    def mla_q1(h, cT, r_cT, sT, r_sT, Qdst, RQ, wqp, r_wqp):
        hb = (h % 2) * 256
        for kc in range(4):
            P.op("pe", lambda e, kc=kc: e.matmul(bank(6)[:, 0:128], lhsT=wqp[:, kc, hb:hb + 128], rhs=cqnT[:, kc, :],
                                                 start=(kc == 0), stop=(kc == 3)), r=[r_wqp, R_cqnT], w=[R_bank[6]])
        for kc in range(4):
            P.op("pe", lambda e, kc=kc: e.matmul(bank(6)[0:64, 128:256], lhsT=wqp[:, kc, hb + 128:hb + 192], rhs=cqnT[:, kc, :],
                                                 start=(kc == 0), stop=(kc == 3)), r=[r_wqp, R_cqnT], w=[R_bank[6]])
        for kc in range(4):
            P.op("pe", lambda e, kc=kc: e.matmul(bank(6)[0:64, 256:384], lhsT=wqp[:, kc, hb + 192:hb + 256], rhs=cqnT[:, kc, :],
                                                 start=(kc == 0), stop=(kc == 3)), r=[r_wqp, R_cqnT], w=[R_bank[6]])
        P.op("act", lambda e: e.activation(out=qnT[:], in_=bank(6)[:, 0:128], func=AF.Copy), r=[R_bank[6]], w=[R_qnT])
        P.op("dve", lambda e: e.tensor_tensor(out=qtmp[:, 0, :], in0=bank(6)[0:64, 128:256], in1=cT[:], op=ALU.mult),
             r=[R_bank[6], r_cT], w=[R_qtmp])
        P.op("dve", lambda e: e.tensor_tensor(out=qtmp[:, 1, :], in0=bank(6)[0:64, 256:384], in1=sT[:], op=ALU.mult),
             r=[R_bank[6], r_sT], w=[R_qtmp])
        P.op("dve", lambda e: e.tensor_tensor(out=Qdst[0:64, 2, :], in0=qtmp[:, 0, :], in1=qtmp[:, 1, :], op=ALU.add),
             r=[R_qtmp], w=[RQ])

    def mla_q2(h, Qdst, RQ):
        for ck in range(2):
            P.op("pe", lambda e, ck=ck: e.matmul(bank(7)[:, ck * 128:(ck + 1) * 128], lhsT=wukT[:, h, ck * 128:(ck + 1) * 128],
                                                 rhs=qnT[:], start=True, stop=True), r=[R_wukT, R_qnT], w=[R_bank[7]])
        P.op("act", lambda e: e.activation(out=Qdst[:, 0:2, :], in_=bank(7)[:, 0:256].rearrange("p (k t) -> p k t", k=2),
                                           func=AF.Copy, scale=SCALE), r=[R_bank[7]], w=[RQ])

    def mla_q(h, cT, r_cT, sT, r_sT, Qdst, RQ, wqp, r_wqp):
        mla_q1(h, cT, r_cT, sT, r_sT, Qdst, RQ, wqp, r_wqp)
        mla_q2(h, Qdst, RQ)

    def pa_S(T, h, Qs):
        nk = (T + 1) * 128
        ngrp = (nk + 511) // 512
        Sreg = ps[:, 1024:1024 + 2048]
        RS = [R_bank[2], R_bank[3], R_bank[4], R_bank[5]]
        Qt, r_Q = Qs
        for g in range(ngrp):
            k0 = g * 512
            kw = min(512, nk - k0)
            isdiag = (g == ngrp - 1)
            for c in range(3):
                rows = 128 if c < 2 else 64
                P.op("pe", lambda e, c=c, rows=rows, k0=k0, kw=kw, isdiag=isdiag: e.matmul(
                    Sreg[:, k0:k0 + kw], lhsT=Qt[0:rows, c, :], rhs=KaugT[0:rows, c, k0:k0 + kw],
                    start=(c == 0), stop=(c == 2 and not isdiag)), r=[r_Q, R_K], w=[RS[g]])
            if isdiag:
                P.op("pe", lambda e: e.matmul(Sreg[:, T * 128:(T + 1) * 128], lhsT=ident[:], rhs=cmask[:],
                                              start=False, stop=True), r=[R_const], w=[RS[g]])

    def pa_softmax(T, h):
        nk = (T + 1) * 128
        ngrp = (nk + 511) // 512
        Sreg = ps[:, 1024:1024 + 2048]
        used = [R_bank[2], R_bank[3], R_bank[4], R_bank[5]][:ngrp]
        P.op("dve", lambda e: e.tensor_reduce(out=st2[:, 0:1], in_=Sreg[:, 0:nk], axis=AX.X, op=ALU.max),
             r=used, w=[R_st2])
        P.op("dve", lambda e: e.tensor_scalar(out=st2[:, 0:1], in0=st2[:, 0:1], scalar1=-1.0, scalar2=None, op0=ALU.mult),
             r=[R_st2], w=[R_st2])
        P.op("act", lambda e: e.activation(out=Pf[:, 0:nk], in_=Sreg[:, 0:nk], func=AF.Exp, bias=st2[:, 0:1],
                                           accum_out=st2[:, 1:2]), r=used + [R_st2], w=[R_Pf, R_st2])
        P.op("dve", lambda e: e.reciprocal(out=st2[:, 1:2], in_=st2[:, 1:2]), r=[R_st2], w=[R_st2])
        P.op("act", lambda e: e.activation(out=Pn[:, 0:nk], in_=Pf[:, 0:nk], func=AF.Copy, scale=st2[:, 1:2]),
             r=[R_Pf, R_st2], w=[R_Pn])

    def pa_PT(T, h):
        nk = (T + 1) * 128
        for b in range(T + 1):
            P.op("pe", lambda e, b=b: e.transpose(out=T2[:, b * 128:(b + 1) * 128], in_=Pn[:, b * 128:(b + 1) * 128],
                                                  identity=ident[:]), r=[R_Pn, R_const], w=[R_T])
        P.op("dve", lambda e: e.tensor_copy(out=PT[:, 0:T + 1, :].rearrange("p b q -> p (b q)"), in_=T2[:, 0:nk]),
             r=[R_T], w=[R_PT])

    def pa_rest(T, h):
        for ck in range(2):
            for b in range(T + 1):
                P.op("pe", lambda e, ck=ck, b=b: e.matmul(bank(6)[:, ck * 128:(ck + 1) * 128],
                                                          lhsT=ckv_tm[:, b, ck * 128:(ck + 1) * 128], rhs=PT[:, b, :],
                                                          start=(b == 0), stop=(b == T)), r=[R_K, R_PT], w=[R_bank[6]])
        P.op("act", lambda e: e.activation(out=latT[:].rearrange("p k t -> p (k t)"), in_=bank(6)[:, 0:256], func=AF.Copy),
             r=[R_bank[6]], w=[R_latT])
        for ck in range(2):
            P.op("pe", lambda e, ck=ck: e.matmul(bank(7)[:, 0:128], lhsT=wuv[:, ck, h, :], rhs=latT[:, ck, :],
                                                 start=(ck == 0), stop=(ck == 1)), r=[R_wq, R_latT], w=[R_bank[7]])
        P.op("dve", lambda e: e.tensor_copy(out=mixT[:, 8 + h, :], in_=bank(7)[:, 0:128]), r=[R_bank[7]], w=[R_mix])

    def hg_norm_gate(h, oT, r_oT):
        Rh = R_hg[h // 4]
        P.op("act", lambda e: e.activation(out=osq[:], in_=oT, func=AF.Square), r=[r_oT], w=[R_osq])
        P.op("pe", lambda e: e.matmul(bank(2)[:, 0:128], lhsT=ones[:], rhs=osq[:], start=True, stop=True),
             r=[R_osq, R_const], w=[R_bank[2]])
        P.op("act", lambda e: e.activation(out=orst[:], in_=bank(2)[:, 0:128], func=AF.Ln, scale=1.0 / 128, bias=epsb[:, 0:1]),
             r=[R_bank[2], R_const], w=[R_orst])
        P.op("act", lambda e: e.activation(out=orst[:], in_=orst[:], func=AF.Exp, scale=-0.5), r=[R_orst], w=[R_orst])
        P.op("dve", lambda e: e.tensor_tensor(out=otmp[:], in0=oT, in1=orst[:], op=ALU.mult),
             r=[r_oT, R_orst], w=[R_otmp])
        P.op("dve", lambda e: e.scalar_tensor_tensor(out=mixT[:, h, :], in0=otmp[:], scalar=onormT[:, h:h + 1],
                                                     in1=gateT[:, h, :], op0=ALU.mult, op1=ALU.mult),
             r=[R_otmp, R_const, Rh], w=[R_mix])

    def hg_recurrence(T, sample):
        bd = bd8 if sample else bd64
        for h in range(H):
            P.op("pe", lambda e, h=h: e.transpose(out=T2[:, h * 128:(h + 1) * 128], in_=keT[:, h, :], identity=ident[:]),
                 r=[R_hg[h // 4], R_const], w=[R_T])
        P.op("dve", lambda e: e.tensor_copy(out=ketm[:].rearrange("p h k -> p (h k)"), in_=T2[:, 0:1024]),
             r=[R_T], w=[R_ketm])
        if not sample:
            for h in range(H):
                Rh = R_hg[h // 4]
                vh = vtm[:, h * 128:(h + 1) * 128]
                P.op("pe", lambda e, h=h: e.matmul(bank(5)[:, 0:128], lhsT=kiT[:, h, :], rhs=qdT[:, h, :], start=True, stop=True),
                     r=[Rh], w=[R_bank[5]])
                P.op("dve", lambda e: e.tensor_tensor(out=ATm[:], in0=bank(5)[:, 0:128], in1=bd[:], op=ALU.mult),
                     r=[R_bank[5], R_const], w=[R_ATm])
                P.op("pe", lambda e, vh=vh: e.matmul(bank(6)[:, 0:128], lhsT=vh, rhs=ATm[:], start=True, stop=False),
                     r=[R_v, R_ATm], w=[R_bank[6]])
                for c in range(2):
                    cs = slice(c * 64, (c + 1) * 64)
                    P.op("pe", lambda e, h=h, cs=cs, c=c: e.matmul(bank(6)[:, cs], lhsT=Sbf[:, h, :], rhs=qdT[:, h, cs],
                                                                   start=False, stop=(c == 1)), r=[R_S[h], Rh], w=[R_bank[6]])
                    P.op("pe", lambda e, h=h, cs=cs, vh=vh: e.matmul(bank(7)[:, 0:128], lhsT=ketm[cs, h, :], rhs=vh[cs, :],
                                                                     start=True, stop=True), r=[R_ketm, R_v], w=[R_bank[7]])
                    dcol = Edec[:, h, c * 64 + 63:c * 64 + 64]
                    P.op("dve", lambda e, h=h, dcol=dcol: e.scalar_tensor_tensor(out=Sst[:, h, :], in0=Sst[:, h, :], scalar=dcol,
                                                                                 in1=bank(7)[:, 0:128], op0=ALU.mult, op1=ALU.add),
                         r=[R_bank[7], Rh, R_S[h]], w=[R_S[h]])
                    P.op("act", lambda e, h=h: e.activation(out=Sbf[:, h, :], in_=Sst[:, h, :], func=AF.Copy),
                         r=[R_S[h]], w=[R_S[h]])
                hg_norm_gate(h, bank(6)[:, 0:128], R_bank[6])
            if T == 15:
                P.dma("sp", lambda e: e.dma_start(out=hgp_o.rearrange("h k v -> k h v"), in_=Sst[:]), r=R_S, key="hgp")
            return
        def oTap(h):
            return bank(3 + h // 4)[:, (h % 4) * 128:(h % 4 + 1) * 128]
        for h in range(H):
            Rh = R_hg[h // 4]
            vh = vtm[:, h * 128:(h + 1) * 128]
            P.op("pe", lambda e, h=h: e.matmul(bank(5)[:, 0:128], lhsT=kiT[:, h, :], rhs=qdT[:, h, :], start=True, stop=True),
                 r=[Rh], w=[R_bank[5]])
            P.op("dve", lambda e: e.tensor_tensor(out=ATm[:], in0=bank(5)[:, 0:128], in1=bd[:], op=ALU.mult),
                 r=[R_bank[5], R_const], w=[R_ATm])
            P.op("pe", lambda e, vh=vh, h=h: e.matmul(oTap(h), lhsT=vh, rhs=ATm[:], start=(h % 4 == 0), stop=False,
                                                      skip_group_check=True),
                 r=[R_v, R_ATm], w=[R_bank[3 + h // 4]])
        for b in range(NB):
            sl, r_sl, k_sl = Sload.next()
            slb, r_slb, _ = Sloadbf.next()
            sn, r_sn, k_sn = sl, r_sl, k_sl + "o"
            P.dma("sp", lambda e, sl=sl, b=b: e.dma_start(out=sl[:], in_=sth_d[b].rearrange("h k v -> k h v")), w=[r_sl], key=k_sl)
            P.op("act", lambda e, sl=sl, slb=slb: e.activation(out=slb[:].rearrange("p h v -> p (h v)"),
                                                               in_=sl[:].rearrange("p h v -> p (h v)"), func=AF.Copy),
                 r=[r_sl], w=[r_slb])
            for h in range(H):
                Rh = R_hg[h // 4]
                vh = vtm[:, h * 128:(h + 1) * 128]
                cs = slice(b * 8, (b + 1) * 8)
                P.op("pe", lambda e, h=h, cs=cs, slb=slb, b=b: e.matmul(oTap(h)[:, cs], lhsT=slb[:, h, :], rhs=qdT[:, h, cs],
                                                                        start=False, stop=(b == NB - 1), skip_group_check=True),
                     r=[r_slb, Rh], w=[R_bank[3 + h // 4]])
                km, r_km, _ = kemr.next()
                P.op("dve" if h % 2 == 0 else "pool",
                     lambda e, h=h, b=b, km=km: e.tensor_scalar(out=km[:], in0=ketm[:, h, :], scalar1=bmask[:, b:b + 1],
                                                                scalar2=None, op0=ALU.mult), r=[R_ketm, R_const], w=[r_km])
                bk = 6 + (h % 2)
                P.op("pe", lambda e, km=km, vh=vh, bk=bk: e.matmul(bank(bk)[:, 0:128], lhsT=km[:], rhs=vh, start=True, stop=True),
                     r=[r_km, R_v], w=[R_bank[bk]])
                dcol = Edec[:, h, b * 8 + 7:b * 8 + 8]
                P.op("dve", lambda e, h=h, dcol=dcol, sl=sl, sn=sn, bk=bk: e.scalar_tensor_tensor(
                    out=sn[:, h, :], in0=sl[:, h, :], scalar=dcol, in1=bank(bk)[:, 0:128], op0=ALU.mult, op1=ALU.add),
                    r=[R_bank[bk], Rh, r_sl], w=[r_sn])
            P.dma("sp", lambda e, sn=sn, b=b: e.dma_start(out=hgs_o[b].rearrange("h k v -> k h v"), in_=sn[:]), r=[r_sn], key=k_sn)
        for h in range(H):
            hg_norm_gate(h, oTap(h), R_bank[3 + h // 4])

    kiT = sb("kiT", [128, H, 128], BF16)
    kemr = Rot("kemr", [128, 128], BF16, 3)

    ae4 = sb("ae4", [128, 4, 130], F32)
    ca4 = sb("ca4", [128, 4, 128], F32)
    cb4 = sb("cb4", [128, 4, 128], F32)
    R_ae4 = Res("ae4")
    R_ca4 = Res("ca4")
    R_cb4 = Res("cb4")

    def ffn_group(pa, bA, bG):
        c0 = pa * 4
        P.op("act", lambda e: e.activation(out=ae4[:, :, 0:2], in_=convc[:, c0:c0 + 4, :], func=AF.Copy), r=[R_convc], w=[R_ae4])
        P.op("act", lambda e: e.activation(out=ae4[:, :, 2:130], in_=bank(bA).rearrange("p (c t) -> p c t", c=4), func=AF.Copy),
             r=[R_bank[bA]], w=[R_ae4])
        P.op("act", lambda e: e.activation(out=convc[:, c0:c0 + 4, :], in_=ae4[:, :, 128:130], func=AF.Copy), r=[R_ae4], w=[R_convc])
        for j in range(4):
            ci = c0 + j
            P.op("dve", lambda e, j=j, ci=ci: e.tensor_scalar(out=ca4[:, j, :], in0=ae4[:, j, 0:128], scalar1=cwT[:, 0, ci:ci + 1],
                                                              scalar2=cbT[:, ci:ci + 1], op0=ALU.mult, op1=ALU.add),
                 r=[R_ae4, R_const], w=[R_ca4])
            P.op("dve", lambda e, j=j, ci=ci: e.scalar_tensor_tensor(out=cb4[:, j, :], in0=ae4[:, j, 1:129], scalar=cwT[:, 1, ci:ci + 1],
                                                                     in1=ca4[:, j, :], op0=ALU.mult, op1=ALU.add),
                 r=[R_ae4, R_ca4, R_const], w=[R_cb4])
            P.op("dve", lambda e, j=j, ci=ci: e.scalar_tensor_tensor(out=ca4[:, j, :], in0=ae4[:, j, 2:130], scalar=cwT[:, 2, ci:ci + 1],
                                                                     in1=cb4[:, j, :], op0=ALU.mult, op1=ALU.add),
                 r=[R_ae4, R_cb4, R_const], w=[R_ca4])
        P.op("act", lambda e: e.activation(out=cb4[:].rearrange("p c t -> p (c t)"), in_=ca4[:].rearrange("p c t -> p (c t)"), func=AF.Silu),
             r=[R_ca4], w=[R_cb4])
        P.op("dve", lambda e: e.tensor_tensor(out=hT[:, c0:c0 + 4, :].rearrange("p c t -> p (c t)"), in0=cb4[:].rearrange("p c t -> p (c t)"),
                                              in1=bank(bG), op=ALU.mult), r=[R_cb4, R_bank[bG]], w=[R_hT])

    def ffn_chunk(ci, a_ap, r_a, g_ap, r_g, sample):
        ae, r_ae, _ = aext.next()
        ca, r_ca, _ = cva.next()
        cb_, r_cb, _ = cvb.next()
        if not sample:
            P.op("act", lambda e: e.activation(out=ae[:, 0:2], in_=convc[:, ci, :], func=AF.Copy), r=[R_convc], w=[r_ae])
            P.op("act", lambda e: e.activation(out=ae[:, 2:130], in_=a_ap, func=AF.Copy), r=[r_a], w=[r_ae])
            P.op("act", lambda e: e.activation(out=convc[:, ci, :], in_=ae[:, 128:130], func=AF.Copy), r=[r_ae], w=[R_convc])
            s0, s1, s2 = ae[:, 0:128], ae[:, 1:129], ae[:, 2:130]
            o1, o2 = ca[:], cb_[:]
        else:
            ae3 = ae[:].rearrange("p (b t) -> p b t", t=10)
            P.op("act", lambda e: e.activation(out=ae3[:, :, 0:2], in_=sconvT[:, ci, :].rearrange("p (b j) -> p b j", j=2),
                                               func=AF.Copy), r=[R_sconv], w=[r_ae])
            P.op("act", lambda e: e.activation(out=ae3[:, :, 2:10], in_=a_ap.rearrange("p (b t) -> p b t", t=8),
                                               func=AF.Copy), r=[r_a], w=[r_ae])
            s0, s1, s2 = ae3[:, :, 0:8], ae3[:, :, 1:9], ae3[:, :, 2:10]
            o1 = ca[:].rearrange("p (b t) -> p b t", t=8)
            o2 = cb_[:].rearrange("p (b t) -> p b t", t=8)
        P.op("dve", lambda e: e.tensor_scalar(out=o1, in0=s0, scalar1=cwT[:, 0, ci:ci + 1], scalar2=cbT[:, ci:ci + 1],
                                              op0=ALU.mult, op1=ALU.add), r=[r_ae, R_const], w=[r_ca])
        P.op("dve", lambda e: e.scalar_tensor_tensor(out=o2, in0=s1, scalar=cwT[:, 1, ci:ci + 1], in1=o1,
                                                     op0=ALU.mult, op1=ALU.add), r=[r_ae, r_ca, R_const], w=[r_cb])
        P.op("dve", lambda e: e.scalar_tensor_tensor(out=o1, in0=s2, scalar=cwT[:, 2, ci:ci + 1], in1=o2,
                                                     op0=ALU.mult, op1=ALU.add), r=[r_ae, r_cb, R_const], w=[r_ca])
        P.op("act", lambda e: e.activation(out=cb_[:], in_=ca[:], func=AF.Silu), r=[r_ca], w=[r_cb])
        P.op("dve", lambda e: e.tensor_tensor(out=hT[:, ci, :], in0=cb_[:], in1=g_ap, op=ALU.mult),
             r=[r_cb, r_g], w=[R_hT])

    R_negm = Res("negm")
    R_T67 = Res("T67", excl=True)
    R_l = Res("l_run")
    R_par = [Res("par0"), Res("par1")]

    def sample_attention():
        NGT = NGRP + 1
        TR = (bankbf(0, 2), bankbf(6, 2))
        R_TR = (R_T, R_T67)
        for b in range(NB):
            Qb = QaugT_all[:, :, b, :]
            P.op("dve", lambda e: e.memset(accs[:], 0.0), w=[R_acc])
            P.op("dve", lambda e: e.memset(fst[:, 0:2], 1e30), w=[R_negm])
            P.op("dve", lambda e: e.memset(fst[:, 2:3], 0.0), w=[R_l])
            gath = {}

            def issue_gather(G, b=b, gath=gath):
                gc, r_gc, k_gc = Gc.next()
                gr, r_gr, k_gr = Gr.next()
                col = b * 16 + G
                P.dma("pool", lambda e, gc=gc, col=col: e.indirect_dma_start(
                    out=gc[:].rearrange("p s c -> p (s c)"), out_offset=None, in_=ck_d,
                    in_offset=bass.IndirectOffsetOnAxis(ap=idx[:, col:col + 1], axis=0)), r=[R_idx], w=[r_gc], key=k_gc)
                P.dma("pool", lambda e, gr=gr, col=col: e.indirect_dma_start(
                    out=gr[:].rearrange("p s c -> p (s c)"), out_offset=None, in_=kr_d,
                    in_offset=bass.IndirectOffsetOnAxis(ap=idx[:, col:col + 1], axis=0)), r=[R_idx], w=[r_gr], key=k_gr)
                gath[G] = (gc, r_gc, gr, r_gr)

            issue_gather(0)
            info = {}
            for it in range(NGT + 2):
                g = it
                if g % 2 == 0 and g // 2 + 1 < 16:
                    issue_gather(g // 2 + 1)
                if g < NGT:
                    selfg = (g == NGRP)
                    bS = 2 + (g % 2)
                    if not selfg:
                        gc, r_gc, gr, r_gr = gath[g // 2]
                        so = 4 * (g % 2)
                        kt, r_kt, _ = KT.next()
                        Tg = TR[g % 2]
                        r_Tg = R_TR[g % 2]
                        for s_ in range(4):
                            for c in range(2):
                                P.op("pe", lambda e, s_=s_, c=c, gc=gc, Tg=Tg, so=so: e.transpose(
                                    out=Tg[:, c * 512 + s_ * 128:c * 512 + (s_ + 1) * 128], in_=gc[:, so + s_, c * 128:(c + 1) * 128],
                                    identity=ident[:]), r=[r_gc, R_const], w=[r_Tg])
                            P.op("pe", lambda e, s_=s_, gr=gr, Tg=Tg, so=so: e.transpose(out=Tg[0:64, 1024 + s_ * 128:1024 + (s_ + 1) * 128],
                                                                                         in_=gr[:, so + s_, :], identity=ident[:]), r=[r_gr, R_const], w=[r_Tg])
                        P.op("dve", lambda e, kt=kt, Tg=Tg: e.tensor_copy(out=kt[:, 0:2, :].rearrange("p c k -> p (c k)"), in_=Tg[:, 0:1024]),
                             r=[r_Tg], w=[r_kt])
                        P.op("act", lambda e, kt=kt, Tg=Tg: e.activation(out=kt[0:64, 2, :], in_=Tg[0:64, 1024:1536], func=AF.Copy),
                             r=[r_Tg], w=[r_kt])
                        nkeys, ktv, r_ktv, nsl = 512, kt, r_kt, 4
                        vfun = (lambda s_, gc=gc, so=so: gc[:, so + s_, :])
                        r_v = r_gc
                    else:
                        nkeys, ktv, r_ktv, nsl = 128, KaugT_s, R_Ks, 1
                        vfun = (lambda s_: ckvbf_s[:])
                        r_v = R_Ks
                    for c in range(3):
                        rows = 128 if c < 2 else 64
                        P.op("pe", lambda e, c=c, rows=rows, ktv=ktv, bS=bS, nkeys=nkeys, selfg=selfg, Qb=Qb: e.matmul(
                            bank(bS)[0:64, 0:nkeys], lhsT=Qb[0:rows, c], rhs=ktv[0:rows, c, 0:nkeys],
                            start=(c == 0), stop=(c == 2 and not selfg)), r=[R_Qall, r_ktv], w=[R_bank[bS]])
                    if selfg:
                        P.op("pe", lambda e, bS=bS, b=b: e.matmul(bank(bS)[0:64, 0:128], lhsT=ident[0:64, 0:64],
                                                                  rhs=smask[:, 120 - 8 * b:248 - 8 * b], start=False, stop=True),
                             r=[R_const], w=[R_bank[bS]])
                    info[g] = [bS, nkeys, vfun, r_v, nsl, None, None]
                g = it - 1
                if 0 <= g < NGT:
                    bS, nkeys, vfun, r_v, nsl, _, _ = info[g]
                    par = g % 2
                    Rp = R_par[par]
                    old = fst[:, (g + 1) % 2:(g + 1) % 2 + 1]
                    new_ = fst[:, g % 2:g % 2 + 1]
                    pb, r_pb, _ = Pbs.next()
                    P.op("dve", lambda e, pb=pb, bS=bS, nkeys=nkeys, old=old, new_=new_: e.tensor_scalar(
                        out=pb[:, 0:nkeys], in0=bank(bS)[0:64, 0:nkeys], scalar1=-1.0, scalar2=old,
                        op0=ALU.mult, op1=ALU.min, accum_out=new_), r=[R_bank[bS], R_negm], w=[r_pb, R_negm])
                    corr = fst[:, 3 + par:4 + par]
                    rsum = fst[:, 5 + par:6 + par]
                    P.op("act", lambda e, corr=corr, old=old, new_=new_: e.activation(out=corr, in_=old, func=AF.Exp, scale=-1.0, bias=new_),
                         r=[R_negm], w=[Rp])
                    P.op("act", lambda e, pb=pb, bS=bS, nkeys=nkeys, new_=new_, rsum=rsum: e.activation(
                        out=pb[:, 0:nkeys], in_=bank(bS)[0:64, 0:nkeys], func=AF.Exp, bias=new_, accum_out=rsum),
                        r=[R_bank[bS], R_negm], w=[r_pb, Rp])
                    P.op("dve", lambda e, corr=corr, rsum=rsum: e.scalar_tensor_tensor(out=fst[:, 2:3], in0=fst[:, 2:3], scalar=corr, in1=rsum,
                                                                                     op0=ALU.mult, op1=ALU.add), r=[Rp, R_l], w=[R_l])
                    pt_, r_pt, _ = PTs.next()
                    po = par * 256
                    for s_ in range(nsl):
                        P.op("pe", lambda e, s_=s_, pb=pb, po=po: e.transpose(out=bankbf(4)[:, po + s_ * 64:po + (s_ + 1) * 64], in_=pb[:, s_ * 128:(s_ + 1) * 128],
                                                                              identity=ident[0:64, 0:64]), r=[r_pb, R_const], w=[R_bank[4]])
                    P.op("dve", lambda e, pt_=pt_, nsl=nsl, po=po: e.tensor_copy(out=pt_[:, 0:nsl, :].rearrange("p s q -> p (s q)"),
                                                                                 in_=bankbf(4)[:, po:po + nsl * 64]), r=[R_bank[4]], w=[r_pt])
                    info[g][5] = pt_
                    info[g][6] = r_pt
                g = it - 2
                if 0 <= g < NGT:
                    bS, nkeys, vfun, r_v, nsl, pt_, r_pt = info.pop(g)
                    par = g % 2
                    vo = par * 256
                    corr = fst[:, 3 + par:4 + par]
                    for s_ in range(nsl):
                        P.op("pe", lambda e, s_=s_, pt_=pt_, vfun=vfun, nsl=nsl, vo=vo: e.matmul(
                            bank(5)[0:64, vo:vo + 256], lhsT=pt_[:, s_, :], rhs=vfun(s_), start=(s_ == 0), stop=(s_ == nsl - 1)),
                            r=[r_pt, r_v], w=[R_bank[5]])
                    P.op("dve", lambda e, corr=corr, vo=vo: e.scalar_tensor_tensor(out=accs[:], in0=accs[:], scalar=corr, in1=bank(5)[0:64, vo:vo + 256],
                                                                                 op0=ALU.mult, op1=ALU.add), r=[R_bank[5], R_par[par], R_acc], w=[R_acc])
            P.op("dve", lambda e: e.reciprocal(out=fst[:, 7:8], in_=fst[:, 2:3]), r=[R_l], w=[R_fst])
            P.op("act", lambda e: e.activation(out=lat_s[:], in_=accs[:], func=AF.Copy, scale=fst[:, 7:8]),
                 r=[R_acc, R_fst], w=[R_lats])
            for ck in range(2):
                P.op("pe", lambda e, ck=ck: e.transpose(out=bankbf(4)[:, 512 + ck * 64:512 + (ck + 1) * 64],
                                                        in_=lat_s[:, ck * 128:(ck + 1) * 128], identity=ident[0:64, 0:64]),
                     r=[R_lats, R_const], w=[R_bank[4]])
            for ck in range(2):
                P.op("dve", lambda e, b=b, ck=ck: e.tensor_copy(
                    out=latT_s[:, ck, :, b * 8:(b + 1) * 8],
                    in_=bankbf(4)[:, 512 + ck * 64:512 + (ck + 1) * 64].rearrange("p (h t) -> p h t", h=H)),
                    r=[R_bank[4]], w=[R_latTs])
        for h in range(H):
            for ck in range(2):
                P.op("pe", lambda e, ck=ck, h=h: e.matmul(bank(7)[:, 0:128], lhsT=wuv[:, ck, h, :], rhs=latT_s[:, ck, h, :],
                                                          start=(ck == 0), stop=(ck == 1)), r=[R_wq, R_latTs], w=[R_bank[7], R_T67])
            P.op("dve", lambda e, h=h: e.tensor_copy(out=mixT[:, 8 + h, :], in_=bank(7)[:, 0:128]), r=[R_bank[7], R_T67], w=[R_mix])

    bar = sb("bar", [128, 8], F32)

    def barrier():
        P.op("dve", lambda e: e.memset(bar[:], 0.0), w=list(Res.ALL))

    for T in range(min(nblk_run, 16)):
        block(T)
    if do_sample:
        barrier()
        panq[1] = True
        P.dma("sp", lambda e: e.dma_start(out=m8[:], in_=m8_d), w=[R_m], key="m8")
        for q in range(8):
            P.dma("sp", lambda e, q=q: e.dma_start(out=ptb[16 * q:16 * q + 16, :],
                                                   in_=ptq_d[q].partition_broadcast(16)), w=[R_idx], key="ptb")
        P.op("dve", lambda e: e.tensor_scalar(out=idx[:], in0=ptb[:], scalar1=16.0, scalar2=rcol[:, 0:1],
                                              op0=ALU.mult, op1=ALU.add), r=[R_idx, R_const], w=[R_idx])
        block(16)

    P.finalize()
    global LAST_PROG
    LAST_PROG = P
    with nc.Block() as blk:
        @blk.sync
        def _(e):
            P.emit("sp", e, final_waits=True)

        @blk.tensor
        def _(e):
            P.emit("pe", e)

        @blk.scalar
        def _(e):
            P.emit("act", e)

        @blk.vector
        def _(e):
            P.emit("dve", e)

        @blk.gpsimd
        def _(e):
            P.emit("pool", e)
    es.close()
    return nc


def _consts():
    bf = ml_dtypes.bfloat16
    c = {}
    c["c_ident"] = np.eye(128, dtype=np.float32).astype(bf)
    c["c_identf"] = np.eye(128, dtype=np.float32)
    c["c_ones"] = np.ones((128, 128), np.float32).astype(bf)
    q = np.arange(128)[:, None]
    k = np.arange(128)[None, :]
    c["c_cmask"] = np.where(k <= q, 0.0, NEG).astype(np.float32).astype(bf)
    s = np.arange(128)[:, None]
    t = np.arange(128)[None, :]
    c["c_bd64"] = ((s // 64 == t // 64) & (s <= t)).astype(np.float32).astype(bf)
    c["c_bd8"] = ((s // 8 == t // 8) & (s <= t)).astype(np.float32).astype(bf)
    j = np.arange(512)
    c["c_m64"] = np.broadcast_to((j % 64 != 0).astype(np.float32), (128, 512)).copy()
    c["c_m8"] = np.broadcast_to((j % 8 != 0).astype(np.float32), (128, 512)).copy()
    c["c_bmask"] = (np.arange(128)[:, None] // 8 == np.arange(NB)[None, :]).astype(np.float32)
    sm = np.full((64, 248), NEG, np.float32)
    qt = np.arange(64) % 8
    for kk in range(8):
        sm[:, 120 + kk] = np.where(kk <= qt, 0.0, NEG)
    c["c_smask"] = sm.astype(bf)
    c["c_rcol"] = (np.arange(128) % 16).astype(np.float32)[:, None]
    half = 32
    inv = (1.0 / (np.float32(10000.0) ** (np.arange(half, dtype=np.float32) / np.float32(half)))).astype(np.float32)
    pos = np.concatenate([np.arange(NTOK_P), np.tile(16384 + np.arange(8), NB)]).astype(np.float32)
    ang = (pos[:, None] * inv[None, :]).astype(np.float32)
    cs = np.cos(ang).astype(np.float32)
    sn = np.sin(ang).astype(np.float32)
    c["c_cosm"] = cs
    c["c_sinm"] = sn
    c["c_cosT"] = (np.concatenate([cs, cs], axis=1).T * np.float32(SCALE)).astype(np.float32).copy()
    c["c_sinT"] = (np.concatenate([-sn, sn], axis=1).T * np.float32(SCALE)).astype(np.float32).copy()
    return c


def make_in_map(inp, c, consts):
    f = np.ascontiguousarray
    npool = inp["cache_ckv"].shape[1]
    pt = inp["page_table"][NB * c:NB * (c + 1)]
    ptq = f(pt.reshape(NB, 16, 8).transpose(2, 0, 1).reshape(8, NB * 16)).astype(np.int32)
    m = {
        "x": f(np.concatenate([inp["x_prompt"][c], inp["x_sample"][NB * c:NB * (c + 1)].reshape(128, D)], axis=0)),
        "pl": f(np.concatenate([inp["p_prompt"][0, c], inp["p_sample"][0, NB * c:NB * (c + 1)].reshape(128, 256)], axis=0)),
        "cache_ckv": inp["cache_ckv"][0].reshape(npool * 16, 2048),
        "cache_krope": inp["cache_krope"][0].reshape(npool * 16, 512),
        "ptq": ptq,
        "state_hgrn": f(inp["state_hgrn"][0, NB * c:NB * (c + 1)]),
        "state_conv": f(inp["state_conv"][0, NB * c:NB * (c + 1)].reshape(NB * 2, DFF)),
        "w_in": inp["w_in"][0], "w_q_b": inp["w_q_b"][0], "w_kv_b": inp["w_kv_b"][0], "w_out": inp["w_out"][0],
        "w_up": inp["w_up"][0], "w_down": inp["w_down"][0], "w_ple_gate": inp["w_ple_gate"][0],
        "w_ple_proj": inp["w_ple_proj"][0],
        "norm_mix": inp["norm_mix"][0], "norm_ffn": inp["norm_ffn"][0], "norm_ple": inp["norm_ple"][0],
        "norm_final": inp["norm_final"], "hg_lower": inp["hg_lower"], "hg_onorm": inp["hg_onorm"][0],
        "mla_q_norm": inp["mla_q_norm"][0], "mla_kv_norm": inp["mla_kv_norm"][0],
        "conv_w": inp["conv_w"][0], "conv_b": inp["conv_b"][0],
    }
    m.update(consts)
    return m


def assemble(results, ncores):
    y = np.stack([r["y"] for r in results])
    ckv = np.stack([r["ckv_o"] for r in results])
    kr = np.stack([r["kr_o"] for r in results])
    y_prompt = y[:, :NTOK_P]
    y_sample = y[:, NTOK_P:].reshape(ncores * NB, 8, D)
    ckv_prompt = ckv[:, :NTOK_P][None]
    ckv_sample = ckv[:, NTOK_P:].reshape(ncores * NB, 8, 256)[None]
    kr_prompt = kr[:, :NTOK_P][None]
    kr_sample = kr[:, NTOK_P:].reshape(ncores * NB, 8, 64)[None]
    hg_p = np.stack([r["hg_p"] for r in results])[None]
    cv_p = np.stack([r["conv_p"] for r in results])[None]
    hg_s = np.concatenate([r["hg_s"] for r in results], axis=0)[None]
    cv_s = np.concatenate([r["conv_s"] for r in results], axis=0)[None]
    return tuple(np.ascontiguousarray(a, dtype=np.float32) for a in
                 (y_prompt, y_sample, ckv_prompt, kr_prompt, hg_p, cv_p, ckv_sample, kr_sample, hg_s, cv_s))


def kernel(**inputs):
    inp = {k: np.asarray(v) for k, v in inputs.items()}
    npool = inp["cache_ckv"].shape[1]
    consts = _consts()
    nc = build_program(npool)
    ncores = 8
    in_maps = [make_in_map(inp, c, consts) for c in range(ncores)]
    res = run_bass_kernel_spmd(nc, in_maps, core_ids=list(range(ncores)))
    return assemble(res.results, ncores)
```
